# Optimizing a Trainium2 kernel written in Bass

```python
import math
import jax, jax.numpy as jnp
from jax import lax
import numpy as np

D_MODEL = 2048
BATCH = 4
SEQ = 4096
DEPTH = 4

CHUNK = 64
Q_BLOCK = 128
ROPE_THETA = 500000.0
NORM_EPS = 1e-6
NEG_INF = -1e30
PLE_DIM = 256
HEAD_DIM = 128

A_HEADS = 6
A_Q_RANK = 384
A_KV_RANK = 256
A_NOPE = 128
A_ROPE = 64
A_V = 128
A_WIDTH = A_HEADS * A_V
B_HEADS = 5
B_WIDTH = B_HEADS * HEAD_DIM
IDX_HEADS = 8
IDX_DIM = 64
TOPK_MAX = 256
C_HEADS = 5
C_QK = 64
C_V = 128
C_WIDTH = C_HEADS * C_V
D_MIX = A_WIDTH + B_WIDTH + C_WIDTH

IN_SPLITS = (A_Q_RANK, A_KV_RANK, A_ROPE, A_WIDTH,
             B_WIDTH, B_WIDTH, B_WIDTH, IDX_HEADS * IDX_DIM, IDX_DIM, IDX_HEADS, B_WIDTH,
             C_HEADS * 2 * C_QK, C_HEADS * 2 * C_QK, C_WIDTH, C_WIDTH)
D_IN = sum(IN_SPLITS)

kernel_name = "hybrid_mla_dsa_diff_streaming_block"


def rms_norm(x, g):
    xf = x.astype(jnp.float32)
    y = xf * lax.rsqrt(jnp.mean(xf * xf, axis=-1, keepdims=True) + NORM_EPS)
    return (y * g.astype(jnp.float32)).astype(x.dtype)


def rope(x, pos, n_rot):
    half = n_rot // 2
    inv_freq = 1.0 / (ROPE_THETA ** (jnp.arange(half, dtype=jnp.float32) * (2.0 / n_rot)))
    ang = pos.astype(jnp.float32)[:, :, None] * inv_freq
    cos = jnp.cos(ang)[:, :, None, :]
    sin = jnp.sin(ang)[:, :, None, :]
    x1 = x[..., :half].astype(jnp.float32)
    x2 = x[..., half:n_rot].astype(jnp.float32)
    rot = jnp.concatenate([x1 * cos - x2 * sin, x2 * cos + x1 * sin], axis=-1).astype(x.dtype)
    return jnp.concatenate([rot, x[..., n_rot:]], axis=-1)


def split_cols(z, sizes):
    out, o = [], 0
    for sz in sizes:
        out.append(z[..., o:o + sz])
        o += sz
    return out


def to_blocks(a):
    b, s = a.shape[:2]
    return jnp.swapaxes(a.reshape(b, s // Q_BLOCK, Q_BLOCK, *a.shape[2:]), 0, 1)


def from_blocks(a):
    nb, b, qb = a.shape[:3]
    return jnp.swapaxes(a, 0, 1).reshape(b, nb * qb, *a.shape[3:])


def chunk_mask(blk, seq):
    q_pos = blk * Q_BLOCK + jnp.arange(Q_BLOCK)
    k_pos = jnp.arange(seq)
    return (k_pos // CHUNK)[None, :] <= (q_pos // CHUNK)[:, None]


def masked_softmax(scores, mask):
    return jax.nn.softmax(jnp.where(mask, scores.astype(jnp.float32), NEG_INF), axis=-1)


def mla_mixer(c_q, c_kv, k_rope, pos, q_norm_g, kv_norm_g, w_uq, w_ukv):
    b, s, _ = c_q.shape
    q = (rms_norm(c_q, q_norm_g) @ w_uq).reshape(b, s, A_HEADS, A_NOPE + A_ROPE)
    q = jnp.concatenate([q[..., :A_NOPE], rope(q[..., A_NOPE:], pos, A_ROPE)], axis=-1)
    kv = (rms_norm(c_kv, kv_norm_g) @ w_ukv).reshape(b, s, A_HEADS, A_NOPE + A_V)
    k_pe = rope(k_rope[:, :, None, :], pos, A_ROPE)
    k = jnp.concatenate([kv[..., :A_NOPE], jnp.broadcast_to(k_pe, (b, s, A_HEADS, A_ROPE))], axis=-1)
    v = kv[..., A_NOPE:]
    scale = (A_NOPE + A_ROPE) ** -0.5

    def block(args):
        blk, qb = args
        sc = jnp.einsum('bqhd,bkhd->bhqk', qb, k) * scale
        pr = masked_softmax(sc, chunk_mask(blk, s)).astype(v.dtype)
        return jnp.einsum('bhqk,bkhd->bqhd', pr, v)

    o = from_blocks(lax.map(block, (jnp.arange(s // Q_BLOCK), to_blocks(q))))
    return o.reshape(b, s, A_WIDTH)


def dsa_mixer(q, k, v, q_idx, k_idx, w_idx, pos):
    b, s, _ = q.shape
    q = rope(q.reshape(b, s, B_HEADS, HEAD_DIM), pos, HEAD_DIM // 4)
    k = rope(k.reshape(b, s, B_HEADS, HEAD_DIM), pos, HEAD_DIM // 4)
    v = v.reshape(b, s, B_HEADS, HEAD_DIM)
    qi = rope(q_idx.reshape(b, s, IDX_HEADS, IDX_DIM), pos, IDX_DIM // 4)
    ki = rope(k_idx[:, :, None, :], pos, IDX_DIM // 4)[:, :, 0, :]
    n_sel = min(TOPK_MAX, s // 4)
    scale = HEAD_DIM ** -0.5

    def block(args):
        blk, qb, qib, wb = args
        mask = chunk_mask(blk, s)
        rel = jax.nn.relu(jnp.einsum('bqhd,bkd->bqhk', qib, ki).astype(jnp.float32))
        score = jnp.einsum('bqhk,bqh->bqk', rel, wb.astype(jnp.float32))
        score = jnp.where(mask[None], score, NEG_INF)
        top_val, top_idx = lax.top_k(score, n_sel)
        valid = top_val > 0.5 * NEG_INF
        kg = jax.vmap(lambda kk, ii: kk[ii])(k, top_idx)
        vg = jax.vmap(lambda vv, ii: vv[ii])(v, top_idx)
        sc = jnp.einsum('bqhd,bqkhd->bhqk', qb, kg) * scale
        pr = masked_softmax(sc, valid[:, None]).astype(v.dtype)
        return jnp.einsum('bhqk,bqkhd->bqhd', pr, vg)

    o = from_blocks(lax.map(block, (jnp.arange(s // Q_BLOCK), to_blocks(q), to_blocks(qi), to_blocks(w_idx))))
    return o.reshape(b, s, B_WIDTH)


def diff_mixer(q, k, v, pos, lam, lam_init, subln_g):
    b, s, _ = q.shape
    q = rope(q.reshape(b, s, C_HEADS * 2, C_QK), pos, C_QK // 4).reshape(b, s, C_HEADS, 2, C_QK)
    k = rope(k.reshape(b, s, C_HEADS * 2, C_QK), pos, C_QK // 4).reshape(b, s, C_HEADS, 2, C_QK)
    v = v.reshape(b, s, C_HEADS, C_V)
    k1, k2 = k[..., 0, :], k[..., 1, :]
    scale = C_QK ** -0.5

    def block(args):
        blk, qb = args
        mask = chunk_mask(blk, s)
        a1 = masked_softmax(jnp.einsum('bqhd,bkhd->bhqk', qb[..., 0, :], k1) * scale, mask)
        a2 = masked_softmax(jnp.einsum('bqhd,bkhd->bhqk', qb[..., 1, :], k2) * scale, mask)
        pr = (a1 - lam * a2).astype(v.dtype)
        return jnp.einsum('bhqk,bkhd->bqhd', pr, v)

    o = from_blocks(lax.map(block, (jnp.arange(s // Q_BLOCK), to_blocks(q))))
    o = rms_norm(o, subln_g) * (1.0 - lam_init)
    return o.reshape(b, s, C_WIDTH)


def setup_inputs(seed: int = 0) -> dict:
    key = jax.random.key(seed)
    ks = jax.random.split(key, 20)
    f32 = jnp.float32
    nrm = lambda k, shape, sc: jax.random.normal(k, shape, f32) * sc
    x = nrm(ks[0], (BATCH, SEQ, D_MODEL), 1.0)
    p = nrm(ks[1], (DEPTH, BATCH, SEQ, PLE_DIM), 1.0)
    start = jax.random.randint(ks[2], (BATCH, 1), 0, 64, dtype=jnp.int32) * CHUNK
    positions = (start + jnp.arange(SEQ, dtype=jnp.int32)[None, :]).astype(jnp.int32)
    return {
        "x": x,
        "p": p,
        "positions": positions,
        "w_in": nrm(ks[3], (DEPTH, D_MODEL, D_IN), D_MODEL ** -0.5),
        "w_uq": nrm(ks[4], (DEPTH, A_Q_RANK, A_HEADS * (A_NOPE + A_ROPE)), A_Q_RANK ** -0.5),
        "w_ukv": nrm(ks[5], (DEPTH, A_KV_RANK, A_HEADS * (A_NOPE + A_V)), A_KV_RANK ** -0.5),
        "w_o": nrm(ks[6], (DEPTH, D_MIX, D_MODEL), D_MIX ** -0.5),
        "norm_g": 1.0 + nrm(ks[7], (DEPTH, D_MODEL), 0.02),
        "q_norm_g": 1.0 + nrm(ks[8], (DEPTH, A_Q_RANK), 0.02),
        "kv_norm_g": 1.0 + nrm(ks[9], (DEPTH, A_KV_RANK), 0.02),
        "lam_q1": nrm(ks[10], (DEPTH, C_QK), 0.1),
        "lam_k1": nrm(ks[11], (DEPTH, C_QK), 0.1),
        "lam_q2": nrm(ks[12], (DEPTH, C_QK), 0.1),
        "lam_k2": nrm(ks[13], (DEPTH, C_QK), 0.1),
        "subln_g": 1.0 + nrm(ks[14], (DEPTH, C_V), 0.02),
        "w_ple": nrm(ks[15], (DEPTH, PLE_DIM, D_MODEL), PLE_DIM ** -0.5),
        "w_pg": nrm(ks[16], (DEPTH, D_MODEL, D_MODEL), D_MODEL ** -0.5),
        "final_g": 1.0 + nrm(ks[17], (D_MODEL,), 0.02),
    }


def reference(x, p, positions, w_in, w_uq, w_ukv, w_o, norm_g, q_norm_g, kv_norm_g,
              lam_q1, lam_k1, lam_q2, lam_k2, subln_g, w_ple, w_pg, final_g):
    h = x
    f32 = jnp.float32
    for i in range(DEPTH):
        u = rms_norm(h, norm_g[i])
        z = u @ w_in[i]
        (c_q, c_kv, k_rope, g_a,
         q_b, k_b, v_b, q_idx, k_idx, w_idx, g_b,
         q_c, k_c, v_c, g_c) = split_cols(z, IN_SPLITS)
        o_a = mla_mixer(c_q, c_kv, k_rope, positions, q_norm_g[i], kv_norm_g[i], w_uq[i], w_ukv[i])
        o_b = dsa_mixer(q_b, k_b, v_b, q_idx, k_idx, w_idx, positions)
        lam_init = 0.8 - 0.6 * math.exp(-0.3 * i)
        lam = (jnp.exp(jnp.sum(lam_q1[i].astype(f32) * lam_k1[i].astype(f32)))
               - jnp.exp(jnp.sum(lam_q2[i].astype(f32) * lam_k2[i].astype(f32))) + lam_init)
        o_c = diff_mixer(q_c, k_c, v_c, positions, lam, lam_init, subln_g[i])
        mixed = jnp.concatenate([o_a * jax.nn.silu(g_a),
                                 o_b * jax.nn.silu(g_b),
                                 o_c * jax.nn.silu(g_c)], axis=-1)
        h = h + mixed @ w_o[i]
        h = h + (p[i] @ w_ple[i]) * jax.nn.sigmoid(h @ w_pg[i])
    return rms_norm(h, final_g)
```

```python
import math
import os
from contextlib import ExitStack

import numpy as np
import ml_dtypes
import concourse.bass as bass
import concourse.mybir as mybir
from concourse.bass_utils import run_bass_kernel_spmd

F32 = mybir.dt.float32
BF16 = mybir.dt.bfloat16
I32 = mybir.dt.int32
ALU = mybir.AluOpType
AF = mybir.ActivationFunctionType
AX = mybir.AxisListType

D = 2048
PLE = 256
D_IN = 7176
EPS = 1e-6
THETA = 500000.0
NIT = 22
NEGM = -30000.0
SA = 192.0 ** -0.5
SB = 128.0 ** -0.5
SC = 64.0 ** -0.5
VW = 132


class Buf:
    __slots__ = ("name", "writers", "readers", "ap", "psum")

    def __init__(self, name="", ap=None, psum=False):
        self.psum = psum
        self.name = name
        self.writers = {}
        self.readers = {}
        self.ap = ap


class Sched:
    NDMA = 8

    def __init__(self, nc):
        self.nc = nc
        self.eng = {"pe": nc.tensor, "act": nc.scalar, "dve": nc.vector,
                    "pool": nc.gpsimd, "sp": nc.sync}
        self.sems = {}
        self.cnt = {}
        for e in ("pe", "act", "dve", "pool"):
            self.sems[e] = nc.alloc_semaphore("s_" + e)
            self.cnt[e] = 0
        self.dq = {}
        for q in ("sp", "pool"):
            for i in range(self.NDMA):
                k = "d_%s%d" % (q, i)
                self.sems[k] = nc.alloc_semaphore(k)
                self.cnt[k] = 0
            self.dq[q] = 0
        self.seen = {e: {} for e in self.eng}
        self.nins = 0
        self.nwait = 0

    def _wait(self, e, evs):
        best = {}
        for (k, v) in evs:
            if v > best.get(k, 0):
                best[k] = v
        seen = self.seen[e]
        for k, v in best.items():
            if k == "pe" and e == "pe":
                continue
            if seen.get(k, 0) < v:
                self.eng[e].wait_ge(self.sems[k], v)
                seen[k] = v
                self.nwait += 1

    @staticmethod
    def _deps(reads, writes, e=None):
        evs = []
        for b in reads:
            evs.extend(b.writers.items())
            if b.psum:
                evs.extend((k, v) for k, v in b.readers.items() if k != e)
        for b in writes:
            evs.extend(b.writers.items())
            evs.extend(b.readers.items())
        return evs

    @staticmethod
    def _commit(ev, reads, writes):
        k, v = ev
        for b in reads:
            if b.readers.get(k, 0) < v:
                b.readers[k] = v
        for b in writes:
            b.writers = {k: v}
            b.readers = {}

    def op(self, e, ins_fn, reads=(), writes=()):
        self._wait(e, self._deps(reads, writes, e))
        ins = ins_fn()
        self.cnt[e] += 1
        ins.then_inc(self.sems[e], 1)
        self._commit((e, self.cnt[e]), reads, writes)
        self.nins += 1

    def dma(self, q, out, in_, reads=(), writes=()):
        i = self.dq[q]
        self.dq[q] = (i + 1) % self.NDMA
        k = "d_%s%d" % (q, i)
        evs = self._deps(reads, writes)
        if self.cnt[k] > 0:
            evs.append((k, self.cnt[k]))
        self._wait(q, evs)
        ins = self.eng[q].dma_start(out=out, in_=in_)
        self.cnt[k] += 16
        ins.then_inc(self.sems[k], 16)
        self._commit((k, self.cnt[k]), reads, writes)
        self.nins += 1

    def barrier(self):
        evs = [(k, v) for k, v in self.cnt.items() if v > 0]
        for e in self.eng:
            self._wait(e, list(evs))


def host_consts():
    c = {}
    c["ident"] = np.eye(128, dtype=np.float32).astype(ml_dtypes.bfloat16)
    invf = np.zeros((128, 56), np.float32)
    off = 0
    for n_rot in (64, 32, 16):
        half = n_rot // 2
        f = 1.0 / (np.float32(THETA) ** (np.arange(half, dtype=np.float32) * np.float32(2.0 / n_rot)))
        invf[:, off:off + half] = f.astype(np.float32)[None, :]
        off += half
    c["invf"] = invf
    dm = np.zeros((4, 128, 512), np.float32)
    for j in range(4):
        kk = 128 * j + np.arange(128)[:, None]
        qq = np.arange(512)[None, :]
        dm[j] = np.where((kk // 64) <= (qq // 64), 0.0, NEGM)
    c["dmask"] = dm.astype(ml_dtypes.bfloat16)
    dv = np.zeros((4, 128, 512), np.float32)
    for qs in range(4):
        qq = 128 * qs + np.arange(128)[:, None]
        kk = np.arange(512)[None, :]
        dv[qs] = ((kk // 64) <= (qq // 64)).astype(np.float32)
    c["dvalid"] = dv
    c["dneg"] = ((dv - 1.0) * 1e30).astype(np.float32)
    c["pow2"] = np.tile((0.5 ** np.arange(1, NIT + 1, dtype=np.float64)).astype(np.float32)[None, :], (128, 1))
    return c


CHUNKS = [
    (0, 384, "cq", None),
    (384, 320, "ckv", None),
    (704, 512, "gate", 0),
    (1216, 256, "gate", 512),
    (1472, 512, "qb", (0, 4)),
    (1984, 128, "qb", (4, 1)),
    (2112, 512, "kb", (0, 4)),
    (2624, 128, "kb", (4, 1)),
    (2752, 512, "vb", (0, 4)),
    (3264, 128, "vb", (4, 1)),
    (3392, 512, "qi", None),
    (3904, 72, "kiw", None),
    (3976, 512, "gate", 768),
    (4488, 128, "gate", 1280),
    (4616, 512, "qc", (0, 8)),
    (5128, 128, "qc", (8, 2)),
    (5256, 512, "kc", (0, 8)),
    (5768, 128, "kc", (8, 2)),
    (5896, 512, "vc", (0, 4)),
    (6408, 128, "vc", (4, 1)),
    (6536, 512, "gate", 1408),
    (7048, 128, "gate", 1920),
]
SEGS = [(2 * i, 2 * i + 1) for i in range(11)]


class _StopBuild(Exception):
    pass


def build_program(S, DEPTH, debug_layers=None):
    assert S % 512 == 0
    STOP = os.environ.get("MK_STOP", "")

    def maybe_stop(tag):
        if STOP == tag:
            raise _StopBuild()
    NT = S // 128
    NQB = S // 512
    nc = bass.Bass("TRN2", target_bir_lowering=False)
    sch = Sched(nc)
    uid = [0]

    def din(name, shape, dt):
        return nc.dram_tensor(name, list(shape), dt, kind="ExternalInput").ap()

    def dscr(name, shape, dt):
        return nc.dram_tensor(name, list(shape), dt).ap()

    x_d = din("x", [S, D], F32)
    p_d = din("p", [DEPTH, S, PLE], F32)
    pos_d = din("positions", [128, S // 128], I32)
    w_in_d = din("w_in", [DEPTH, D, D_IN], F32)
    w_uq_d = din("w_uq", [DEPTH, 384, 1152], F32)
    w_ukv_d = din("w_ukv", [DEPTH, 256, 1536], F32)
    w_o_d = din("w_o", [DEPTH, D, D], F32)
    norm_g_d = din("norm_g", [DEPTH, D], F32)
    q_norm_g_d = din("q_norm_g", [DEPTH, 384], F32)
    kv_norm_g_d = din("kv_norm_g", [DEPTH, 256], F32)
    lam_d = {n: din(n, [DEPTH, 64], F32) for n in ("lam_q1", "lam_k1", "lam_q2", "lam_k2")}
    subln_g_d = din("subln_g", [DEPTH, 128], F32)
    w_ple_d = din("w_ple", [DEPTH, PLE, D], F32)
    w_pg_d = din("w_pg", [DEPTH, D, D], F32)
    final_g_d = din("final_g", [1, D], F32)
    ident_d = din("ident", [128, 128], BF16)
    invf_d = din("invf", [128, 56], F32)
    dmask_d = din("dmask", [4, 128, 512], BF16)
    dvalid_d = din("dvalid", [4, 128, 512], F32)
    dneg_d = din("dneg", [4, 128, 512], F32)
    pow2_d = din("pow2", [128, NIT], F32)
    y_d = nc.dram_tensor("y", [S, D], F32, kind="ExternalOutput").ap()

    hbuf = dscr("hbuf", [S, D], F32)
    uT_d = dscr("uT_d", [NT, 128, 16, 128], BF16)
    rope_d = dscr("rope_d", [128, NT, 224], F32)
    qaTn = dscr("qaTn", [6, 128, S], BF16)
    qaTr = dscr("qaTr", [3, 128, S], BF16)
    kaTn = dscr("kaTn", [6, 128, S], BF16)
    kropeT = dscr("kropeT", [1, 128, S], BF16)
    Va = dscr("Va", [S, 6 * VW], BF16)
    qbT = dscr("qbT", [5, 128, S], BF16)
    kbT = dscr("kbT", [5, 128, S], BF16)
    Vb = dscr("Vb", [S, 5 * VW], BF16)
    qiT = dscr("qiT", [4, 128, S], BF16)
    kiT = dscr("kiT", [1, 128, S], BF16)
    WI = dscr("WI", [S, 16], F32)
    qcT = dscr("qcT", [5, 128, S], BF16)
    kcT = dscr("kcT", [5, 128, S], BF16)
    Vc = dscr("Vc", [S, 5 * VW], BF16)
    SG = dscr("SG", [S, D], F32)
    MX = dscr("MX", [S, D], BF16)
    NM = dscr("NM", [NQB, 128, 4 * NQB, 512], BF16)

    def sb(es, name, shape, dt):
        uid[0] += 1
        h = es.enter_context(nc.sbuf_tensor("%s_%d" % (name, uid[0]), list(shape), dt))
        return Buf(name, h.ap())

    def ps(es, name, shape, dt):
        uid[0] += 1
        h = es.enter_context(nc.psum_tensor("%s_%d" % (name, uid[0]), list(shape), dt))
        return Buf(name, h.ap(), psum=True)

    V = nc.vector
    G = nc.gpsimd
    A = nc.scalar
    PE = nc.tensor
    op = sch.op
    dma = sch.dma

    top = ExitStack()
    ident = sb(top, "ident", [128, 128], BF16)
    dmask = sb(top, "dmask", [128, 4, 512], BF16)
    dma("sp", ident.ap, ident_d, writes=[ident])
    dma("sp", dmask.ap, dmask_d.rearrange("j p q -> p j q"), writes=[dmask])

    def bcast_rows(src_row_ap, n):
        return src_row_ap.to_broadcast([128, n])

    ROFF = {"aq": (0, 32), "ak": (64, 32), "bq": (128, 16), "bk": (160, 16), "i": (192, 8), "cq": (208, 8)}
    with ExitStack() as es:
        posi = sb(es, "posi", [128, NT], I32)
        posf = sb(es, "posf", [128, NT], F32)
        invf = sb(es, "invf", [128, 56], F32)
        ang = sb(es, "ang", [128, NT, 56], F32)
        r1 = sb(es, "r1", [128, NT, 56], F32)
        cs = sb(es, "cs", [128, NT, 56], F32)
        sn = sb(es, "sn", [128, NT, 56], F32)
        tab = sb(es, "tab", [128, NT, 224], F32)
        dma("sp", posi.ap, pos_d, writes=[posi])
        dma("sp", invf.ap, invf_d, writes=[invf])
        op("dve", lambda: V.tensor_copy(out=posf.ap, in_=posi.ap), [posi], [posf])
        for t in range(NT):
            op("dve", lambda: V.tensor_scalar(out=ang.ap[:, t, :], in0=invf.ap, scalar1=posf.ap[:, t:t + 1],
                                              scalar2=None, op0=ALU.mult), [invf, posf], [ang])
        twopi = 2.0 * math.pi
        ki = sb(es, "ki", [128, NT, 56], I32)
        kf = sb(es, "kf", [128, NT, 56], F32)

        def sin_of(shift, dst):
            op("dve", lambda: V.tensor_scalar(out=r1.ap, in0=ang.ap, scalar1=float(shift), scalar2=None, op0=ALU.add),
               [ang], [r1])
            op("dve", lambda: V.tensor_scalar(out=kf.ap, in0=r1.ap, scalar1=1.0 / twopi, scalar2=None, op0=ALU.mult),
               [r1], [kf])
            op("dve", lambda: V.tensor_copy(out=ki.ap, in_=kf.ap), [kf], [ki])
            op("dve", lambda: V.tensor_copy(out=kf.ap, in_=ki.ap), [ki], [kf])
            op("dve", lambda: V.scalar_tensor_tensor(out=r1.ap, in0=kf.ap, scalar=-twopi, in1=r1.ap,
                                                     op0=ALU.mult, op1=ALU.add), [kf, r1], [r1])
            op("dve", lambda: V.tensor_scalar(out=kf.ap, in0=r1.ap, scalar1=math.pi, scalar2=-twopi,
                                              op0=ALU.is_gt, op1=ALU.mult), [r1], [kf])
            op("dve", lambda: V.tensor_tensor(out=r1.ap, in0=r1.ap, in1=kf.ap, op=ALU.add), [r1, kf], [r1])
            op("dve", lambda: V.tensor_scalar(out=kf.ap, in0=r1.ap, scalar1=-math.pi, scalar2=twopi,
                                              op0=ALU.is_lt, op1=ALU.mult), [r1], [kf])
            op("dve", lambda: V.tensor_tensor(out=r1.ap, in0=r1.ap, in1=kf.ap, op=ALU.add), [r1, kf], [r1])
            op("act", lambda: A.activation(out=dst.ap, in_=r1.ap, func=AF.Sin), [r1], [dst])

        sin_of(0.0, sn)
        sin_of(0.5 * math.pi, cs)
        specs = [("aq", 0, SA), ("ak", 0, 1.0), ("bq", 32, SB), ("bk", 32, 1.0), ("i", 48, 1.0), ("cq", 48, SC)]
        for (nm, so, scl) in specs:
            o, hf = ROFF[nm]
            op("dve", lambda: V.tensor_scalar(out=tab.ap[:, :, o:o + hf], in0=cs.ap[:, :, so:so + hf],
                                              scalar1=float(scl), scalar2=None, op0=ALU.mult), [cs], [tab])
            op("dve", lambda: V.tensor_scalar(out=tab.ap[:, :, o + hf:o + 2 * hf], in0=sn.ap[:, :, so:so + hf],
                                              scalar1=float(scl), scalar2=None, op0=ALU.mult), [sn], [tab])
        dma("sp", rope_d, tab.ap, reads=[tab])
        sch.barrier()

    def rms_rstd(es_tmp, src_ap, n, junk, st, reads):
        op("act", lambda: A.activation(out=junk.ap[:, 0:n], in_=src_ap, func=AF.Square, accum_out=st.ap[:, 0:1]),
           reads, [junk, st])
        op("act", lambda: A.activation(out=st.ap[:, 1:2], in_=st.ap[:, 0:1], func=AF.Sqrt, bias=EPS, scale=1.0 / n),
           [st], [st])
        op("dve", lambda: V.reciprocal(out=st.ap[:, 0:1], in_=st.ap[:, 1:2]), [st], [st])

    maybe_stop_holder = [None]
    try:
      maybe_stop("P")
      for L in range(DEPTH):
        h_src = x_d if L == 0 else hbuf
        last = (L == DEPTH - 1)
        lam_init = 0.8 - 0.6 * math.exp(-0.3 * L)

        with ExitStack() as es:
            gbc = sb(es, "gbc", [128, D], F32)
            dma("sp", gbc.ap, bcast_rows(norm_g_d[L:L + 1, :], D), writes=[gbc])
            hin = [sb(es, "hin", [128, D], F32) for _ in range(2)]
            ub = [sb(es, "ub", [128, D], BF16) for _ in range(2)]
            uTs = [sb(es, "uTs", [128, 16, 128], BF16) for _ in range(2)]
            junk = sb(es, "junk", [128, D], BF16)
            st = [sb(es, "st", [128, 2], F32) for _ in range(2)]
            pT = [ps(es, "pT", [128, 8, 128], BF16) for _ in range(4)]
            npT = 0
            dma("sp", hin[0].ap, h_src[0:128, :], writes=[hin[0]])
            for t in range(NT):
                i = t % 2
                if t + 1 < NT:
                    dma("sp", hin[1 - i].ap, h_src[(t + 1) * 128:(t + 2) * 128, :], writes=[hin[1 - i]])
                rms_rstd(es, hin[i].ap, D, junk, st[i], [hin[i]])
                op("dve", lambda: V.scalar_tensor_tensor(out=ub[i].ap, in0=hin[i].ap, scalar=st[i].ap[:, 0:1],
                                                         in1=gbc.ap, op0=ALU.mult, op1=ALU.mult),
                   [hin[i], st[i], gbc], [ub[i]])
                for g in range(2):
                    pt = pT[npT % 4]
                    npT += 1
                    for c in range(8):
                        cc = 8 * g + c
                        op("pe", lambda: PE.transpose(out=pt.ap[:, c, :], in_=ub[i].ap[:, cc * 128:(cc + 1) * 128],
                                                      identity=ident.ap), [ub[i], ident], [pt])
                    op("act", lambda: A.copy(out=uTs[i].ap[:, 8 * g:8 * g + 8, :], in_=pt.ap), [pt], [uTs[i]])
                dma("sp", uT_d[t], uTs[i].ap, reads=[uTs[i]])
            sch.barrier()
            maybe_stop("A1")

        with ExitStack() as es:
            WMAX = 768
            wbuf = [sb(es, "wbuf", [128, 16, WMAX], BF16) for _ in range(2)]
            wuq = sb(es, "wuq", [128, 3, 1152], BF16)
            wukv = sb(es, "wukv", [128, 2, 1536], BF16)
            qgb = sb(es, "qgb", [128, 384], F32)
            kvgb = sb(es, "kvgb", [128, 256], F32)
            tab = sb(es, "tab", [128, NT, 224], F32)
            uTs = [sb(es, "uTs", [128, 16, 128], BF16) for _ in range(3)]
            pz = [ps(es, "pz", [128, 512], F32) for _ in range(4)]
            ptr = [ps(es, "ptr", [128, 8, 128], BF16) for _ in range(2)]
            junkf = sb(es, "junkf", [128, 512], F32)
            st = sb(es, "st", [128, 2], F32)
            cqn = sb(es, "cqn", [128, 384], BF16)
            cqT = sb(es, "cqT", [128, 3, 128], BF16)
            ckvn = sb(es, "ckvn", [128, 256], BF16)
            ckvT = sb(es, "ckvT", [128, 2, 128], BF16)
            qn = sb(es, "qn", [128, 6, 128], BF16)
            qr = sb(es, "qr", [128, 6, 64], BF16)
            kn = sb(es, "kn", [128, 6, 128], BF16)
            kr = sb(es, "kr", [128, 2, 64], BF16)
            vx6 = sb(es, "vx6", [128, 6, VW], BF16)
            vx5 = sb(es, "vx5", [128, 5, VW], BF16)
            hd5 = sb(es, "hd5", [128, 5, 128], BF16)
            qis = sb(es, "qis", [128, 8, 64], BF16)
            kis = sb(es, "kis", [128, 2, 64], BF16)
            wi = sb(es, "wi", [128, 16], F32)
            sgt = [sb(es, "sgt", [128, 512], F32) for _ in range(2)]
            rt = [[sb(es, "rt", [128, 128], F32) for _ in range(4)] for _ in range(2)]
            stage = [sb(es, "stage", [128, 8, 128], BF16) for _ in range(2)]
            cnt = {"pz": 0, "ptr": 0, "rt": 0, "stage": 0, "sgt": 0}

            dma("sp", tab.ap, rope_d, writes=[tab])
            dma("sp", qgb.ap, bcast_rows(q_norm_g_d[L:L + 1, :], 384), writes=[qgb])
            dma("sp", kvgb.ap, bcast_rows(kv_norm_g_d[L:L + 1, :], 256), writes=[kvgb])
            dma("pool", wuq.ap, w_uq_d[L].rearrange("(c p) n -> p c n", p=128), writes=[wuq])
            dma("pool", wukv.ap, w_ukv_d[L].rearrange("(c p) n -> p c n", p=128), writes=[wukv])
            op("dve", lambda: V.memset(vx6.ap, 1.0), [], [vx6])
            op("dve", lambda: V.memset(vx5.ap, 1.0), [], [vx5])

            def load_w(si):
                c0 = CHUNKS[SEGS[si][0]][0]
                c1 = CHUNKS[SEGS[si][1]][0] + CHUNKS[SEGS[si][1]][1]
                wb = wbuf[si % 2]
                src = w_in_d[L].rearrange("(c p) n -> p c n", p=128)
                for kc0 in range(0, 16, 4):
                    dma("pool", wb.ap[:, kc0:kc0 + 4, 0:c1 - c0], src[:, kc0:kc0 + 4, c0:c1], writes=[wb])

            def nxt(key, lst):
                cnt[key] += 1
                return lst[cnt[key] % len(lst)]

            def rope_evac(src3, dst3, H, Dh, n_rot, tname, t, scale, reads, writes):
                half = n_rot // 2
                o, hf = ROFF[tname]
                assert hf == half
                cb = tab.ap[:, t, o:o + half].unsqueeze(1).to_broadcast([128, H, half])
                sbc = tab.ap[:, t, o + half:o + 2 * half].unsqueeze(1).to_broadcast([128, H, half])
                r = nxt("rt", rt)
                n = H * half
                v = [r[k].ap[:, 0:n].rearrange("p (h d) -> p h d", h=H) for k in range(4)]
                x1 = src3[:, :, 0:half]
                x2 = src3[:, :, half:n_rot]
                op("dve", lambda: V.tensor_tensor(out=v[0], in0=x1, in1=cb, op=ALU.mult), reads + [tab], [r[0]])
                op("dve", lambda: V.tensor_tensor(out=v[1], in0=x2, in1=sbc, op=ALU.mult), reads + [tab], [r[1]])
                op("pool", lambda: G.tensor_tensor(out=dst3[:, :, 0:half], in0=v[0], in1=v[1], op=ALU.subtract),
                   [r[0], r[1]], writes)
                op("dve", lambda: V.tensor_tensor(out=v[2], in0=x2, in1=cb, op=ALU.mult), reads + [tab], [r[2]])
                op("dve", lambda: V.tensor_tensor(out=v[3], in0=x1, in1=sbc, op=ALU.mult), reads + [tab], [r[3]])
                op("pool", lambda: G.tensor_tensor(out=dst3[:, :, half:n_rot], in0=v[2], in1=v[3], op=ALU.add),
                   [r[2], r[3]], writes)
                if Dh > n_rot:
                    op("act", lambda: A.mul(out=dst3[:, :, n_rot:Dh], in_=src3[:, :, n_rot:Dh], mul=float(scale)),
                       reads, writes)

            def fm_store(src, src3, n, dst, t):
                for g0 in range(0, n, 8):
                    g1 = min(n, g0 + 8)
                    pt = nxt("ptr", ptr)
                    sg_ = nxt("stage", stage)
                    for k in range(g0, g1):
                        op("pe", lambda: PE.transpose(out=pt.ap[:, k - g0, :], in_=src3[:, k, :], identity=ident.ap),
                           [src, ident], [pt])
                    op("act", lambda: A.copy(out=sg_.ap[:, 0:g1 - g0, :], in_=pt.ap[:, 0:g1 - g0, :]), [pt], [sg_])
                    dma("sp", dst.rearrange("n p s -> p n s")[:, g0:g1, t * 128:(t + 1) * 128],
                        sg_.ap[:, 0:g1 - g0, :], reads=[sg_])

            def rmsnorm_to(src_ap, n, gb, dst, reads):
                rms_rstd(es, src_ap, n, junkf, st, reads)
                op("dve", lambda: V.scalar_tensor_tensor(out=dst.ap, in0=src_ap, scalar=st.ap[:, 0:1], in1=gb.ap,
                                                         op0=ALU.mult, op1=ALU.mult), reads + [st, gb], [dst])

            def epilogue(ci, z, t):
                col0, width, kind, meta = CHUNKS[ci]
                zs = z.ap[:, 0:width]
                r0 = t * 128
                if kind == "cq":
                    rmsnorm_to(zs, 384, qgb, cqn, [z])
                    pt = nxt("ptr", ptr)
                    for c in range(3):
                        op("pe", lambda: PE.transpose(out=pt.ap[:, c, :], in_=cqn.ap[:, c * 128:(c + 1) * 128],
                                                      identity=ident.ap), [cqn, ident], [pt])
                    op("act", lambda: A.copy(out=cqT.ap, in_=pt.ap[:, 0:3, :]), [pt], [cqT])
                    for c in range(3):
                        z2 = nxt("pz", pz)
                        for kc in range(3):
                            op("pe", lambda: PE.matmul(z2.ap[:, 0:384], lhsT=cqT.ap[:, kc, :],
                                                       rhs=wuq.ap[:, kc, c * 384:(c + 1) * 384],
                                                       start=(kc == 0), stop=(kc == 2)), [cqT, wuq], [z2])
                        v3 = z2.ap[:, 0:384].rearrange("p (h d) -> p h d", h=2)
                        op("act", lambda: A.mul(out=qn.ap[:, 2 * c:2 * c + 2, :], in_=v3[:, :, 0:128], mul=float(SA)),
                           [z2], [qn])
                        rope_evac(v3[:, :, 128:192], qr.ap[:, 2 * c:2 * c + 2, :], 2, 64, 64, "aq", t, SA, [z2], [qr])
                    fm_store(qn, qn.ap, 6, qaTn, t)
                    fm_store(qr, qr.ap.rearrange("p (a b) d -> p a (b d)", b=2), 3, qaTr, t)
                elif kind == "ckv":
                    LVL = int(os.environ.get("MK_LVL", 99))
                    rmsnorm_to(z.ap[:, 0:256], 256, kvgb, ckvn, [z])
                    if LVL < 2:
                        return
                    pt = nxt("ptr", ptr)
                    for c in range(2):
                        op("pe", lambda: PE.transpose(out=pt.ap[:, c, :], in_=ckvn.ap[:, c * 128:(c + 1) * 128],
                                                      identity=ident.ap), [ckvn, ident], [pt])
                    op("act", lambda: A.copy(out=ckvT.ap, in_=pt.ap[:, 0:2, :]), [pt], [ckvT])
                    if LVL < 3:
                        return
                    rope_evac(z.ap[:, 256:320].rearrange("p (h d) -> p h d", h=1), kr.ap[:, 0:1, :], 1, 64, 64, "ak", t,
                              1.0, [z], [kr])
                    if LVL < 4:
                        return
                    op("act", lambda: A.copy(out=kr.ap[:, 1, :], in_=kr.ap[:, 0, :]), [kr], [kr])
                    if LVL < 5:
                        return
                    for c in range(3):
                        z2 = nxt("pz", pz)
                        for kc in range(2):
                            op("pe", lambda: PE.matmul(z2.ap, lhsT=ckvT.ap[:, kc, :],
                                                       rhs=wukv.ap[:, kc, c * 512:(c + 1) * 512],
                                                       start=(kc == 0), stop=(kc == 1)), [ckvT, wukv], [z2])
                        v3 = z2.ap.rearrange("p (h d) -> p h d", h=2)
                        SUB = int(os.environ.get("MK_SUB", 3))
                        if SUB & 1:
                            op("act", lambda: A.copy(out=kn.ap[:, 2 * c:2 * c + 2, :], in_=v3[:, :, 0:128]), [z2], [kn])
                        if SUB & 2:
                            op("dve", lambda: V.tensor_copy(out=vx6.ap[:, 2 * c:2 * c + 2, 0:128], in_=v3[:, :, 128:256]),
                               [z2], [vx6])
                    if LVL < 6:
                        return
                    fm_store(kn, kn.ap, 6, kaTn, t)
                    if LVL < 7:
                        return
                    fm_store(kr, kr.ap.rearrange("p (a b) d -> p a (b d)", b=2), 1, kropeT, t)
                    if LVL < 8:
                        return
                    dma("sp", Va[r0:r0 + 128, :], vx6.ap.rearrange("p h d -> p (h d)"), reads=[vx6])
                elif kind == "gate":
                    s_ = nxt("sgt", sgt)
                    op("act", lambda: A.activation(out=s_.ap[:, 0:width], in_=zs, func=AF.Silu), [z], [s_])
                    dma("sp", SG[r0:r0 + 128, meta:meta + width], s_.ap[:, 0:width], reads=[s_])
                elif kind in ("qb", "kb"):
                    h0, nh = meta
                    v3 = zs.rearrange("p (h d) -> p h d", h=nh)
                    rope_evac(v3, hd5.ap[:, h0:h0 + nh, :], nh, 128, 32, "bq" if kind == "qb" else "bk", t,
                              SB if kind == "qb" else 1.0, [z], [hd5])
                    if h0 + nh == 5:
                        fm_store(hd5, hd5.ap, 5, qbT if kind == "qb" else kbT, t)
                elif kind in ("vb", "vc"):
                    h0, nh = meta
                    v3 = zs.rearrange("p (h d) -> p h d", h=nh)
                    op("dve", lambda: V.tensor_copy(out=vx5.ap[:, h0:h0 + nh, 0:128], in_=v3), [z], [vx5])
                    if h0 + nh == 5:
                        dst = Vb if kind == "vb" else Vc
                        dma("sp", dst[r0:r0 + 128, :], vx5.ap.rearrange("p h d -> p (h d)"), reads=[vx5])
                elif kind == "qi":
                    v3 = zs.rearrange("p (h d) -> p h d", h=8)
                    rope_evac(v3, qis.ap, 8, 64, 16, "i", t, 1.0, [z], [qis])
                    fm_store(qis, qis.ap.rearrange("p (a b) d -> p a (b d)", b=2), 4, qiT, t)
                elif kind == "kiw":
                    rope_evac(z.ap[:, 0:64].rearrange("p (h d) -> p h d", h=1), kis.ap[:, 0:1, :], 1, 64, 16, "i", t,
                              1.0, [z], [kis])
                    op("act", lambda: A.copy(out=kis.ap[:, 1, :], in_=kis.ap[:, 0, :]), [kis], [kis])
                    fm_store(kis, kis.ap.rearrange("p (a b) d -> p a (b d)", b=2), 1, kiT, t)
                    op("act", lambda: A.activation(out=wi.ap[:, 0:8], in_=z.ap[:, 64:72], func=AF.Abs), [z], [wi])
                    op("dve", lambda: V.tensor_scalar(out=wi.ap[:, 8:16], in0=z.ap[:, 64:72], scalar1=0.0, scalar2=2.0,
                                                      op0=ALU.is_ge, op1=ALU.mult), [z], [wi])
                    op("dve", lambda: V.tensor_scalar(out=wi.ap[:, 8:16], in0=wi.ap[:, 8:16], scalar1=-1.0,
                                                      scalar2=None, op0=ALU.add), [wi], [wi])
                    dma("sp", WI[r0:r0 + 128, :], wi.ap, reads=[wi])
                elif kind in ("qc", "kc"):
                    h0, nh = meta
                    v3 = zs.rearrange("p (h d) -> p h d", h=nh)
                    d3 = hd5.ap.rearrange("p a (b d) -> p (a b) d", b=2)
                    rope_evac(v3, d3[:, h0:h0 + nh, :], nh, 64, 16, "cq" if kind == "qc" else "i", t,
                              SC if kind == "qc" else 1.0, [z], [hd5])
                    if h0 + nh == 10:
                        fm_store(hd5, hd5.ap, 5, qcT if kind == "qc" else kcT, t)
                else:
                    raise AssertionError(kind)

            NSEG = int(os.environ.get("MK_NSEG", len(SEGS)))
            EPI = os.environ.get("MK_EPI", "1") == "1"
            EPK = os.environ.get("MK_EPK")
            EPK = set(EPK.split(",")) if EPK else None
            if NSEG > 0:
                load_w(0)
            for si in range(NSEG):
                if si + 1 < NSEG:
                    load_w(si + 1)
                wb = wbuf[si % 2]
                c0 = CHUNKS[SEGS[si][0]][0]
                dma("sp", uTs[0].ap, uT_d[0], writes=[uTs[0]])
                for t in range(NT):
                    u = uTs[t % 3]
                    if t + 1 < NT:
                        dma("sp", uTs[(t + 1) % 3].ap, uT_d[t + 1], writes=[uTs[(t + 1) % 3]])
                    zl = []
                    for ci in SEGS[si]:
                        col0, width, kind, meta = CHUNKS[ci]
                        z = nxt("pz", pz)
                        for kc in range(16):
                            op("pe", lambda: PE.matmul(z.ap[:, 0:width], lhsT=u.ap[:, kc, :],
                                                       rhs=wb.ap[:, kc, col0 - c0:col0 - c0 + width],
                                                       start=(kc == 0), stop=(kc == 15)), [u, wb], [z])
                        zl.append((ci, z))
                    if CHUNKS[SEGS[si][0]][2] == "qi":
                        zl = zl[::-1]
                    for (ci, z) in zl:
                        if EPI and (EPK is None or CHUNKS[ci][2] in EPK):
                            epilogue(ci, z, t)
            sch.barrier()
            maybe_stop("A2")

        with ExitStack() as es:
            kiTs = sb(es, "kiTs", [128, S], BF16)
            dvl = sb(es, "dvl", [128, 4, 512], F32)
            dng = sb(es, "dng", [128, 4, 512], F32)
            pw2 = sb(es, "pw2", [128, NIT], F32)
            Sc = [sb(es, "Sc", [128, S], F32) for _ in range(2)]
            rl = [sb(es, "rl", [128, 512], F32) for _ in range(3)]
            qit = [sb(es, "qit", [128, 4, 128], BF16) for _ in range(2)]
            wit = [sb(es, "wit", [128, 16], F32) for _ in range(2)]
            sel = sb(es, "sel", [128, S], BF16)
            junkb = sb(es, "junkb", [128, S], BF16)
            nms = [sb(es, "nms", [128, 4 * NQB, 128], BF16) for _ in range(2)]
            bs = [sb(es, "bs", [128, 8], F32) for _ in range(2)]
            halves = [sb(es, "halves", [128, NIT], F32) for _ in range(2)]
            psI = [ps(es, "psI", [128, 512], F32) for _ in range(3)]
            pst = [ps(es, "pst", [128, 8, 128], BF16) for _ in range(2)]
            nI = [0, 0, 0]
            dma("sp", kiTs.ap, kiT[0], writes=[kiTs])
            dma("sp", dvl.ap, dvalid_d.rearrange("j p q -> p j q"), writes=[dvl])
            dma("sp", dng.ap, dneg_d.rearrange("j p q -> p j q"), writes=[dng])
            dma("sp", pw2.ap, pow2_d, writes=[pw2])
            for it in range(NT):
                qb, qs = it // 4, it % 4
                nkb = qb + 1
                Lk = 512 * nkb
                sc = Sc[it % 2]
                q_ = qit[it % 2]
                w_ = wit[it % 2]
                b_ = bs[it % 2]
                hv = halves[it % 2]
                dma("sp", q_.ap, qiT.rearrange("n p s -> p n s")[:, :, it * 128:(it + 1) * 128], writes=[q_])
                dma("sp", w_.ap, WI[it * 128:(it + 1) * 128, :], writes=[w_])
                for kb in range(nkb):
                    for h in range(8):
                        pI = psI[nI[0] % 3]
                        nI[0] += 1
                        r_ = rl[nI[1] % 3]
                        nI[1] += 1
                        lo_p = 64 * (h % 2)
                        op("pe", lambda: PE.matmul(pI.ap, lhsT=q_.ap[lo_p:lo_p + 64, h // 2, :],
                                                   rhs=kiTs.ap[lo_p:lo_p + 64, kb * 512:(kb + 1) * 512],
                                                   start=True, stop=True), [q_, kiTs], [pI])
                        op("act", lambda: A.activation(out=r_.ap, in_=pI.ap, func=AF.Relu, scale=w_.ap[:, h:h + 1]),
                           [pI, w_], [r_])
                        scs = sc.ap[:, kb * 512:(kb + 1) * 512]
                        if h == 0:
                            op("dve", lambda: V.tensor_scalar(out=scs, in0=r_.ap, scalar1=w_.ap[:, 8:9], scalar2=None,
                                                               op0=ALU.mult), [r_, w_], [sc])
                        else:
                            op("dve", lambda: V.scalar_tensor_tensor(out=scs, in0=r_.ap, scalar=w_.ap[:, 8 + h:9 + h],
                                                                     in1=scs, op0=ALU.mult, op1=ALU.add),
                               [r_, w_, sc], [sc])
                dg = sc.ap[:, qb * 512:(qb + 1) * 512]
                op("dve", lambda: V.tensor_tensor(out=dg, in0=dg, in1=dvl.ap[:, qs, :], op=ALU.mult), [sc, dvl], [sc])
                if it >= 2:
                    op("dve", lambda: V.tensor_reduce(out=b_.ap[:, 0:1], in_=sc.ap[:, 0:Lk], axis=AX.X, op=ALU.max),
                       [sc], [b_])
                    op("dve", lambda: V.tensor_reduce(out=b_.ap[:, 1:2], in_=sc.ap[:, 0:Lk], axis=AX.X, op=ALU.min),
                       [sc], [b_])
                op("dve", lambda: V.tensor_tensor(out=dg, in0=dg, in1=dng.ap[:, qs, :], op=ALU.add), [sc, dng], [sc])
                if it >= 2:
                    op("dve", lambda: V.tensor_tensor(out=b_.ap[:, 2:3], in0=b_.ap[:, 0:1], in1=b_.ap[:, 1:2],
                                                      op=ALU.subtract), [b_], [b_])
                    op("dve", lambda: V.tensor_scalar(out=hv.ap, in0=pw2.ap, scalar1=b_.ap[:, 2:3], scalar2=None,
                                                      op0=ALU.mult), [pw2, b_], [hv])
                    for k in range(NIT):
                        op("dve", lambda: V.tensor_tensor(out=b_.ap[:, 3:4], in0=b_.ap[:, 1:2], in1=hv.ap[:, k:k + 1],
                                                          op=ALU.add), [b_, hv], [b_])
                        op("dve", lambda: V.tensor_scalar(out=junkb.ap[:, 0:Lk], in0=sc.ap[:, 0:Lk],
                                                          scalar1=b_.ap[:, 3:4], scalar2=None, op0=ALU.is_ge,
                                                          op1=ALU.add, accum_out=b_.ap[:, 4:5]), [sc, b_], [junkb, b_])
                        op("dve", lambda: V.tensor_scalar(out=b_.ap[:, 5:6], in0=b_.ap[:, 4:5], scalar1=255.5,
                                                          scalar2=hv.ap[:, k:k + 1], op0=ALU.is_ge, op1=ALU.mult),
                           [b_, hv], [b_])
                        op("dve", lambda: V.tensor_tensor(out=b_.ap[:, 1:2], in0=b_.ap[:, 1:2], in1=b_.ap[:, 5:6],
                                                          op=ALU.add), [b_], [b_])
                else:
                    op("dve", lambda: V.memset(b_.ap[:, 1:2], -1e29), [], [b_])
                op("dve", lambda: V.tensor_scalar(out=sel.ap[:, 0:Lk], in0=sc.ap[:, 0:Lk], scalar1=b_.ap[:, 1:2],
                                                  scalar2=None, op0=ALU.is_ge), [sc, b_], [sel])
                nm_ = nms[it % 2]
                for g in range(nkb):
                    pt = pst[nI[2] % 2]
                    nI[2] += 1
                    for k in range(4):
                        kt = 4 * g + k
                        op("pe", lambda: PE.transpose(out=pt.ap[:, k, :], in_=sel.ap[:, kt * 128:(kt + 1) * 128],
                                                      identity=ident.ap), [sel, ident], [pt])
                    op("act", lambda: A.activation(out=nm_.ap[:, 4 * g:4 * g + 4, :], in_=pt.ap[:, 0:4, :], func=AF.Identity,
                                                   bias=float(NEGM), scale=float(-NEGM)), [pt], [nm_])
                dma("sp", NM[qb][:, 0:4 * nkb, qs * 128:(qs + 1) * 128], nm_.ap[:, 0:4 * nkb, :], reads=[nm_])
            sch.barrier()
            maybe_stop("B1")

        def attention(mixer):
            with ExitStack() as es:
                if mixer == "a":
                    H, width, col0 = 6, 768, 0
                    Vd, kTd, nkt_extra = Va, kaTn, True
                elif mixer == "b":
                    H, width, col0 = 5, 640, 768
                    Vd, kTd, nkt_extra = Vb, kbT, False
                else:
                    H, width, col0 = 5, 640, 1408
                    Vd, kTd, nkt_extra = Vc, kcT, False
                kTs = sb(es, "kTs", [128, H, S], BF16)
                Vs = sb(es, "Vs", [128, NT, H * VW], BF16)
                for h in range(H):
                    dma("sp", kTs.ap[:, h, :], kTd[h], writes=[kTs])
                for t0 in range(0, NT, 8):
                    t1 = min(NT, t0 + 8)
                    dma("sp", Vs.ap[:, t0:t1, :], Vd[t0 * 128:t1 * 128, :].rearrange("(t p) c -> p t c", p=128),
                        writes=[Vs])
                if mixer == "a":
                    krs = sb(es, "krs", [128, S], BF16)
                    dma("sp", krs.ap, kropeT[0], writes=[krs])
                    qrb = [sb(es, "qrb", [128, 3, 512], BF16) for _ in range(2)]
                if mixer == "b":
                    nmT = [sb(es, "nmT", [128, 4 * NQB, 512], BF16) for _ in range(1)]
                if mixer == "c":
                    subg = sb(es, "subg", [128, 128], F32)
                    lamt = sb(es, "lamt", [128, 4, 64], F32)
                    lams = sb(es, "lams", [128, 8], F32)
                    dma("sp", subg.ap, bcast_rows(subln_g_d[L:L + 1, :], 128), writes=[subg])
                    for i, n in enumerate(("lam_q1", "lam_k1", "lam_q2", "lam_k2")):
                        dma("sp", lamt.ap[:, i, :], bcast_rows(lam_d[n][L:L + 1, :], 64), writes=[lamt])
                    op("dve", lambda: V.tensor_scalar(out=subg.ap, in0=subg.ap, scalar1=float(1.0 - lam_init),
                                                      scalar2=None, op0=ALU.mult), [subg], [subg])
                    for j in range(2):
                        op("dve", lambda: V.tensor_tensor(out=lamt.ap[:, 2 * j, :], in0=lamt.ap[:, 2 * j, :],
                                                          in1=lamt.ap[:, 2 * j + 1, :], op=ALU.mult), [lamt], [lamt])
                        op("dve", lambda: V.reduce_sum(out=lams.ap[:, j:j + 1], in_=lamt.ap[:, 2 * j, :], axis=AX.X),
                           [lamt], [lams])
                    op("act", lambda: A.activation(out=lams.ap[:, 2:4], in_=lams.ap[:, 0:2], func=AF.Exp), [lams], [lams])
                    op("dve", lambda: V.tensor_tensor(out=lams.ap[:, 4:5], in0=lams.ap[:, 3:4], in1=lams.ap[:, 2:3],
                                                      op=ALU.subtract), [lams], [lams])
                    op("dve", lambda: V.tensor_scalar(out=lams.ap[:, 5:6], in0=lams.ap[:, 4:5], scalar1=float(-lam_init),
                                                      scalar2=None, op0=ALU.add), [lams], [lams])
                    t1s = [sb(es, "t1s", [128, 4, 128], F32) for _ in range(1)]
                    osb = [sb(es, "osb", [128, 128], F32) for _ in range(2)]
                    junkc = sb(es, "junkc", [128, 128], F32)
                qnb = [sb(es, "qnb", [128, H, 512], BF16) for _ in range(2)]
                sgb = [sb(es, "sgb", [128, 4, width], F32) for _ in range(2)]
                mxo = [sb(es, "mxo", [128, 4, width], BF16) for _ in range(2)]
                PTs = [sb(es, "PT", [128, 512], BF16) for _ in range(4)]
                rd = [sb(es, "rd", [128, 4], F32) for _ in range(4)]
                pS = [ps(es, "pS", [128, 512], F32) for _ in range(3)]
                pO = [[ps(es, "pO", [128, 512], F32) for _ in range(2)] for _ in range(2)]
                n = {"S": 0, "P": 0, "O": 0, "rd": 0, "osb": 0}

                def units_of(h):
                    if mixer == "c":
                        return [(h, 1), (h, 2)]
                    return [(h, 0)]

                for qb in range(NQB):
                    qq = qnb[qb % 2]
                    qsl = slice(qb * 512, (qb + 1) * 512)
                    qsrc = {"a": qaTn, "b": qbT, "c": qcT}[mixer]
                    dma("sp", qq.ap, qsrc.rearrange("n p s -> p n s")[:, :, qsl], writes=[qq])
                    if mixer == "a":
                        qr_ = qrb[qb % 2]
                        dma("sp", qr_.ap, qaTr.rearrange("n p s -> p n s")[:, :, qsl], writes=[qr_])
                    if mixer == "b":
                        nm_ = nmT[0]
                        dma("sp", nm_.ap[:, 0:4 * (qb + 1), :], NM[qb][:, 0:4 * (qb + 1), :], writes=[nm_])
                    sg_ = sgb[qb % 2]
                    dma("sp", sg_.ap, SG[qsl, col0:col0 + width].rearrange("(a p) c -> p a c", p=128), writes=[sg_])
                    mo = mxo[qb % 2]
                    nkt = 4 * (qb + 1)
                    for h in range(H):
                        for (hh, u) in units_of(h):
                            po = pO[n["O"] % 2]
                            n["O"] += 1
                            for kt in range(nkt):
                                j = kt - 4 * qb
                                c_lo = 128 * j if j > 0 else 0
                                cs_ = slice(c_lo, 512)
                                ksl = slice(kt * 128, (kt + 1) * 128)
                                s_ = pS[n["S"] % 3]
                                n["S"] += 1
                                need_mask = (mixer == "b") or (j >= 0)
                                if mixer == "a":
                                    lo_p = 64 * (h % 2)
                                    op("pe", lambda: PE.matmul(s_.ap[:, cs_], lhsT=kTs.ap[:, h, ksl], rhs=qq.ap[:, h, cs_],
                                                               start=True, stop=False), [kTs, qq], [s_])
                                    op("pe", lambda: PE.matmul(s_.ap[:, cs_], lhsT=krs.ap[lo_p:lo_p + 64, ksl],
                                                               rhs=qr_.ap[lo_p:lo_p + 64, h // 2, cs_],
                                                               start=False, stop=not need_mask), [krs, qr_], [s_])
                                elif mixer == "b":
                                    op("pe", lambda: PE.matmul(s_.ap[:, cs_], lhsT=kTs.ap[:, h, ksl], rhs=qq.ap[:, h, cs_],
                                                               start=True, stop=False), [kTs, qq], [s_])
                                else:
                                    lo_p = 64 * (u - 1)
                                    op("pe", lambda: PE.matmul(s_.ap[:, cs_], lhsT=kTs.ap[lo_p:lo_p + 64, h, ksl],
                                                               rhs=qq.ap[lo_p:lo_p + 64, h, cs_],
                                                               start=True, stop=not need_mask), [kTs, qq], [s_])
                                if need_mask:
                                    if mixer == "b":
                                        op("pe", lambda: PE.matmul(s_.ap[:, cs_], lhsT=ident.ap, rhs=nm_.ap[:, kt, cs_],
                                                                   start=False, stop=True), [ident, nm_], [s_])
                                    else:
                                        op("pe", lambda: PE.matmul(s_.ap[:, cs_], lhsT=ident.ap, rhs=dmask.ap[:, j, cs_],
                                                                   start=False, stop=True), [ident, dmask], [s_])
                                P_ = PTs[n["P"] % 4]
                                n["P"] += 1
                                op("act", lambda: A.activation(out=P_.ap[:, cs_], in_=s_.ap[:, cs_], func=AF.Exp),
                                   [s_], [P_])
                                for qs in range(max(j, 0), 4):
                                    bank = po[qs // 2]
                                    oc = (qs % 2) * 256
                                    first = (kt == 0 and qs % 2 == 0)
                                    op("pe", lambda: PE.matmul(bank.ap[:, oc:oc + 129], lhsT=P_.ap[:, qs * 128:(qs + 1) * 128],
                                                               rhs=Vs.ap[:, kt, h * VW:h * VW + 129],
                                                               start=first, stop=(kt == 4 * qb + qs),
                                                               skip_group_check=True), [P_, Vs], [bank])
                            for qs in range(4):
                                bank = po[qs // 2]
                                oc = (qs % 2) * 256
                                r_ = rd[n["rd"] % 4]
                                n["rd"] += 1
                                op("dve", lambda: V.reciprocal(out=r_.ap[:, 0:1], in_=bank.ap[:, oc + 128:oc + 129]),
                                   [bank], [r_])
                                dst = mo.ap[:, qs, h * 128:(h + 1) * 128]
                                gsl = sg_.ap[:, qs, h * 128:(h + 1) * 128]
                                if mixer != "c":
                                    op("dve", lambda: V.scalar_tensor_tensor(out=dst, in0=bank.ap[:, oc:oc + 128],
                                                                             scalar=r_.ap[:, 0:1], in1=gsl,
                                                                             op0=ALU.mult, op1=ALU.mult),
                                       [bank, r_, sg_], [mo])
                                elif u == 1:
                                    t1_ = t1s[0]
                                    op("act", lambda: A.mul(out=t1_.ap[:, qs, :], in_=bank.ap[:, oc:oc + 128],
                                                            mul=r_.ap[:, 0:1]), [bank, r_], [t1_])
                                else:
                                    t1_ = t1s[0]
                                    o_ = osb[n["osb"] % 2]
                                    n["osb"] += 1
                                    op("dve", lambda: V.tensor_tensor(out=r_.ap[:, 1:2], in0=r_.ap[:, 0:1],
                                                                      in1=lams.ap[:, 5:6], op=ALU.mult), [r_, lams], [r_])
                                    op("dve", lambda: V.scalar_tensor_tensor(out=o_.ap, in0=bank.ap[:, oc:oc + 128],
                                                                             scalar=r_.ap[:, 1:2], in1=t1_.ap[:, qs, :],
                                                                             op0=ALU.mult, op1=ALU.add),
                                       [bank, r_, t1_], [o_])
                                    op("act", lambda: A.activation(out=junkc.ap, in_=o_.ap, func=AF.Square,
                                                                   accum_out=r_.ap[:, 2:3]), [o_], [junkc, r_])
                                    op("act", lambda: A.activation(out=r_.ap[:, 3:4], in_=r_.ap[:, 2:3], func=AF.Sqrt,
                                                                   bias=EPS, scale=1.0 / 128.0), [r_], [r_])
                                    op("dve", lambda: V.reciprocal(out=r_.ap[:, 2:3], in_=r_.ap[:, 3:4]), [r_], [r_])
                                    op("dve", lambda: V.scalar_tensor_tensor(out=o_.ap, in0=o_.ap, scalar=r_.ap[:, 2:3],
                                                                             in1=subg.ap, op0=ALU.mult, op1=ALU.mult),
                                       [o_, r_, subg], [o_])
                                    op("dve", lambda: V.tensor_tensor(out=dst, in0=o_.ap, in1=gsl, op=ALU.mult),
                                       [o_, sg_], [mo])
                    dma("sp", MX[qsl, col0:col0 + width].rearrange("(a p) c -> p a c", p=128), mo.ap, reads=[mo])
                sch.barrier()
                maybe_stop("B" + mixer)

        attention("a")
        attention("b")
        attention("c")

        with ExitStack() as es:
            wo = sb(es, "wo", [128, 16, D], BF16)
            for c in range(4):
                for kc0 in range(0, 16, 8):
                    dma("pool", wo.ap[:, kc0:kc0 + 8, c * 512:(c + 1) * 512],
                        w_o_d[L].rearrange("(c p) n -> p c n", p=128)[:, kc0:kc0 + 8, c * 512:(c + 1) * 512], writes=[wo])
            mxt = [sb(es, "mxt", [128, D], BF16) for _ in range(2)]
            mT = [sb(es, "mT", [128, 16, 128], BF16) for _ in range(2)]
            hin = [sb(es, "hin", [128, D], F32) for _ in range(2)]
            h1 = [sb(es, "h1", [128, D], F32) for _ in range(2)]
            h1b = [sb(es, "h1b", [128, D], BF16) for _ in range(2)]
            h1T = [sb(es, "h1T", [128, 16, 128], BF16) for _ in range(2)]
            pT = [ps(es, "pT", [128, 8, 128], BF16) for _ in range(4)]
            pz = [ps(es, "pz", [128, 512], F32) for _ in range(4)]
            npT = 0
            for t in range(NT):
                i = t % 2
                rs = slice(t * 128, (t + 1) * 128)
                dma("sp", mxt[i].ap, MX[rs, :], writes=[mxt[i]])
                dma("sp", hin[i].ap, h_src[rs, :], writes=[hin[i]])
                for g in range(2):
                    pt = pT[npT % 4]
                    npT += 1
                    for c in range(8):
                        cc = 8 * g + c
                        op("pe", lambda: PE.transpose(out=pt.ap[:, c, :], in_=mxt[i].ap[:, cc * 128:(cc + 1) * 128],
                                                      identity=ident.ap), [mxt[i], ident], [pt])
                    op("act", lambda: A.copy(out=mT[i].ap[:, 8 * g:8 * g + 8, :], in_=pt.ap), [pt], [mT[i]])
                for c in range(4):
                    z = pz[c]
                    for kc in range(16):
                        op("pe", lambda: PE.matmul(z.ap, lhsT=mT[i].ap[:, kc, :], rhs=wo.ap[:, kc, c * 512:(c + 1) * 512],
                                                   start=(kc == 0), stop=(kc == 15)), [mT[i], wo], [z])
                    op("dve", lambda: V.tensor_tensor(out=h1[i].ap[:, c * 512:(c + 1) * 512], in0=z.ap,
                                                      in1=hin[i].ap[:, c * 512:(c + 1) * 512], op=ALU.add),
                       [z, hin[i]], [h1[i]])
                dma("sp", hbuf[rs, :], h1[i].ap, reads=[h1[i]])
                op("dve", lambda: V.tensor_copy(out=h1b[i].ap, in_=h1[i].ap), [h1[i]], [h1b[i]])
                for g in range(2):
                    pt = pT[npT % 4]
                    npT += 1
                    for c in range(8):
                        cc = 8 * g + c
                        op("pe", lambda: PE.transpose(out=pt.ap[:, c, :], in_=h1b[i].ap[:, cc * 128:(cc + 1) * 128],
                                                      identity=ident.ap), [h1b[i], ident], [pt])
                    op("act", lambda: A.copy(out=h1T[i].ap[:, 8 * g:8 * g + 8, :], in_=pt.ap), [pt], [h1T[i]])
                dma("sp", uT_d[t], h1T[i].ap, reads=[h1T[i]])
            sch.barrier()
            maybe_stop("C1")

        with ExitStack() as es:
            wpg = sb(es, "wpg", [128, 16, D], BF16)
            wple = sb(es, "wple", [128, 2, D], BF16)
            for c in range(4):
                for kc0 in range(0, 16, 8):
                    dma("pool", wpg.ap[:, kc0:kc0 + 8, c * 512:(c + 1) * 512],
                        w_pg_d[L].rearrange("(c p) n -> p c n", p=128)[:, kc0:kc0 + 8, c * 512:(c + 1) * 512], writes=[wpg])
            dma("pool", wple.ap, w_ple_d[L].rearrange("(c p) n -> p c n", p=128), writes=[wple])
            if last:
                fgb = sb(es, "fgb", [128, D], F32)
                dma("sp", fgb.ap, bcast_rows(final_g_d[0:1, :], D), writes=[fgb])
                junk = sb(es, "junk", [128, D], BF16)
                st = [sb(es, "st", [128, 2], F32) for _ in range(2)]
            h1 = [sb(es, "h1", [128, D], F32) for _ in range(2)]
            h1T = [sb(es, "h1T", [128, 16, 128], BF16) for _ in range(2)]
            pt_ = [sb(es, "pt_", [128, PLE], F32) for _ in range(2)]
            pb = [sb(es, "pb", [128, PLE], BF16) for _ in range(2)]
            pTs = [sb(es, "pTs", [128, 2, 128], BF16) for _ in range(2)]
            sg = [sb(es, "sg", [128, 512], F32) for _ in range(2)]
            tm = [sb(es, "tm", [128, 512], F32) for _ in range(2)]
            h2 = [sb(es, "h2", [128, D], F32) for _ in range(2)]
            pz = [ps(es, "pz", [128, 512], F32) for _ in range(6)]
            pT = [ps(es, "pT", [128, 8, 128], BF16) for _ in range(2)]
            nz = 0
            ns = 0
            for t in range(NT):
                i = t % 2
                rs = slice(t * 128, (t + 1) * 128)
                dma("sp", h1[i].ap, hbuf[rs, :], writes=[h1[i]])
                dma("sp", h1T[i].ap, uT_d[t], writes=[h1T[i]])
                dma("sp", pt_[i].ap, p_d[L][rs, :], writes=[pt_[i]])
                op("dve", lambda: V.tensor_copy(out=pb[i].ap, in_=pt_[i].ap), [pt_[i]], [pb[i]])
                for c in range(2):
                    op("pe", lambda: PE.transpose(out=pT[i].ap[:, c, :], in_=pb[i].ap[:, c * 128:(c + 1) * 128],
                                                  identity=ident.ap), [pb[i], ident], [pT[i]])
                op("act", lambda: A.copy(out=pTs[i].ap, in_=pT[i].ap[:, 0:2, :]), [pT[i]], [pTs[i]])
                for c in range(4):
                    cs_ = slice(c * 512, (c + 1) * 512)
                    zg = pz[nz % 6]
                    nz += 1
                    for kc in range(16):
                        op("pe", lambda: PE.matmul(zg.ap, lhsT=h1T[i].ap[:, kc, :], rhs=wpg.ap[:, kc, cs_],
                                                   start=(kc == 0), stop=(kc == 15)), [h1T[i], wpg], [zg])
                    zp = pz[nz % 6]
                    nz += 1
                    for kc in range(2):
                        op("pe", lambda: PE.matmul(zp.ap, lhsT=pTs[i].ap[:, kc, :], rhs=wple.ap[:, kc, cs_],
                                                   start=(kc == 0), stop=(kc == 1)), [pTs[i], wple], [zp])
                    s_ = sg[ns % 2]
                    t_ = tm[ns % 2]
                    ns += 1
                    op("act", lambda: A.activation(out=s_.ap, in_=zg.ap, func=AF.Sigmoid), [zg], [s_])
                    op("dve", lambda: V.tensor_tensor(out=t_.ap, in0=zp.ap, in1=s_.ap, op=ALU.mult), [zp, s_], [t_])
                    op("pool", lambda: G.tensor_tensor(out=h2[i].ap[:, cs_], in0=t_.ap, in1=h1[i].ap[:, cs_], op=ALU.add),
                       [t_, h1[i]], [h2[i]])
                if not last:
                    dma("sp", hbuf[rs, :], h2[i].ap, reads=[h2[i]])
                else:
                    rms_rstd(es, h2[i].ap, D, junk, st[i], [h2[i]])
                    op("dve", lambda: V.scalar_tensor_tensor(out=h1[i].ap, in0=h2[i].ap, scalar=st[i].ap[:, 0:1],
                                                             in1=fgb.ap, op0=ALU.mult, op1=ALU.mult),
                       [h2[i], st[i], fgb], [h1[i]])
                    dma("sp", y_d[rs, :], h1[i].ap, reads=[h1[i]])
            sch.barrier()
            maybe_stop("C2")

    except _StopBuild:
        return nc, sch
    top.close()
    return nc, sch


_CACHE = {}


def kernel(x, p, positions, w_in, w_uq, w_ukv, w_o, norm_g, q_norm_g, kv_norm_g,
           lam_q1, lam_k1, lam_q2, lam_k2, subln_g, w_ple, w_pg, final_g):
    x = np.asarray(x)
    B, S, _ = x.shape
    DEPTH = int(np.asarray(w_in).shape[0])
    key = (S, DEPTH)
    if key not in _CACHE:
        _CACHE[key] = build_program(S, DEPTH)[0]
    nc = _CACHE[key]
    consts = host_consts()
    f32 = lambda a: np.ascontiguousarray(np.asarray(a), dtype=np.float32)
    shared = {
        "w_in": f32(w_in), "w_uq": f32(w_uq), "w_ukv": f32(w_ukv), "w_o": f32(w_o),
        "norm_g": f32(norm_g), "q_norm_g": f32(q_norm_g), "kv_norm_g": f32(kv_norm_g),
        "lam_q1": f32(lam_q1), "lam_k1": f32(lam_k1), "lam_q2": f32(lam_q2), "lam_k2": f32(lam_k2),
        "subln_g": f32(subln_g), "w_ple": f32(w_ple), "w_pg": f32(w_pg),
        "final_g": f32(final_g).reshape(1, D),
    }
    shared.update(consts)
    p = np.asarray(p)
    positions = np.asarray(positions)
    in_maps = []
    for c in range(8):
        b = c % B
        m = dict(shared)
        m["x"] = f32(x[b])
        m["p"] = f32(p[:, b])
        m["positions"] = np.ascontiguousarray(positions[b].astype(np.int32).reshape(S // 128, 128).T)
        in_maps.append(m)
    res = run_bass_kernel_spmd(nc, in_maps, core_ids=list(range(8)))
    out = np.stack([np.asarray(res.results[b]["y"], dtype=np.float32) for b in range(B)], axis=0)
    return out
```

```python
import math
import os
from contextlib import ExitStack

import numpy as np
import ml_dtypes
import concourse.bass as bass
import concourse.mybir as mybir
from concourse.bass_utils import run_bass_kernel_spmd

F32 = mybir.dt.float32
BF16 = mybir.dt.bfloat16
I32 = mybir.dt.int32
ALU = mybir.AluOpType
AF = mybir.ActivationFunctionType
AX = mybir.AxisListType

D = 2048
PLE = 256
D_IN = 7176
EPS = 1e-6
THETA = 500000.0
NIT = 22
NEGM = -30000.0
SA = 192.0 ** -0.5
SB = 128.0 ** -0.5
SC = 64.0 ** -0.5
VW = 132


class Buf:
    __slots__ = ("name", "writers", "readers", "ap", "psum")

    def __init__(self, name="", ap=None, psum=False):
        self.psum = psum
        self.name = name
        self.writers = {}
        self.readers = {}
        self.ap = ap


class Sched:
    NDMA = 8

    def __init__(self, nc):
        self.nc = nc
        self.eng = {"pe": nc.tensor, "act": nc.scalar, "dve": nc.vector,
                    "pool": nc.gpsimd, "sp": nc.sync}
        self.sems = {}
        self.cnt = {}
        for e in ("pe", "act", "dve", "pool"):
            self.sems[e] = nc.alloc_semaphore("s_" + e)
            self.cnt[e] = 0
        self.dq = {}
        for q in ("sp", "pool"):
            for i in range(self.NDMA):
                k = "d_%s%d" % (q, i)
                self.sems[k] = nc.alloc_semaphore(k)
                self.cnt[k] = 0
            self.dq[q] = 0
        self.seen = {e: {} for e in self.eng}
        self.nins = 0
        self.nwait = 0

    def _wait(self, e, evs):
        best = {}
        for (k, v) in evs:
            if v > best.get(k, 0):
                best[k] = v
        seen = self.seen[e]
        for k, v in best.items():
            if k == "pe" and e == "pe":
                continue
            if seen.get(k, 0) < v:
                self.eng[e].wait_ge(self.sems[k], v)
                seen[k] = v
                self.nwait += 1

    @staticmethod
    def _deps(reads, writes, e=None):
        evs = []
        for b in reads:
            evs.extend(b.writers.items())
            if b.psum:
                evs.extend((k, v) for k, v in b.readers.items() if k != e)
        for b in writes:
            evs.extend(b.writers.items())
            evs.extend(b.readers.items())
        return evs

    @staticmethod
    def _commit(ev, reads, writes):
        k, v = ev
        for b in reads:
            if b.readers.get(k, 0) < v:
                b.readers[k] = v
        for b in writes:
            b.writers = {k: v}
            b.readers = {}

    def op(self, e, ins_fn, reads=(), writes=()):
        self._wait(e, self._deps(reads, writes, e))
        ins = ins_fn()
        self.cnt[e] += 1
        ins.then_inc(self.sems[e], 1)
        self._commit((e, self.cnt[e]), reads, writes)
        self.nins += 1

    def dma(self, q, out, in_, reads=(), writes=()):
        i = self.dq[q]
        self.dq[q] = (i + 1) % self.NDMA
        k = "d_%s%d" % (q, i)
        evs = self._deps(reads, writes)
        if self.cnt[k] > 0:
            evs.append((k, self.cnt[k]))
        self._wait(q, evs)
        ins = self.eng[q].dma_start(out=out, in_=in_)
        self.cnt[k] += 16
        ins.then_inc(self.sems[k], 16)
        self._commit((k, self.cnt[k]), reads, writes)
        self.nins += 1

    def barrier(self):
        evs = [(k, v) for k, v in self.cnt.items() if v > 0]
        for e in self.eng:
            self._wait(e, list(evs))


def host_consts():
    c = {}
    c["ident"] = np.eye(128, dtype=np.float32).astype(ml_dtypes.bfloat16)
    invf = np.zeros((128, 56), np.float32)
    off = 0
    for n_rot in (64, 32, 16):
        half = n_rot // 2
        f = 1.0 / (np.float32(THETA) ** (np.arange(half, dtype=np.float32) * np.float32(2.0 / n_rot)))
        invf[:, off:off + half] = f.astype(np.float32)[None, :]
        off += half
    c["invf"] = invf
    dm = np.zeros((4, 128, 512), np.float32)
    for j in range(4):
        kk = 128 * j + np.arange(128)[:, None]
        qq = np.arange(512)[None, :]
        dm[j] = np.where((kk // 64) <= (qq // 64), 0.0, NEGM)
    c["dmask"] = dm.astype(ml_dtypes.bfloat16)
    dv = np.zeros((4, 128, 512), np.float32)
    for qs in range(4):
        qq = 128 * qs + np.arange(128)[:, None]
        kk = np.arange(512)[None, :]
        dv[qs] = ((kk // 64) <= (qq // 64)).astype(np.float32)
    c["dvalid"] = dv
    c["dneg"] = ((dv - 1.0) * 1e30).astype(np.float32)
    c["pow2"] = np.tile((0.5 ** np.arange(1, NIT + 1, dtype=np.float64)).astype(np.float32)[None, :], (128, 1))
    return c


CHUNKS = [
    (0, 384, "cq", None),
    (384, 320, "ckv", None),
    (704, 512, "gate", 0),
    (1216, 256, "gate", 512),
    (1472, 512, "qb", (0, 4)),
    (1984, 128, "qb", (4, 1)),
    (2112, 512, "kb", (0, 4)),
    (2624, 128, "kb", (4, 1)),
    (2752, 512, "vb", (0, 4)),
    (3264, 128, "vb", (4, 1)),
    (3392, 512, "qi", None),
    (3904, 72, "kiw", None),
    (3976, 512, "gate", 768),
    (4488, 128, "gate", 1280),
    (4616, 512, "qc", (0, 8)),
    (5128, 128, "qc", (8, 2)),
    (5256, 512, "kc", (0, 8)),
    (5768, 128, "kc", (8, 2)),
    (5896, 512, "vc", (0, 4)),
    (6408, 128, "vc", (4, 1)),
    (6536, 512, "gate", 1408),
    (7048, 128, "gate", 1920),
]
SEGS = [(2 * i, 2 * i + 1) for i in range(11)]


class _StopBuild(Exception):
    pass


def build_program(S, DEPTH, debug_layers=None):
    assert S % 512 == 0
    STOP = os.environ.get("MK_STOP", "")

    def maybe_stop(tag):
        if STOP == tag:
            raise _StopBuild()
    NT = S // 128
    NQB = S // 512
    nc = bass.Bass("TRN2", target_bir_lowering=False)
    sch = Sched(nc)
    uid = [0]

    def din(name, shape, dt):
        return nc.dram_tensor(name, list(shape), dt, kind="ExternalInput").ap()

    def dscr(name, shape, dt):
        return nc.dram_tensor(name, list(shape), dt).ap()

    x_d = din("x", [S, D], F32)
    p_d = din("p", [DEPTH, S, PLE], F32)
    pos_d = din("positions", [128, S // 128], I32)
    w_in_d = din("w_in", [DEPTH, D, D_IN], F32)
    w_uq_d = din("w_uq", [DEPTH, 384, 1152], F32)
    w_ukv_d = din("w_ukv", [DEPTH, 256, 1536], F32)
    w_o_d = din("w_o", [DEPTH, D, D], F32)
    norm_g_d = din("norm_g", [DEPTH, D], F32)
    q_norm_g_d = din("q_norm_g", [DEPTH, 384], F32)
    kv_norm_g_d = din("kv_norm_g", [DEPTH, 256], F32)
    lam_d = {n: din(n, [DEPTH, 64], F32) for n in ("lam_q1", "lam_k1", "lam_q2", "lam_k2")}
    subln_g_d = din("subln_g", [DEPTH, 128], F32)
    w_ple_d = din("w_ple", [DEPTH, PLE, D], F32)
    w_pg_d = din("w_pg", [DEPTH, D, D], F32)
    final_g_d = din("final_g", [1, D], F32)
    ident_d = din("ident", [128, 128], BF16)
    invf_d = din("invf", [128, 56], F32)
    dmask_d = din("dmask", [4, 128, 512], BF16)
    dvalid_d = din("dvalid", [4, 128, 512], F32)
    dneg_d = din("dneg", [4, 128, 512], F32)
    pow2_d = din("pow2", [128, NIT], F32)
    y_d = nc.dram_tensor("y", [S, D], F32, kind="ExternalOutput").ap()

    hbuf = dscr("hbuf", [S, D], F32)
    uT_d = dscr("uT_d", [NT, 128, 16, 128], BF16)
    rope_d = dscr("rope_d", [128, NT, 224], F32)
    qaTn = dscr("qaTn", [6, 128, S], BF16)
    qaTr = dscr("qaTr", [3, 128, S], BF16)
    kaTn = dscr("kaTn", [6, 128, S], BF16)
    kropeT = dscr("kropeT", [1, 128, S], BF16)
    Va = dscr("Va", [S, 6 * VW], BF16)
    qbT = dscr("qbT", [5, 128, S], BF16)
    kbT = dscr("kbT", [5, 128, S], BF16)
    Vb = dscr("Vb", [S, 5 * VW], BF16)
    qiT = dscr("qiT", [4, 128, S], BF16)
    kiT = dscr("kiT", [1, 128, S], BF16)
    WI = dscr("WI", [S, 16], F32)
    qcT = dscr("qcT", [5, 128, S], BF16)
    kcT = dscr("kcT", [5, 128, S], BF16)
    Vc = dscr("Vc", [S, 5 * VW], BF16)
    SG = dscr("SG", [S, D], F32)
    MX = dscr("MX", [S, D], BF16)
    NM = dscr("NM", [NQB, 128, 4 * NQB, 512], BF16)

    def sb(es, name, shape, dt):
        uid[0] += 1
        h = es.enter_context(nc.sbuf_tensor("%s_%d" % (name, uid[0]), list(shape), dt))
        return Buf(name, h.ap())

    def ps(es, name, shape, dt):
        uid[0] += 1
        h = es.enter_context(nc.psum_tensor("%s_%d" % (name, uid[0]), list(shape), dt))
        return Buf(name, h.ap(), psum=True)

    V = nc.vector
    G = nc.gpsimd
    A = nc.scalar
    PE = nc.tensor
    op = sch.op
    dma = sch.dma

    top = ExitStack()
    ident = sb(top, "ident", [128, 128], BF16)
    dmask = sb(top, "dmask", [128, 4, 512], BF16)
    dma("sp", ident.ap, ident_d, writes=[ident])
    dma("sp", dmask.ap, dmask_d.rearrange("j p q -> p j q"), writes=[dmask])

    def bcast_rows(src_row_ap, n):
        return src_row_ap.to_broadcast([128, n])

    ROFF = {"aq": (0, 32), "ak": (64, 32), "bq": (128, 16), "bk": (160, 16), "i": (192, 8), "cq": (208, 8)}
    with ExitStack() as es:
        posi = sb(es, "posi", [128, NT], I32)
        posf = sb(es, "posf", [128, NT], F32)
        invf = sb(es, "invf", [128, 56], F32)
        ang = sb(es, "ang", [128, NT, 56], F32)
        r1 = sb(es, "r1", [128, NT, 56], F32)
        cs = sb(es, "cs", [128, NT, 56], F32)
        sn = sb(es, "sn", [128, NT, 56], F32)
        tab = sb(es, "tab", [128, NT, 224], F32)
        dma("sp", posi.ap, pos_d, writes=[posi])
        dma("sp", invf.ap, invf_d, writes=[invf])
        op("dve", lambda: V.tensor_copy(out=posf.ap, in_=posi.ap), [posi], [posf])
        for t in range(NT):
            op("dve", lambda: V.tensor_scalar(out=ang.ap[:, t, :], in0=invf.ap, scalar1=posf.ap[:, t:t + 1],
                                              scalar2=None, op0=ALU.mult), [invf, posf], [ang])
        twopi = 2.0 * math.pi
        ki = sb(es, "ki", [128, NT, 56], I32)
        kf = sb(es, "kf", [128, NT, 56], F32)

        def sin_of(shift, dst):
            op("dve", lambda: V.tensor_scalar(out=r1.ap, in0=ang.ap, scalar1=float(shift), scalar2=None, op0=ALU.add),
               [ang], [r1])
            op("dve", lambda: V.tensor_scalar(out=kf.ap, in0=r1.ap, scalar1=1.0 / twopi, scalar2=None, op0=ALU.mult),
               [r1], [kf])
            op("dve", lambda: V.tensor_copy(out=ki.ap, in_=kf.ap), [kf], [ki])
            op("dve", lambda: V.tensor_copy(out=kf.ap, in_=ki.ap), [ki], [kf])
            op("dve", lambda: V.scalar_tensor_tensor(out=r1.ap, in0=kf.ap, scalar=-twopi, in1=r1.ap,
                                                     op0=ALU.mult, op1=ALU.add), [kf, r1], [r1])
            op("dve", lambda: V.tensor_scalar(out=kf.ap, in0=r1.ap, scalar1=math.pi, scalar2=-twopi,
                                              op0=ALU.is_gt, op1=ALU.mult), [r1], [kf])
            op("dve", lambda: V.tensor_tensor(out=r1.ap, in0=r1.ap, in1=kf.ap, op=ALU.add), [r1, kf], [r1])
            op("dve", lambda: V.tensor_scalar(out=kf.ap, in0=r1.ap, scalar1=-math.pi, scalar2=twopi,
                                              op0=ALU.is_lt, op1=ALU.mult), [r1], [kf])
            op("dve", lambda: V.tensor_tensor(out=r1.ap, in0=r1.ap, in1=kf.ap, op=ALU.add), [r1, kf], [r1])
            op("act", lambda: A.activation(out=dst.ap, in_=r1.ap, func=AF.Sin), [r1], [dst])

        sin_of(0.0, sn)
        sin_of(0.5 * math.pi, cs)
        specs = [("aq", 0, SA), ("ak", 0, 1.0), ("bq", 32, SB), ("bk", 32, 1.0), ("i", 48, 1.0), ("cq", 48, SC)]
        for (nm, so, scl) in specs:
            o, hf = ROFF[nm]
            op("dve", lambda: V.tensor_scalar(out=tab.ap[:, :, o:o + hf], in0=cs.ap[:, :, so:so + hf],
                                              scalar1=float(scl), scalar2=None, op0=ALU.mult), [cs], [tab])
            op("dve", lambda: V.tensor_scalar(out=tab.ap[:, :, o + hf:o + 2 * hf], in0=sn.ap[:, :, so:so + hf],
                                              scalar1=float(scl), scalar2=None, op0=ALU.mult), [sn], [tab])
        dma("sp", rope_d, tab.ap, reads=[tab])
        sch.barrier()

    def rms_rstd(es_tmp, src_ap, n, junk, st, reads):
        op("act", lambda: A.activation(out=junk.ap[:, 0:n], in_=src_ap, func=AF.Square, accum_out=st.ap[:, 0:1]),
           reads, [junk, st])
        op("act", lambda: A.activation(out=st.ap[:, 1:2], in_=st.ap[:, 0:1], func=AF.Sqrt, bias=EPS, scale=1.0 / n),
           [st], [st])
        op("dve", lambda: V.reciprocal(out=st.ap[:, 0:1], in_=st.ap[:, 1:2]), [st], [st])

    maybe_stop_holder = [None]
    try:
      maybe_stop("P")
      for L in range(DEPTH):
        h_src = x_d if L == 0 else hbuf
        last = (L == DEPTH - 1)
        lam_init = 0.8 - 0.6 * math.exp(-0.3 * L)

        with ExitStack() as es:
            gbc = sb(es, "gbc", [128, D], F32)
            dma("sp", gbc.ap, bcast_rows(norm_g_d[L:L + 1, :], D), writes=[gbc])
            hin = [sb(es, "hin", [128, D], F32) for _ in range(2)]
            ub = [sb(es, "ub", [128, D], BF16) for _ in range(2)]
            uTs = [sb(es, "uTs", [128, 16, 128], BF16) for _ in range(2)]
            junk = sb(es, "junk", [128, D], BF16)
            st = [sb(es, "st", [128, 2], F32) for _ in range(2)]
            pT = [ps(es, "pT", [128, 8, 128], BF16) for _ in range(4)]
            npT = 0
            dma("sp", hin[0].ap, h_src[0:128, :], writes=[hin[0]])
            for t in range(NT):
                i = t % 2
                if t + 1 < NT:
                    dma("sp", hin[1 - i].ap, h_src[(t + 1) * 128:(t + 2) * 128, :], writes=[hin[1 - i]])
                rms_rstd(es, hin[i].ap, D, junk, st[i], [hin[i]])
                op("dve", lambda: V.scalar_tensor_tensor(out=ub[i].ap, in0=hin[i].ap, scalar=st[i].ap[:, 0:1],
                                                         in1=gbc.ap, op0=ALU.mult, op1=ALU.mult),
                   [hin[i], st[i], gbc], [ub[i]])
                for g in range(2):
                    pt = pT[npT % 4]
                    npT += 1
                    for c in range(8):
                        cc = 8 * g + c
                        op("pe", lambda: PE.transpose(out=pt.ap[:, c, :], in_=ub[i].ap[:, cc * 128:(cc + 1) * 128],
                                                      identity=ident.ap), [ub[i], ident], [pt])
                    op("act", lambda: A.copy(out=uTs[i].ap[:, 8 * g:8 * g + 8, :], in_=pt.ap), [pt], [uTs[i]])
                dma("sp", uT_d[t], uTs[i].ap, reads=[uTs[i]])
            sch.barrier()
            maybe_stop("A1")

        with ExitStack() as es:
            WMAX = 768
            wbuf = [sb(es, "wbuf", [128, 16, WMAX], BF16) for _ in range(2)]
            wuq = sb(es, "wuq", [128, 3, 1152], BF16)
            wukv = sb(es, "wukv", [128, 2, 1536], BF16)
            qgb = sb(es, "qgb", [128, 384], F32)
            kvgb = sb(es, "kvgb", [128, 256], F32)
            tab = sb(es, "tab", [128, NT, 224], F32)
            uTs = [sb(es, "uTs", [128, 16, 128], BF16) for _ in range(3)]
            pz = [ps(es, "pz", [128, 512], F32) for _ in range(4)]
            ptr = [ps(es, "ptr", [128, 8, 128], BF16) for _ in range(2)]
            junkf = sb(es, "junkf", [128, 512], F32)
            st = sb(es, "st", [128, 2], F32)
            cqn = sb(es, "cqn", [128, 384], BF16)
            cqT = sb(es, "cqT", [128, 3, 128], BF16)
            ckvn = sb(es, "ckvn", [128, 256], BF16)
            ckvT = sb(es, "ckvT", [128, 2, 128], BF16)
            qn = sb(es, "qn", [128, 6, 128], BF16)
            qr = sb(es, "qr", [128, 6, 64], BF16)
            kn = sb(es, "kn", [128, 6, 128], BF16)
            kr = sb(es, "kr", [128, 2, 64], BF16)
            vx6 = sb(es, "vx6", [128, 6, VW], BF16)
            vx5 = sb(es, "vx5", [128, 5, VW], BF16)
            hd5 = sb(es, "hd5", [128, 5, 128], BF16)
            qis = sb(es, "qis", [128, 8, 64], BF16)
            kis = sb(es, "kis", [128, 2, 64], BF16)
            wi = sb(es, "wi", [128, 16], F32)
            sgt = [sb(es, "sgt", [128, 512], F32) for _ in range(2)]
            rt = [[sb(es, "rt", [128, 128], F32) for _ in range(4)] for _ in range(2)]
            stage = [sb(es, "stage", [128, 8, 128], BF16) for _ in range(2)]
            cnt = {"pz": 0, "ptr": 0, "rt": 0, "stage": 0, "sgt": 0}

            dma("sp", tab.ap, rope_d, writes=[tab])
            dma("sp", qgb.ap, bcast_rows(q_norm_g_d[L:L + 1, :], 384), writes=[qgb])
            dma("sp", kvgb.ap, bcast_rows(kv_norm_g_d[L:L + 1, :], 256), writes=[kvgb])
            dma("pool", wuq.ap, w_uq_d[L].rearrange("(c p) n -> p c n", p=128), writes=[wuq])
            dma("pool", wukv.ap, w_ukv_d[L].rearrange("(c p) n -> p c n", p=128), writes=[wukv])
            op("dve", lambda: V.memset(vx6.ap, 1.0), [], [vx6])
            op("dve", lambda: V.memset(vx5.ap, 1.0), [], [vx5])

            def load_w(si):
                c0 = CHUNKS[SEGS[si][0]][0]
                c1 = CHUNKS[SEGS[si][1]][0] + CHUNKS[SEGS[si][1]][1]
                wb = wbuf[si % 2]
                src = w_in_d[L].rearrange("(c p) n -> p c n", p=128)
                for kc0 in range(0, 16, 4):
                    dma("pool", wb.ap[:, kc0:kc0 + 4, 0:c1 - c0], src[:, kc0:kc0 + 4, c0:c1], writes=[wb])

            def nxt(key, lst):
                cnt[key] += 1
                return lst[cnt[key] % len(lst)]

            def rope_evac(src3, dst3, H, Dh, n_rot, tname, t, scale, reads, writes):
                half = n_rot // 2
                o, hf = ROFF[tname]
                assert hf == half
                cb = tab.ap[:, t, o:o + half].unsqueeze(1).to_broadcast([128, H, half])
                sbc = tab.ap[:, t, o + half:o + 2 * half].unsqueeze(1).to_broadcast([128, H, half])
                r = nxt("rt", rt)
                n = H * half
                v = [r[k].ap[:, 0:n].rearrange("p (h d) -> p h d", h=H) for k in range(4)]
                x1 = src3[:, :, 0:half]
                x2 = src3[:, :, half:n_rot]
                op("dve", lambda: V.tensor_tensor(out=v[0], in0=x1, in1=cb, op=ALU.mult), reads + [tab], [r[0]])
                op("dve", lambda: V.tensor_tensor(out=v[1], in0=x2, in1=sbc, op=ALU.mult), reads + [tab], [r[1]])
                op("dve", lambda: V.tensor_tensor(out=dst3[:, :, 0:half], in0=v[0], in1=v[1], op=ALU.subtract),
                   [r[0], r[1]], writes)
                op("dve", lambda: V.tensor_tensor(out=v[2], in0=x2, in1=cb, op=ALU.mult), reads + [tab], [r[2]])
                op("dve", lambda: V.tensor_tensor(out=v[3], in0=x1, in1=sbc, op=ALU.mult), reads + [tab], [r[3]])
                op("dve", lambda: V.tensor_tensor(out=dst3[:, :, half:n_rot], in0=v[2], in1=v[3], op=ALU.add),
                   [r[2], r[3]], writes)
                if Dh > n_rot:
                    op("act", lambda: A.mul(out=dst3[:, :, n_rot:Dh], in_=src3[:, :, n_rot:Dh], mul=float(scale)),
                       reads, writes)

            def fm_store(src, src3, n, dst, t):
                for g0 in range(0, n, 8):
                    g1 = min(n, g0 + 8)
                    pt = nxt("ptr", ptr)
                    sg_ = nxt("stage", stage)
                    for k in range(g0, g1):
                        op("pe", lambda: PE.transpose(out=pt.ap[:, k - g0, :], in_=src3[:, k, :], identity=ident.ap),
                           [src, ident], [pt])
                    op("act", lambda: A.copy(out=sg_.ap[:, 0:g1 - g0, :], in_=pt.ap[:, 0:g1 - g0, :]), [pt], [sg_])
                    dma("sp", dst.rearrange("n p s -> p n s")[:, g0:g1, t * 128:(t + 1) * 128],
                        sg_.ap[:, 0:g1 - g0, :], reads=[sg_])

            def rmsnorm_to(src_ap, n, gb, dst, reads):
                rms_rstd(es, src_ap, n, junkf, st, reads)
                op("dve", lambda: V.scalar_tensor_tensor(out=dst.ap, in0=src_ap, scalar=st.ap[:, 0:1], in1=gb.ap,
                                                         op0=ALU.mult, op1=ALU.mult), reads + [st, gb], [dst])

            def epilogue(ci, z, t):
                col0, width, kind, meta = CHUNKS[ci]
                zs = z.ap[:, 0:width]
                r0 = t * 128
                if kind == "cq":
                    rmsnorm_to(zs, 384, qgb, cqn, [z])
                    pt = nxt("ptr", ptr)
                    for c in range(3):
                        op("pe", lambda: PE.transpose(out=pt.ap[:, c, :], in_=cqn.ap[:, c * 128:(c + 1) * 128],
                                                      identity=ident.ap), [cqn, ident], [pt])
                    op("act", lambda: A.copy(out=cqT.ap, in_=pt.ap[:, 0:3, :]), [pt], [cqT])
                    for c in range(3):
                        z2 = nxt("pz", pz)
                        for kc in range(3):
                            op("pe", lambda: PE.matmul(z2.ap[:, 0:384], lhsT=cqT.ap[:, kc, :],
                                                       rhs=wuq.ap[:, kc, c * 384:(c + 1) * 384],
                                                       start=(kc == 0), stop=(kc == 2)), [cqT, wuq], [z2])
                        v3 = z2.ap[:, 0:384].rearrange("p (h d) -> p h d", h=2)
                        op("act", lambda: A.mul(out=qn.ap[:, 2 * c:2 * c + 2, :], in_=v3[:, :, 0:128], mul=float(SA)),
                           [z2], [qn])
                        rope_evac(v3[:, :, 128:192], qr.ap[:, 2 * c:2 * c + 2, :], 2, 64, 64, "aq", t, SA, [z2], [qr])
                    fm_store(qn, qn.ap, 6, qaTn, t)
                    fm_store(qr, qr.ap.rearrange("p (a b) d -> p a (b d)", b=2), 3, qaTr, t)
                elif kind == "ckv":
                    LVL = int(os.environ.get("MK_LVL", 99))
                    rmsnorm_to(z.ap[:, 0:256], 256, kvgb, ckvn, [z])
                    if LVL < 2:
                        return
                    pt = nxt("ptr", ptr)
                    for c in range(2):
                        op("pe", lambda: PE.transpose(out=pt.ap[:, c, :], in_=ckvn.ap[:, c * 128:(c + 1) * 128],
                                                      identity=ident.ap), [ckvn, ident], [pt])
                    op("act", lambda: A.copy(out=ckvT.ap, in_=pt.ap[:, 0:2, :]), [pt], [ckvT])
                    if LVL < 3:
                        return
                    rope_evac(z.ap[:, 256:320].rearrange("p (h d) -> p h d", h=1), kr.ap[:, 0:1, :], 1, 64, 64, "ak", t,
                              1.0, [z], [kr])
                    if LVL < 4:
                        return
                    op("act", lambda: A.copy(out=kr.ap[:, 1, :], in_=kr.ap[:, 0, :]), [kr], [kr])
                    if LVL < 5:
                        return
                    for c in range(3):
                        z2 = nxt("pz", pz)
                        for kc in range(2):
                            op("pe", lambda: PE.matmul(z2.ap, lhsT=ckvT.ap[:, kc, :],
                                                       rhs=wukv.ap[:, kc, c * 512:(c + 1) * 512],
                                                       start=(kc == 0), stop=(kc == 1)), [ckvT, wukv], [z2])
                        v3 = z2.ap.rearrange("p (h d) -> p h d", h=2)
                        SUB = int(os.environ.get("MK_SUB", 3))
                        if SUB & 1:
                            op("act", lambda: A.copy(out=kn.ap[:, 2 * c:2 * c + 2, :], in_=v3[:, :, 0:128]), [z2], [kn])
                        if SUB & 2:
                            op("dve", lambda: V.tensor_copy(out=vx6.ap[:, 2 * c:2 * c + 2, 0:128], in_=v3[:, :, 128:256]),
                               [z2], [vx6])
                    if LVL < 6:
                        return
                    fm_store(kn, kn.ap, 6, kaTn, t)
                    if LVL < 7:
                        return
                    fm_store(kr, kr.ap.rearrange("p (a b) d -> p a (b d)", b=2), 1, kropeT, t)
                    if LVL < 8:
                        return
                    dma("sp", Va[r0:r0 + 128, :], vx6.ap.rearrange("p h d -> p (h d)"), reads=[vx6])
                elif kind == "gate":
                    s_ = nxt("sgt", sgt)
                    op("act", lambda: A.activation(out=s_.ap[:, 0:width], in_=zs, func=AF.Silu), [z], [s_])
                    dma("sp", SG[r0:r0 + 128, meta:meta + width], s_.ap[:, 0:width], reads=[s_])
                elif kind in ("qb", "kb"):
                    h0, nh = meta
                    v3 = zs.rearrange("p (h d) -> p h d", h=nh)
                    rope_evac(v3, hd5.ap[:, h0:h0 + nh, :], nh, 128, 32, "bq" if kind == "qb" else "bk", t,
                              SB if kind == "qb" else 1.0, [z], [hd5])
                    if h0 + nh == 5:
                        fm_store(hd5, hd5.ap, 5, qbT if kind == "qb" else kbT, t)
                elif kind in ("vb", "vc"):
                    h0, nh = meta
                    v3 = zs.rearrange("p (h d) -> p h d", h=nh)
                    op("dve", lambda: V.tensor_copy(out=vx5.ap[:, h0:h0 + nh, 0:128], in_=v3), [z], [vx5])
                    if h0 + nh == 5:
                        dst = Vb if kind == "vb" else Vc
                        dma("sp", dst[r0:r0 + 128, :], vx5.ap.rearrange("p h d -> p (h d)"), reads=[vx5])
                elif kind == "qi":
                    v3 = zs.rearrange("p (h d) -> p h d", h=8)
                    rope_evac(v3, qis.ap, 8, 64, 16, "i", t, 1.0, [z], [qis])
                    fm_store(qis, qis.ap.rearrange("p (a b) d -> p a (b d)", b=2), 4, qiT, t)
                elif kind == "kiw":
                    rope_evac(z.ap[:, 0:64].rearrange("p (h d) -> p h d", h=1), kis.ap[:, 0:1, :], 1, 64, 16, "i", t,
                              1.0, [z], [kis])
                    op("act", lambda: A.copy(out=kis.ap[:, 1, :], in_=kis.ap[:, 0, :]), [kis], [kis])
                    fm_store(kis, kis.ap.rearrange("p (a b) d -> p a (b d)", b=2), 1, kiT, t)
                    op("act", lambda: A.activation(out=wi.ap[:, 0:8], in_=z.ap[:, 64:72], func=AF.Abs), [z], [wi])
                    op("dve", lambda: V.tensor_scalar(out=wi.ap[:, 8:16], in0=z.ap[:, 64:72], scalar1=0.0, scalar2=2.0,
                                                      op0=ALU.is_ge, op1=ALU.mult), [z], [wi])
                    op("dve", lambda: V.tensor_scalar(out=wi.ap[:, 8:16], in0=wi.ap[:, 8:16], scalar1=-1.0,
                                                      scalar2=None, op0=ALU.add), [wi], [wi])
                    dma("sp", WI[r0:r0 + 128, :], wi.ap, reads=[wi])
                elif kind in ("qc", "kc"):
                    h0, nh = meta
                    v3 = zs.rearrange("p (h d) -> p h d", h=nh)
                    d3 = hd5.ap.rearrange("p a (b d) -> p (a b) d", b=2)
                    rope_evac(v3, d3[:, h0:h0 + nh, :], nh, 64, 16, "cq" if kind == "qc" else "i", t,
                              SC if kind == "qc" else 1.0, [z], [hd5])
                    if h0 + nh == 10:
                        fm_store(hd5, hd5.ap, 5, qcT if kind == "qc" else kcT, t)
                else:
                    raise AssertionError(kind)

            NSEG = int(os.environ.get("MK_NSEG", len(SEGS)))
            EPI = os.environ.get("MK_EPI", "1") == "1"
            EPK = os.environ.get("MK_EPK")
            EPK = set(EPK.split(",")) if EPK else None
            if NSEG > 0:
                load_w(0)
            for si in range(NSEG):
                if si + 1 < NSEG:
                    load_w(si + 1)
                wb = wbuf[si % 2]
                c0 = CHUNKS[SEGS[si][0]][0]
                dma("sp", uTs[0].ap, uT_d[0], writes=[uTs[0]])
                for t in range(NT):
                    u = uTs[t % 3]
                    if t + 1 < NT:
                        dma("sp", uTs[(t + 1) % 3].ap, uT_d[t + 1], writes=[uTs[(t + 1) % 3]])
                    zl = []
                    for ci in SEGS[si]:
                        col0, width, kind, meta = CHUNKS[ci]
                        z = nxt("pz", pz)
                        for kc in range(16):
                            op("pe", lambda: PE.matmul(z.ap[:, 0:width], lhsT=u.ap[:, kc, :],
                                                       rhs=wb.ap[:, kc, col0 - c0:col0 - c0 + width],
                                                       start=(kc == 0), stop=(kc == 15)), [u, wb], [z])
                        zl.append((ci, z))
                    if CHUNKS[SEGS[si][0]][2] == "qi":
                        zl = zl[::-1]
                    for (ci, z) in zl:
                        if EPI and (EPK is None or CHUNKS[ci][2] in EPK):
                            epilogue(ci, z, t)
            sch.barrier()
            maybe_stop("A2")

        with ExitStack() as es:
            GQ = 4
            kiTs = sb(es, "kiTs", [128, S], BF16)
            dvl = sb(es, "dvl", [128, 4, 512], F32)
            dng = sb(es, "dng", [128, 4, 512], F32)
            pw2 = sb(es, "pw2", [128, NIT], F32)
            Sc = [sb(es, "Sc", [128, S], F32) for _ in range(GQ)]
            rl = [sb(es, "rl", [128, 512], F32) for _ in range(4)]
            qit = [sb(es, "qit", [128, 4, 128], BF16) for _ in range(GQ)]
            wit = [sb(es, "wit", [128, 16], F32) for _ in range(GQ)]
            sel = [sb(es, "sel", [128, S], BF16) for _ in range(2)]
            junkb = [sb(es, "junkb", [128, S], BF16) for _ in range(GQ)]
            nms = [sb(es, "nms", [128, 4 * NQB, 128], BF16) for _ in range(2)]
            bs = [sb(es, "bs", [128, 8], F32) for _ in range(GQ)]
            halves = [sb(es, "halves", [128, NIT], F32) for _ in range(GQ)]
            psI = [ps(es, "psI", [128, 512], F32) for _ in range(4)]
            pst = [ps(es, "pst", [128, 8, 128], BF16) for _ in range(2)]
            nI = [0, 0, 0]
            dma("sp", kiTs.ap, kiT[0], writes=[kiTs])
            dma("sp", dvl.ap, dvalid_d.rearrange("j p q -> p j q"), writes=[dvl])
            dma("sp", dng.ap, dneg_d.rearrange("j p q -> p j q"), writes=[dng])
            dma("sp", pw2.ap, pow2_d, writes=[pw2])
            for qb in range(NQB):
                nkb = qb + 1
                Lk = 512 * nkb
                tiles = list(range(GQ))
                for qs in tiles:
                    it = 4 * qb + qs
                    dma("sp", qit[qs].ap, qiT.rearrange("n p s -> p n s")[:, :, it * 128:(it + 1) * 128], writes=[qit[qs]])
                    dma("sp", wit[qs].ap, WI[it * 128:(it + 1) * 128, :], writes=[wit[qs]])
                for kb in range(nkb):
                    for h in range(8):
                        for qs in tiles:
                            q_, w_, sc = qit[qs], wit[qs], Sc[qs]
                            pI = psI[nI[0] % 4]
                            nI[0] += 1
                            r_ = rl[nI[1] % 4]
                            nI[1] += 1
                            lo_p = 64 * (h % 2)
                            op("pe", lambda: PE.matmul(pI.ap, lhsT=q_.ap[lo_p:lo_p + 64, h // 2, :],
                                                       rhs=kiTs.ap[lo_p:lo_p + 64, kb * 512:(kb + 1) * 512],
                                                       start=True, stop=True), [q_, kiTs], [pI])
                            op("act", lambda: A.activation(out=r_.ap, in_=pI.ap, func=AF.Relu, scale=w_.ap[:, h:h + 1]),
                               [pI, w_], [r_])
                            scs = sc.ap[:, kb * 512:(kb + 1) * 512]
                            if h == 0:
                                op("dve", lambda: V.tensor_scalar(out=scs, in0=r_.ap, scalar1=w_.ap[:, 8:9], scalar2=None,
                                                                  op0=ALU.mult), [r_, w_], [sc])
                            else:
                                op("dve", lambda: V.scalar_tensor_tensor(out=scs, in0=r_.ap, scalar=w_.ap[:, 8 + h:9 + h],
                                                                         in1=scs, op0=ALU.mult, op1=ALU.add),
                                   [r_, w_, sc], [sc])
                bis = [qs for qs in tiles if 4 * qb + qs >= 2]

                def each(fn, lst=tiles):
                    for qs in lst:
                        fn(qs)

                def dgv(qs):
                    return Sc[qs].ap[:, qb * 512:(qb + 1) * 512]

                each(lambda qs: op("dve", lambda: V.tensor_tensor(out=dgv(qs), in0=dgv(qs), in1=dvl.ap[:, qs, :],
                                                                   op=ALU.mult), [Sc[qs], dvl], [Sc[qs]]))
                each(lambda qs: op("dve", lambda: V.tensor_reduce(out=bs[qs].ap[:, 0:1], in_=Sc[qs].ap[:, 0:Lk], axis=AX.X,
                                                                   op=ALU.max), [Sc[qs]], [bs[qs]]), bis)
                each(lambda qs: op("dve", lambda: V.tensor_reduce(out=bs[qs].ap[:, 1:2], in_=Sc[qs].ap[:, 0:Lk], axis=AX.X,
                                                                   op=ALU.min), [Sc[qs]], [bs[qs]]), bis)
                each(lambda qs: op("dve", lambda: V.tensor_tensor(out=dgv(qs), in0=dgv(qs), in1=dng.ap[:, qs, :],
                                                                   op=ALU.add), [Sc[qs], dng], [Sc[qs]]))
                each(lambda qs: op("dve", lambda: V.tensor_tensor(out=bs[qs].ap[:, 2:3], in0=bs[qs].ap[:, 0:1],
                                                                   in1=bs[qs].ap[:, 1:2], op=ALU.subtract),
                                   [bs[qs]], [bs[qs]]), bis)
                each(lambda qs: op("dve", lambda: V.tensor_scalar(out=halves[qs].ap, in0=pw2.ap, scalar1=bs[qs].ap[:, 2:3],
                                                                   scalar2=None, op0=ALU.mult), [pw2, bs[qs]], [halves[qs]]),
                     bis)
                for k in range(NIT):
                    each(lambda qs: op("dve", lambda: V.tensor_tensor(out=bs[qs].ap[:, 3:4], in0=bs[qs].ap[:, 1:2],
                                                                       in1=halves[qs].ap[:, k:k + 1], op=ALU.add),
                                       [bs[qs], halves[qs]], [bs[qs]]), bis)
                    each(lambda qs: op("dve", lambda: V.tensor_scalar(out=junkb[qs].ap[:, 0:Lk], in0=Sc[qs].ap[:, 0:Lk],
                                                                       scalar1=bs[qs].ap[:, 3:4], scalar2=None,
                                                                       op0=ALU.is_ge, op1=ALU.add,
                                                                       accum_out=bs[qs].ap[:, 4:5]),
                                       [Sc[qs], bs[qs]], [junkb[qs], bs[qs]]), bis)
                    each(lambda qs: op("dve", lambda: V.tensor_scalar(out=bs[qs].ap[:, 5:6], in0=bs[qs].ap[:, 4:5],
                                                                       scalar1=255.5, scalar2=halves[qs].ap[:, k:k + 1],
                                                                       op0=ALU.is_ge, op1=ALU.mult),
                                       [bs[qs], halves[qs]], [bs[qs]]), bis)
                    each(lambda qs: op("dve", lambda: V.tensor_tensor(out=bs[qs].ap[:, 1:2], in0=bs[qs].ap[:, 1:2],
                                                                       in1=bs[qs].ap[:, 5:6], op=ALU.add),
                                       [bs[qs]], [bs[qs]]), bis)
                for qs in tiles:
                    if qs not in bis:
                        op("dve", lambda: V.memset(bs[qs].ap[:, 1:2], -1e29), [], [bs[qs]])
                for qs in tiles:
                    sl_ = sel[qs % 2]
                    op("dve", lambda: V.tensor_scalar(out=sl_.ap[:, 0:Lk], in0=Sc[qs].ap[:, 0:Lk], scalar1=bs[qs].ap[:, 1:2],
                                                      scalar2=None, op0=ALU.is_ge), [Sc[qs], bs[qs]], [sl_])
                    nm_ = nms[qs % 2]
                    for g in range(nkb):
                        pt = pst[nI[2] % 2]
                        nI[2] += 1
                        for k in range(4):
                            kt = 4 * g + k
                            op("pe", lambda: PE.transpose(out=pt.ap[:, k, :], in_=sl_.ap[:, kt * 128:(kt + 1) * 128],
                                                          identity=ident.ap), [sl_, ident], [pt])
                        op("act", lambda: A.activation(out=nm_.ap[:, 4 * g:4 * g + 4, :], in_=pt.ap[:, 0:4, :],
                                                       func=AF.Identity, bias=float(NEGM), scale=float(-NEGM)), [pt], [nm_])
                    dma("sp", NM[qb][:, 0:4 * nkb, qs * 128:(qs + 1) * 128], nm_.ap[:, 0:4 * nkb, :], reads=[nm_])
            sch.barrier()
            maybe_stop("B1")

        def attention(mixer):
            with ExitStack() as es:
                if mixer == "a":
                    H, width, col0 = 6, 768, 0
                    Vd, kTd, nkt_extra = Va, kaTn, True
                elif mixer == "b":
                    H, width, col0 = 5, 640, 768
                    Vd, kTd, nkt_extra = Vb, kbT, False
                else:
                    H, width, col0 = 5, 640, 1408
                    Vd, kTd, nkt_extra = Vc, kcT, False
                kTs = sb(es, "kTs", [128, H, S], BF16)
                Vs = sb(es, "Vs", [128, NT, H * VW], BF16)
                for h in range(H):
                    dma("sp", kTs.ap[:, h, :], kTd[h], writes=[kTs])
                for t0 in range(0, NT, 8):
                    t1 = min(NT, t0 + 8)
                    dma("sp", Vs.ap[:, t0:t1, :], Vd[t0 * 128:t1 * 128, :].rearrange("(t p) c -> p t c", p=128),
                        writes=[Vs])
                if mixer == "a":
                    krs = sb(es, "krs", [128, S], BF16)
                    dma("sp", krs.ap, kropeT[0], writes=[krs])
                    qrb = [sb(es, "qrb", [128, 3, 512], BF16) for _ in range(2)]
                if mixer == "b":
                    nmT = [sb(es, "nmT", [128, 4 * NQB, 512], BF16) for _ in range(1)]
                if mixer == "c":
                    subg = sb(es, "subg", [128, 128], F32)
                    lamt = sb(es, "lamt", [128, 4, 64], F32)
                    lams = sb(es, "lams", [128, 8], F32)
                    dma("sp", subg.ap, bcast_rows(subln_g_d[L:L + 1, :], 128), writes=[subg])
                    for i, n in enumerate(("lam_q1", "lam_k1", "lam_q2", "lam_k2")):
                        dma("sp", lamt.ap[:, i, :], bcast_rows(lam_d[n][L:L + 1, :], 64), writes=[lamt])
                    op("dve", lambda: V.tensor_scalar(out=subg.ap, in0=subg.ap, scalar1=float(1.0 - lam_init),
                                                      scalar2=None, op0=ALU.mult), [subg], [subg])
                    for j in range(2):
                        op("dve", lambda: V.tensor_tensor(out=lamt.ap[:, 2 * j, :], in0=lamt.ap[:, 2 * j, :],
                                                          in1=lamt.ap[:, 2 * j + 1, :], op=ALU.mult), [lamt], [lamt])
                        op("dve", lambda: V.reduce_sum(out=lams.ap[:, j:j + 1], in_=lamt.ap[:, 2 * j, :], axis=AX.X),
                           [lamt], [lams])
                    op("act", lambda: A.activation(out=lams.ap[:, 2:4], in_=lams.ap[:, 0:2], func=AF.Exp), [lams], [lams])
                    op("dve", lambda: V.tensor_tensor(out=lams.ap[:, 4:5], in0=lams.ap[:, 3:4], in1=lams.ap[:, 2:3],
                                                      op=ALU.subtract), [lams], [lams])
                    op("dve", lambda: V.tensor_scalar(out=lams.ap[:, 5:6], in0=lams.ap[:, 4:5], scalar1=float(-lam_init),
                                                      scalar2=None, op0=ALU.add), [lams], [lams])
                    t1s = [sb(es, "t1s", [128, 4, 128], F32) for _ in range(1)]
                    osb = [sb(es, "osb", [128, 128], F32) for _ in range(2)]
                    junkc = sb(es, "junkc", [128, 128], F32)
                qnb = [sb(es, "qnb", [128, H, 512], BF16) for _ in range(2)]
                sgb = [sb(es, "sgb", [128, 4, width], F32) for _ in range(2)]
                mxo = [sb(es, "mxo", [128, 4, width], BF16) for _ in range(2)]
                PTs = [sb(es, "PT", [128, 512], BF16) for _ in range(4)]
                rd = [sb(es, "rd", [128, 4], F32) for _ in range(4)]
                pS = [ps(es, "pS", [128, 512], F32) for _ in range(3)]
                pO = [[ps(es, "pO", [128, 512], F32) for _ in range(2)] for _ in range(2)]
                n = {"S": 0, "P": 0, "O": 0, "rd": 0, "osb": 0}

                def units_of(h):
                    if mixer == "c":
                        return [(h, 1), (h, 2)]
                    return [(h, 0)]

                for qb in range(NQB):
                    qq = qnb[qb % 2]
                    qsl = slice(qb * 512, (qb + 1) * 512)
                    qsrc = {"a": qaTn, "b": qbT, "c": qcT}[mixer]
                    dma("sp", qq.ap, qsrc.rearrange("n p s -> p n s")[:, :, qsl], writes=[qq])
                    if mixer == "a":
                        qr_ = qrb[qb % 2]
                        dma("sp", qr_.ap, qaTr.rearrange("n p s -> p n s")[:, :, qsl], writes=[qr_])
                    if mixer == "b":
                        nm_ = nmT[0]
                        dma("sp", nm_.ap[:, 0:4 * (qb + 1), :], NM[qb][:, 0:4 * (qb + 1), :], writes=[nm_])
                    sg_ = sgb[qb % 2]
                    dma("sp", sg_.ap, SG[qsl, col0:col0 + width].rearrange("(a p) c -> p a c", p=128), writes=[sg_])
                    mo = mxo[qb % 2]
                    nkt = 4 * (qb + 1)
                    for h in range(H):
                        for (hh, u) in units_of(h):
                            po = pO[n["O"] % 2]
                            n["O"] += 1
                            for kt in range(nkt):
                                j = kt - 4 * qb
                                c_lo = 128 * j if j > 0 else 0
                                cs_ = slice(c_lo, 512)
                                ksl = slice(kt * 128, (kt + 1) * 128)
                                s_ = pS[n["S"] % 3]
                                n["S"] += 1
                                need_mask = (mixer == "b") or (j >= 0)
                                if mixer == "a":
                                    lo_p = 64 * (h % 2)
                                    op("pe", lambda: PE.matmul(s_.ap[:, cs_], lhsT=kTs.ap[:, h, ksl], rhs=qq.ap[:, h, cs_],
                                                               start=True, stop=False), [kTs, qq], [s_])
                                    op("pe", lambda: PE.matmul(s_.ap[:, cs_], lhsT=krs.ap[lo_p:lo_p + 64, ksl],
                                                               rhs=qr_.ap[lo_p:lo_p + 64, h // 2, cs_],
                                                               start=False, stop=not need_mask), [krs, qr_], [s_])
                                elif mixer == "b":
                                    op("pe", lambda: PE.matmul(s_.ap[:, cs_], lhsT=kTs.ap[:, h, ksl], rhs=qq.ap[:, h, cs_],
                                                               start=True, stop=False), [kTs, qq], [s_])
                                else:
                                    lo_p = 64 * (u - 1)
                                    op("pe", lambda: PE.matmul(s_.ap[:, cs_], lhsT=kTs.ap[lo_p:lo_p + 64, h, ksl],
                                                               rhs=qq.ap[lo_p:lo_p + 64, h, cs_],
                                                               start=True, stop=not need_mask), [kTs, qq], [s_])
                                if need_mask:
                                    if mixer == "b":
                                        op("pe", lambda: PE.matmul(s_.ap[:, cs_], lhsT=ident.ap, rhs=nm_.ap[:, kt, cs_],
                                                                   start=False, stop=True), [ident, nm_], [s_])
                                    else:
                                        op("pe", lambda: PE.matmul(s_.ap[:, cs_], lhsT=ident.ap, rhs=dmask.ap[:, j, cs_],
                                                                   start=False, stop=True), [ident, dmask], [s_])
                                P_ = PTs[n["P"] % 4]
                                n["P"] += 1
                                op("act", lambda: A.activation(out=P_.ap[:, cs_], in_=s_.ap[:, cs_], func=AF.Exp),
                                   [s_], [P_])
                                for qs in range(max(j, 0), 4):
                                    bank = po[qs // 2]
                                    oc = (qs % 2) * 256
                                    first = (kt == 0 and qs % 2 == 0)
                                    op("pe", lambda: PE.matmul(bank.ap[:, oc:oc + 129], lhsT=P_.ap[:, qs * 128:(qs + 1) * 128],
                                                               rhs=Vs.ap[:, kt, h * VW:h * VW + 129],
                                                               start=first, stop=(kt == 4 * qb + qs),
                                                               skip_group_check=True), [P_, Vs], [bank])
                            for qs in range(4):
                                bank = po[qs // 2]
                                oc = (qs % 2) * 256
                                r_ = rd[n["rd"] % 4]
                                n["rd"] += 1
                                op("dve", lambda: V.reciprocal(out=r_.ap[:, 0:1], in_=bank.ap[:, oc + 128:oc + 129]),
                                   [bank], [r_])
                                dst = mo.ap[:, qs, h * 128:(h + 1) * 128]
                                gsl = sg_.ap[:, qs, h * 128:(h + 1) * 128]
                                if mixer != "c":
                                    op("dve", lambda: V.scalar_tensor_tensor(out=dst, in0=bank.ap[:, oc:oc + 128],
                                                                             scalar=r_.ap[:, 0:1], in1=gsl,
                                                                             op0=ALU.mult, op1=ALU.mult),
                                       [bank, r_, sg_], [mo])
                                elif u == 1:
                                    t1_ = t1s[0]
                                    op("act", lambda: A.mul(out=t1_.ap[:, qs, :], in_=bank.ap[:, oc:oc + 128],
                                                            mul=r_.ap[:, 0:1]), [bank, r_], [t1_])
                                else:
                                    t1_ = t1s[0]
                                    o_ = osb[n["osb"] % 2]
                                    n["osb"] += 1
                                    op("dve", lambda: V.tensor_tensor(out=r_.ap[:, 1:2], in0=r_.ap[:, 0:1],
                                                                      in1=lams.ap[:, 5:6], op=ALU.mult), [r_, lams], [r_])
                                    op("dve", lambda: V.scalar_tensor_tensor(out=o_.ap, in0=bank.ap[:, oc:oc + 128],
                                                                             scalar=r_.ap[:, 1:2], in1=t1_.ap[:, qs, :],
                                                                             op0=ALU.mult, op1=ALU.add),
                                       [bank, r_, t1_], [o_])
                                    op("act", lambda: A.activation(out=junkc.ap, in_=o_.ap, func=AF.Square,
                                                                   accum_out=r_.ap[:, 2:3]), [o_], [junkc, r_])
                                    op("act", lambda: A.activation(out=r_.ap[:, 3:4], in_=r_.ap[:, 2:3], func=AF.Sqrt,
                                                                   bias=EPS, scale=1.0 / 128.0), [r_], [r_])
                                    op("dve", lambda: V.reciprocal(out=r_.ap[:, 2:3], in_=r_.ap[:, 3:4]), [r_], [r_])
                                    op("dve", lambda: V.scalar_tensor_tensor(out=o_.ap, in0=o_.ap, scalar=r_.ap[:, 2:3],
                                                                             in1=subg.ap, op0=ALU.mult, op1=ALU.mult),
                                       [o_, r_, subg], [o_])
                                    op("dve", lambda: V.tensor_tensor(out=dst, in0=o_.ap, in1=gsl, op=ALU.mult),
                                       [o_, sg_], [mo])
                    dma("sp", MX[qsl, col0:col0 + width].rearrange("(a p) c -> p a c", p=128), mo.ap, reads=[mo])
                sch.barrier()
                maybe_stop("B" + mixer)

        attention("a")
        attention("b")
        attention("c")

        with ExitStack() as es:
            wo = sb(es, "wo", [128, 16, D], BF16)
            for c in range(4):
                for kc0 in range(0, 16, 8):
                    dma("pool", wo.ap[:, kc0:kc0 + 8, c * 512:(c + 1) * 512],
                        w_o_d[L].rearrange("(c p) n -> p c n", p=128)[:, kc0:kc0 + 8, c * 512:(c + 1) * 512], writes=[wo])
            mxt = [sb(es, "mxt", [128, D], BF16) for _ in range(2)]
            mT = [sb(es, "mT", [128, 16, 128], BF16) for _ in range(2)]
            hin = [sb(es, "hin", [128, D], F32) for _ in range(2)]
            h1 = [sb(es, "h1", [128, D], F32) for _ in range(2)]
            h1b = [sb(es, "h1b", [128, D], BF16) for _ in range(2)]
            h1T = [sb(es, "h1T", [128, 16, 128], BF16) for _ in range(2)]
            pT = [ps(es, "pT", [128, 8, 128], BF16) for _ in range(4)]
            pz = [ps(es, "pz", [128, 512], F32) for _ in range(4)]
            npT = 0
            for t in range(NT):
                i = t % 2
                rs = slice(t * 128, (t + 1) * 128)
                dma("sp", mxt[i].ap, MX[rs, :], writes=[mxt[i]])
                dma("sp", hin[i].ap, h_src[rs, :], writes=[hin[i]])
                for g in range(2):
                    pt = pT[npT % 4]
                    npT += 1
                    for c in range(8):
                        cc = 8 * g + c
                        op("pe", lambda: PE.transpose(out=pt.ap[:, c, :], in_=mxt[i].ap[:, cc * 128:(cc + 1) * 128],
                                                      identity=ident.ap), [mxt[i], ident], [pt])
                    op("act", lambda: A.copy(out=mT[i].ap[:, 8 * g:8 * g + 8, :], in_=pt.ap), [pt], [mT[i]])
                for c in range(4):
                    z = pz[c]
                    for kc in range(16):
                        op("pe", lambda: PE.matmul(z.ap, lhsT=mT[i].ap[:, kc, :], rhs=wo.ap[:, kc, c * 512:(c + 1) * 512],
                                                   start=(kc == 0), stop=(kc == 15)), [mT[i], wo], [z])
                    op("dve", lambda: V.tensor_tensor(out=h1[i].ap[:, c * 512:(c + 1) * 512], in0=z.ap,
                                                      in1=hin[i].ap[:, c * 512:(c + 1) * 512], op=ALU.add),
                       [z, hin[i]], [h1[i]])
                dma("sp", hbuf[rs, :], h1[i].ap, reads=[h1[i]])
                op("dve", lambda: V.tensor_copy(out=h1b[i].ap, in_=h1[i].ap), [h1[i]], [h1b[i]])
                for g in range(2):
                    pt = pT[npT % 4]
                    npT += 1
                    for c in range(8):
                        cc = 8 * g + c
                        op("pe", lambda: PE.transpose(out=pt.ap[:, c, :], in_=h1b[i].ap[:, cc * 128:(cc + 1) * 128],
                                                      identity=ident.ap), [h1b[i], ident], [pt])
                    op("act", lambda: A.copy(out=h1T[i].ap[:, 8 * g:8 * g + 8, :], in_=pt.ap), [pt], [h1T[i]])
                dma("sp", uT_d[t], h1T[i].ap, reads=[h1T[i]])
            sch.barrier()
            maybe_stop("C1")

        with ExitStack() as es:
            wpg = sb(es, "wpg", [128, 16, D], BF16)
            wple = sb(es, "wple", [128, 2, D], BF16)
            for c in range(4):
                for kc0 in range(0, 16, 8):
                    dma("pool", wpg.ap[:, kc0:kc0 + 8, c * 512:(c + 1) * 512],
                        w_pg_d[L].rearrange("(c p) n -> p c n", p=128)[:, kc0:kc0 + 8, c * 512:(c + 1) * 512], writes=[wpg])
            dma("pool", wple.ap, w_ple_d[L].rearrange("(c p) n -> p c n", p=128), writes=[wple])
            if last:
                fgb = sb(es, "fgb", [128, D], F32)
                dma("sp", fgb.ap, bcast_rows(final_g_d[0:1, :], D), writes=[fgb])
                junk = sb(es, "junk", [128, D], BF16)
                st = [sb(es, "st", [128, 2], F32) for _ in range(2)]
            h1 = [sb(es, "h1", [128, D], F32) for _ in range(2)]
            h1T = [sb(es, "h1T", [128, 16, 128], BF16) for _ in range(2)]
            pt_ = [sb(es, "pt_", [128, PLE], F32) for _ in range(2)]
            pb = [sb(es, "pb", [128, PLE], BF16) for _ in range(2)]
            pTs = [sb(es, "pTs", [128, 2, 128], BF16) for _ in range(2)]
            sg = [sb(es, "sg", [128, 512], F32) for _ in range(2)]
            tm = [sb(es, "tm", [128, 512], F32) for _ in range(2)]
            h2 = [sb(es, "h2", [128, D], F32) for _ in range(2)]
            pz = [ps(es, "pz", [128, 512], F32) for _ in range(6)]
            pT = [ps(es, "pT", [128, 8, 128], BF16) for _ in range(2)]
            nz = 0
            ns = 0
            for t in range(NT):
                i = t % 2
                rs = slice(t * 128, (t + 1) * 128)
                dma("sp", h1[i].ap, hbuf[rs, :], writes=[h1[i]])
                dma("sp", h1T[i].ap, uT_d[t], writes=[h1T[i]])
                dma("sp", pt_[i].ap, p_d[L][rs, :], writes=[pt_[i]])
                op("dve", lambda: V.tensor_copy(out=pb[i].ap, in_=pt_[i].ap), [pt_[i]], [pb[i]])
                for c in range(2):
                    op("pe", lambda: PE.transpose(out=pT[i].ap[:, c, :], in_=pb[i].ap[:, c * 128:(c + 1) * 128],
                                                  identity=ident.ap), [pb[i], ident], [pT[i]])
                op("act", lambda: A.copy(out=pTs[i].ap, in_=pT[i].ap[:, 0:2, :]), [pT[i]], [pTs[i]])
                for c in range(4):
                    cs_ = slice(c * 512, (c + 1) * 512)
                    zg = pz[nz % 6]
                    nz += 1
                    for kc in range(16):
                        op("pe", lambda: PE.matmul(zg.ap, lhsT=h1T[i].ap[:, kc, :], rhs=wpg.ap[:, kc, cs_],
                                                   start=(kc == 0), stop=(kc == 15)), [h1T[i], wpg], [zg])
                    zp = pz[nz % 6]
                    nz += 1
                    for kc in range(2):
                        op("pe", lambda: PE.matmul(zp.ap, lhsT=pTs[i].ap[:, kc, :], rhs=wple.ap[:, kc, cs_],
                                                   start=(kc == 0), stop=(kc == 1)), [pTs[i], wple], [zp])
                    s_ = sg[ns % 2]
                    t_ = tm[ns % 2]
                    ns += 1
                    op("act", lambda: A.activation(out=s_.ap, in_=zg.ap, func=AF.Sigmoid), [zg], [s_])
                    op("dve", lambda: V.tensor_tensor(out=t_.ap, in0=zp.ap, in1=s_.ap, op=ALU.mult), [zp, s_], [t_])
                    op("dve", lambda: V.tensor_tensor(out=h2[i].ap[:, cs_], in0=t_.ap, in1=h1[i].ap[:, cs_], op=ALU.add),
                       [t_, h1[i]], [h2[i]])
                if not last:
                    dma("sp", hbuf[rs, :], h2[i].ap, reads=[h2[i]])
                else:
                    rms_rstd(es, h2[i].ap, D, junk, st[i], [h2[i]])
                    op("dve", lambda: V.scalar_tensor_tensor(out=h1[i].ap, in0=h2[i].ap, scalar=st[i].ap[:, 0:1],
                                                             in1=fgb.ap, op0=ALU.mult, op1=ALU.mult),
                       [h2[i], st[i], fgb], [h1[i]])
                    dma("sp", y_d[rs, :], h1[i].ap, reads=[h1[i]])
            sch.barrier()
            maybe_stop("C2")

    except _StopBuild:
        return nc, sch
    top.close()
    return nc, sch


_CACHE = {}


def kernel(x, p, positions, w_in, w_uq, w_ukv, w_o, norm_g, q_norm_g, kv_norm_g,
           lam_q1, lam_k1, lam_q2, lam_k2, subln_g, w_ple, w_pg, final_g):
    x = np.asarray(x)
    B, S, _ = x.shape
    DEPTH = int(np.asarray(w_in).shape[0])
    key = (S, DEPTH)
    if key not in _CACHE:
        _CACHE[key] = build_program(S, DEPTH)[0]
    nc = _CACHE[key]
    consts = host_consts()
    f32 = lambda a: np.ascontiguousarray(np.asarray(a), dtype=np.float32)
    shared = {
        "w_in": f32(w_in), "w_uq": f32(w_uq), "w_ukv": f32(w_ukv), "w_o": f32(w_o),
        "norm_g": f32(norm_g), "q_norm_g": f32(q_norm_g), "kv_norm_g": f32(kv_norm_g),
        "lam_q1": f32(lam_q1), "lam_k1": f32(lam_k1), "lam_q2": f32(lam_q2), "lam_k2": f32(lam_k2),
        "subln_g": f32(subln_g), "w_ple": f32(w_ple), "w_pg": f32(w_pg),
        "final_g": f32(final_g).reshape(1, D),
    }
    shared.update(consts)
    p = np.asarray(p)
    positions = np.asarray(positions)
    in_maps = []
    for c in range(8):
        b = c % B
        m = dict(shared)
        m["x"] = f32(x[b])
        m["p"] = f32(p[:, b])
        m["positions"] = np.ascontiguousarray(positions[b].astype(np.int32).reshape(S // 128, 128).T)
        in_maps.append(m)
    res = run_bass_kernel_spmd(nc, in_maps, core_ids=list(range(8)))
    out = np.stack([np.asarray(res.results[b]["y"], dtype=np.float32) for b in range(B)], axis=0)
    return out
```

```python
import math
import os
from contextlib import ExitStack

import numpy as np
import ml_dtypes
import concourse.bass as bass
import concourse.mybir as mybir
from concourse.bass_utils import run_bass_kernel_spmd

F32 = mybir.dt.float32
BF16 = mybir.dt.bfloat16
I32 = mybir.dt.int32
ALU = mybir.AluOpType
AF = mybir.ActivationFunctionType
AX = mybir.AxisListType

D = 2048
PLE = 256
D_IN = 7176
EPS = 1e-6
THETA = 500000.0
NIT = 22
NEGM = -30000.0
SA = 192.0 ** -0.5
SB = 128.0 ** -0.5
SC = 64.0 ** -0.5
VW = 132


class Buf:
    __slots__ = ("name", "writers", "readers", "ap", "psum")

    def __init__(self, name="", ap=None, psum=False):
        self.psum = psum
        self.name = name
        self.writers = {}
        self.readers = {}
        self.ap = ap


class Sched:
    NDMA = 8

    def __init__(self, nc):
        self.nc = nc
        self.eng = {"pe": nc.tensor, "act": nc.scalar, "dve": nc.vector,
                    "pool": nc.gpsimd, "sp": nc.sync}
        self.sems = {}
        self.cnt = {}
        for e in ("pe", "act", "dve", "pool"):
            self.sems[e] = nc.alloc_semaphore("s_" + e)
            self.cnt[e] = 0
        self.dq = {}
        for q in ("sp", "pool"):
            for i in range(self.NDMA):
                k = "d_%s%d" % (q, i)
                self.sems[k] = nc.alloc_semaphore(k)
                self.cnt[k] = 0
            self.dq[q] = 0
        self.seen = {e: {} for e in self.eng}
        self.nins = 0
        self.nwait = 0

    def _wait(self, e, evs):
        best = {}
        for (k, v) in evs:
            if v > best.get(k, 0):
                best[k] = v
        seen = self.seen[e]
        for k, v in best.items():
            if k == "pe" and e == "pe":
                continue
            if seen.get(k, 0) < v:
                self.eng[e].wait_ge(self.sems[k], v)
                seen[k] = v
                self.nwait += 1

    @staticmethod
    def _deps(reads, writes, e=None):
        evs = []
        for b in reads:
            evs.extend(b.writers.items())
            if b.psum:
                evs.extend((k, v) for k, v in b.readers.items() if k != e)
        for b in writes:
            evs.extend(b.writers.items())
            evs.extend(b.readers.items())
        return evs

    @staticmethod
    def _commit(ev, reads, writes):
        k, v = ev
        for b in reads:
            if b.readers.get(k, 0) < v:
                b.readers[k] = v
        for b in writes:
            b.writers = {k: v}
            b.readers = {}

    def op(self, e, ins_fn, reads=(), writes=()):
        self._wait(e, self._deps(reads, writes, e))
        ins = ins_fn()
        self.cnt[e] += 1
        ins.then_inc(self.sems[e], 1)
        self._commit((e, self.cnt[e]), reads, writes)
        self.nins += 1

    def dma(self, q, out, in_, reads=(), writes=()):
        i = self.dq[q]
        self.dq[q] = (i + 1) % self.NDMA
        k = "d_%s%d" % (q, i)
        evs = self._deps(reads, writes)
        if self.cnt[k] > 0:
            evs.append((k, self.cnt[k]))
        self._wait(q, evs)
        ins = self.eng[q].dma_start(out=out, in_=in_)
        self.cnt[k] += 16
        ins.then_inc(self.sems[k], 16)
        self._commit((k, self.cnt[k]), reads, writes)
        self.nins += 1

    def barrier(self):
        evs = [(k, v) for k, v in self.cnt.items() if v > 0]
        for e in self.eng:
            self._wait(e, list(evs))


def host_consts():
    c = {}
    c["ident"] = np.eye(128, dtype=np.float32).astype(ml_dtypes.bfloat16)
    invf = np.zeros((128, 56), np.float32)
    off = 0
    for n_rot in (64, 32, 16):
        half = n_rot // 2
        f = 1.0 / (np.float32(THETA) ** (np.arange(half, dtype=np.float32) * np.float32(2.0 / n_rot)))
        invf[:, off:off + half] = f.astype(np.float32)[None, :]
        off += half
    c["invf"] = invf
    dm = np.zeros((4, 128, 512), np.float32)
    for j in range(4):
        kk = 128 * j + np.arange(128)[:, None]
        qq = np.arange(512)[None, :]
        dm[j] = np.where((kk // 64) <= (qq // 64), 0.0, NEGM)
    c["dmask"] = dm.astype(ml_dtypes.bfloat16)
    dv = np.zeros((4, 128, 512), np.float32)
    for qs in range(4):
        qq = 128 * qs + np.arange(128)[:, None]
        kk = np.arange(512)[None, :]
        dv[qs] = ((kk // 64) <= (qq // 64)).astype(np.float32)
    c["dvalid"] = dv
    c["dneg"] = ((dv - 1.0) * 1e30).astype(np.float32)
    c["pow2"] = np.tile((0.5 ** np.arange(1, NIT + 1, dtype=np.float64)).astype(np.float32)[None, :], (128, 1))
    return c


CHUNKS = [
    (0, 384, "cq", None),
    (384, 320, "ckv", None),
    (704, 512, "gate", 0),
    (1216, 256, "gate", 512),
    (1472, 512, "qb", (0, 4)),
    (1984, 128, "qb", (4, 1)),
    (2112, 512, "kb", (0, 4)),
    (2624, 128, "kb", (4, 1)),
    (2752, 512, "vb", (0, 4)),
    (3264, 128, "vb", (4, 1)),
    (3392, 512, "qi", None),
    (3904, 72, "kiw", None),
    (3976, 512, "gate", 768),
    (4488, 128, "gate", 1280),
    (4616, 512, "qc", (0, 8)),
    (5128, 128, "qc", (8, 2)),
    (5256, 512, "kc", (0, 8)),
    (5768, 128, "kc", (8, 2)),
    (5896, 512, "vc", (0, 4)),
    (6408, 128, "vc", (4, 1)),
    (6536, 512, "gate", 1408),
    (7048, 128, "gate", 1920),
]
SEGS = [(2 * i, 2 * i + 1) for i in range(11)]


class _StopBuild(Exception):
    pass


def build_program(S, DEPTH, debug_layers=None):
    assert S % 512 == 0
    STOP = os.environ.get("MK_STOP", "")

    def maybe_stop(tag):
        if STOP == tag:
            raise _StopBuild()
    NT = S // 128
    NQB = S // 512
    nc = bass.Bass("TRN2", target_bir_lowering=False)
    sch = Sched(nc)
    uid = [0]

    def din(name, shape, dt):
        return nc.dram_tensor(name, list(shape), dt, kind="ExternalInput").ap()

    def dscr(name, shape, dt):
        return nc.dram_tensor(name, list(shape), dt).ap()

    x_d = din("x", [S, D], F32)
    p_d = din("p", [DEPTH, S, PLE], F32)
    pos_d = din("positions", [128, S // 128], I32)
    w_in_d = din("w_in", [DEPTH, D, D_IN], F32)
    w_uq_d = din("w_uq", [DEPTH, 384, 1152], F32)
    w_ukv_d = din("w_ukv", [DEPTH, 256, 1536], F32)
    w_o_d = din("w_o", [DEPTH, D, D], F32)
    norm_g_d = din("norm_g", [DEPTH, D], F32)
    q_norm_g_d = din("q_norm_g", [DEPTH, 384], F32)
    kv_norm_g_d = din("kv_norm_g", [DEPTH, 256], F32)
    lam_d = {n: din(n, [DEPTH, 64], F32) for n in ("lam_q1", "lam_k1", "lam_q2", "lam_k2")}
    subln_g_d = din("subln_g", [DEPTH, 128], F32)
    w_ple_d = din("w_ple", [DEPTH, PLE, D], F32)
    w_pg_d = din("w_pg", [DEPTH, D, D], F32)
    final_g_d = din("final_g", [1, D], F32)
    ident_d = din("ident", [128, 128], BF16)
    invf_d = din("invf", [128, 56], F32)
    dmask_d = din("dmask", [4, 128, 512], BF16)
    dvalid_d = din("dvalid", [4, 128, 512], F32)
    dneg_d = din("dneg", [4, 128, 512], F32)
    pow2_d = din("pow2", [128, NIT], F32)
    y_d = nc.dram_tensor("y", [S, D], F32, kind="ExternalOutput").ap()

    hbuf = dscr("hbuf", [S, D], F32)
    uT_d = dscr("uT_d", [NT, 128, 16, 128], BF16)
    rope_d = dscr("rope_d", [128, NT, 224], F32)
    qaTn = dscr("qaTn", [6, 128, S], BF16)
    qaTr = dscr("qaTr", [3, 128, S], BF16)
    kaTn = dscr("kaTn", [6, 128, S], BF16)
    kropeT = dscr("kropeT", [1, 128, S], BF16)
    Va = dscr("Va", [S, 6 * VW], BF16)
    qbT = dscr("qbT", [5, 128, S], BF16)
    kbT = dscr("kbT", [5, 128, S], BF16)
    Vb = dscr("Vb", [S, 5 * VW], BF16)
    qiT = dscr("qiT", [4, 128, S], BF16)
    kiT = dscr("kiT", [1, 128, S], BF16)
    WI = dscr("WI", [S, 16], F32)
    qcT = dscr("qcT", [5, 128, S], BF16)
    kcT = dscr("kcT", [5, 128, S], BF16)
    Vc = dscr("Vc", [S, 5 * VW], BF16)
    SG = dscr("SG", [S, D], F32)
    MX = dscr("MX", [S, D], BF16)
    NM = dscr("NM", [NQB, 128, 4 * NQB, 512], BF16)

    def sb(es, name, shape, dt):
        uid[0] += 1
        h = es.enter_context(nc.sbuf_tensor("%s_%d" % (name, uid[0]), list(shape), dt))
        return Buf(name, h.ap())

    def ps(es, name, shape, dt):
        uid[0] += 1
        h = es.enter_context(nc.psum_tensor("%s_%d" % (name, uid[0]), list(shape), dt))
        return Buf(name, h.ap(), psum=True)

    V = nc.vector
    G = nc.gpsimd
    A = nc.scalar
    PE = nc.tensor
    op = sch.op
    dma = sch.dma

    top = ExitStack()
    ident = sb(top, "ident", [128, 128], BF16)
    dmask = sb(top, "dmask", [128, 4, 512], BF16)
    dma("sp", ident.ap, ident_d, writes=[ident])
    dma("sp", dmask.ap, dmask_d.rearrange("j p q -> p j q"), writes=[dmask])

    def bcast_rows(src_row_ap, n):
        return src_row_ap.to_broadcast([128, n])

    ROFF = {"aq": (0, 32), "ak": (64, 32), "bq": (128, 16), "bk": (160, 16), "i": (192, 8), "cq": (208, 8)}
    with ExitStack() as es:
        posi = sb(es, "posi", [128, NT], I32)
        posf = sb(es, "posf", [128, NT], F32)
        invf = sb(es, "invf", [128, 56], F32)
        ang = sb(es, "ang", [128, NT, 56], F32)
        r1 = sb(es, "r1", [128, NT, 56], F32)
        cs = sb(es, "cs", [128, NT, 56], F32)
        sn = sb(es, "sn", [128, NT, 56], F32)
        tab = sb(es, "tab", [128, NT, 224], F32)
        dma("sp", posi.ap, pos_d, writes=[posi])
        dma("sp", invf.ap, invf_d, writes=[invf])
        op("dve", lambda: V.tensor_copy(out=posf.ap, in_=posi.ap), [posi], [posf])
        for t in range(NT):
            op("dve", lambda: V.tensor_scalar(out=ang.ap[:, t, :], in0=invf.ap, scalar1=posf.ap[:, t:t + 1],
                                              scalar2=None, op0=ALU.mult), [invf, posf], [ang])
        twopi = 2.0 * math.pi
        ki = sb(es, "ki", [128, NT, 56], I32)
        kf = sb(es, "kf", [128, NT, 56], F32)

        def sin_of(shift, dst):
            op("dve", lambda: V.tensor_scalar(out=r1.ap, in0=ang.ap, scalar1=float(shift), scalar2=None, op0=ALU.add),
               [ang], [r1])
            op("dve", lambda: V.tensor_scalar(out=kf.ap, in0=r1.ap, scalar1=1.0 / twopi, scalar2=None, op0=ALU.mult),
               [r1], [kf])
            op("dve", lambda: V.tensor_copy(out=ki.ap, in_=kf.ap), [kf], [ki])
            op("dve", lambda: V.tensor_copy(out=kf.ap, in_=ki.ap), [ki], [kf])
            op("dve", lambda: V.scalar_tensor_tensor(out=r1.ap, in0=kf.ap, scalar=-twopi, in1=r1.ap,
                                                     op0=ALU.mult, op1=ALU.add), [kf, r1], [r1])
            op("dve", lambda: V.tensor_scalar(out=kf.ap, in0=r1.ap, scalar1=math.pi, scalar2=-twopi,
                                              op0=ALU.is_gt, op1=ALU.mult), [r1], [kf])
            op("dve", lambda: V.tensor_tensor(out=r1.ap, in0=r1.ap, in1=kf.ap, op=ALU.add), [r1, kf], [r1])
            op("dve", lambda: V.tensor_scalar(out=kf.ap, in0=r1.ap, scalar1=-math.pi, scalar2=twopi,
                                              op0=ALU.is_lt, op1=ALU.mult), [r1], [kf])
            op("dve", lambda: V.tensor_tensor(out=r1.ap, in0=r1.ap, in1=kf.ap, op=ALU.add), [r1, kf], [r1])
            op("act", lambda: A.activation(out=dst.ap, in_=r1.ap, func=AF.Sin), [r1], [dst])

        sin_of(0.0, sn)
        sin_of(0.5 * math.pi, cs)
        specs = [("aq", 0, SA), ("ak", 0, 1.0), ("bq", 32, SB), ("bk", 32, 1.0), ("i", 48, 1.0), ("cq", 48, SC)]
        for (nm, so, scl) in specs:
            o, hf = ROFF[nm]
            op("dve", lambda: V.tensor_scalar(out=tab.ap[:, :, o:o + hf], in0=cs.ap[:, :, so:so + hf],
                                              scalar1=float(scl), scalar2=None, op0=ALU.mult), [cs], [tab])
            op("dve", lambda: V.tensor_scalar(out=tab.ap[:, :, o + hf:o + 2 * hf], in0=sn.ap[:, :, so:so + hf],
                                              scalar1=float(scl), scalar2=None, op0=ALU.mult), [sn], [tab])
        dma("sp", rope_d, tab.ap, reads=[tab])
        sch.barrier()

    def rms_rstd(es_tmp, src_ap, n, junk, st, reads):
        op("act", lambda: A.activation(out=junk.ap[:, 0:n], in_=src_ap, func=AF.Square, accum_out=st.ap[:, 0:1]),
           reads, [junk, st])
        op("act", lambda: A.activation(out=st.ap[:, 1:2], in_=st.ap[:, 0:1], func=AF.Sqrt, bias=EPS, scale=1.0 / n),
           [st], [st])
        op("dve", lambda: V.reciprocal(out=st.ap[:, 0:1], in_=st.ap[:, 1:2]), [st], [st])

    maybe_stop_holder = [None]
    try:
      maybe_stop("P")
      for L in range(DEPTH):
        h_src = x_d if L == 0 else hbuf
        last = (L == DEPTH - 1)
        lam_init = 0.8 - 0.6 * math.exp(-0.3 * L)

        with ExitStack() as es:
            gbc = sb(es, "gbc", [128, D], F32)
            dma("sp", gbc.ap, bcast_rows(norm_g_d[L:L + 1, :], D), writes=[gbc])
            hin = [sb(es, "hin", [128, D], F32) for _ in range(2)]
            ub = [sb(es, "ub", [128, D], BF16) for _ in range(2)]
            uTs = [sb(es, "uTs", [128, 16, 128], BF16) for _ in range(2)]
            junk = sb(es, "junk", [128, D], BF16)
            st = [sb(es, "st", [128, 2], F32) for _ in range(2)]
            pT = [ps(es, "pT", [128, 8, 128], BF16) for _ in range(4)]
            npT = 0
            dma("sp", hin[0].ap, h_src[0:128, :], writes=[hin[0]])
            for t in range(NT):
                i = t % 2
                if t + 1 < NT:
                    dma("sp", hin[1 - i].ap, h_src[(t + 1) * 128:(t + 2) * 128, :], writes=[hin[1 - i]])
                rms_rstd(es, hin[i].ap, D, junk, st[i], [hin[i]])
                op("dve", lambda: V.scalar_tensor_tensor(out=ub[i].ap, in0=hin[i].ap, scalar=st[i].ap[:, 0:1],
                                                         in1=gbc.ap, op0=ALU.mult, op1=ALU.mult),
                   [hin[i], st[i], gbc], [ub[i]])
                for g in range(2):
                    pt = pT[npT % 4]
                    npT += 1
                    for c in range(8):
                        cc = 8 * g + c
                        op("pe", lambda: PE.transpose(out=pt.ap[:, c, :], in_=ub[i].ap[:, cc * 128:(cc + 1) * 128],
                                                      identity=ident.ap), [ub[i], ident], [pt])
                    op("act", lambda: A.copy(out=uTs[i].ap[:, 8 * g:8 * g + 8, :], in_=pt.ap), [pt], [uTs[i]])
                dma("sp", uT_d[t], uTs[i].ap, reads=[uTs[i]])
            sch.barrier()
            maybe_stop("A1")

        with ExitStack() as es:
            WMAX = 768
            wbuf = [sb(es, "wbuf", [128, 16, WMAX], BF16) for _ in range(2)]
            wuq = sb(es, "wuq", [128, 3, 1152], BF16)
            wukv = sb(es, "wukv", [128, 2, 1536], BF16)
            qgb = sb(es, "qgb", [128, 384], F32)
            kvgb = sb(es, "kvgb", [128, 256], F32)
            tab = sb(es, "tab", [128, NT, 224], F32)
            uTs = [sb(es, "uTs", [128, 16, 128], BF16) for _ in range(3)]
            pz = [ps(es, "pz", [128, 512], F32) for _ in range(4)]
            ptr = [ps(es, "ptr", [128, 8, 128], BF16) for _ in range(2)]
            junkf = sb(es, "junkf", [128, 512], F32)
            st = sb(es, "st", [128, 2], F32)
            cqn = sb(es, "cqn", [128, 384], BF16)
            cqT = sb(es, "cqT", [128, 3, 128], BF16)
            ckvn = sb(es, "ckvn", [128, 256], BF16)
            ckvT = sb(es, "ckvT", [128, 2, 128], BF16)
            qn = sb(es, "qn", [128, 6, 128], BF16)
            qr = sb(es, "qr", [128, 6, 64], BF16)
            kn = sb(es, "kn", [128, 6, 128], BF16)
            kr = sb(es, "kr", [128, 2, 64], BF16)
            vx6 = sb(es, "vx6", [128, 6, VW], BF16)
            vx5 = sb(es, "vx5", [128, 5, VW], BF16)
            hd5 = sb(es, "hd5", [128, 5, 128], BF16)
            qis = sb(es, "qis", [128, 8, 64], BF16)
            kis = sb(es, "kis", [128, 2, 64], BF16)
            wi = sb(es, "wi", [128, 16], F32)
            sgt = [sb(es, "sgt", [128, 512], F32) for _ in range(2)]
            rt = [[sb(es, "rt", [128, 128], F32) for _ in range(4)] for _ in range(2)]
            stage = [sb(es, "stage", [128, 8, 128], BF16) for _ in range(2)]
            cnt = {"pz": 0, "ptr": 0, "rt": 0, "stage": 0, "sgt": 0}

            dma("sp", tab.ap, rope_d, writes=[tab])
            dma("sp", qgb.ap, bcast_rows(q_norm_g_d[L:L + 1, :], 384), writes=[qgb])
            dma("sp", kvgb.ap, bcast_rows(kv_norm_g_d[L:L + 1, :], 256), writes=[kvgb])
            dma("pool", wuq.ap, w_uq_d[L].rearrange("(c p) n -> p c n", p=128), writes=[wuq])
            dma("pool", wukv.ap, w_ukv_d[L].rearrange("(c p) n -> p c n", p=128), writes=[wukv])
            op("dve", lambda: V.memset(vx6.ap, 1.0), [], [vx6])
            op("dve", lambda: V.memset(vx5.ap, 1.0), [], [vx5])

            def load_w(si):
                c0 = CHUNKS[SEGS[si][0]][0]
                c1 = CHUNKS[SEGS[si][1]][0] + CHUNKS[SEGS[si][1]][1]
                wb = wbuf[si % 2]
                src = w_in_d[L].rearrange("(c p) n -> p c n", p=128)
                for kc0 in range(0, 16, 4):
                    dma("pool", wb.ap[:, kc0:kc0 + 4, 0:c1 - c0], src[:, kc0:kc0 + 4, c0:c1], writes=[wb])

            def nxt(key, lst):
                cnt[key] += 1
                return lst[cnt[key] % len(lst)]

            def rope_evac(src3, dst3, H, Dh, n_rot, tname, t, scale, reads, writes):
                half = n_rot // 2
                o, hf = ROFF[tname]
                assert hf == half
                cb = tab.ap[:, t, o:o + half].unsqueeze(1).to_broadcast([128, H, half])
                sbc = tab.ap[:, t, o + half:o + 2 * half].unsqueeze(1).to_broadcast([128, H, half])
                r = nxt("rt", rt)
                n = H * half
                v = [r[k].ap[:, 0:n].rearrange("p (h d) -> p h d", h=H) for k in range(4)]
                x1 = src3[:, :, 0:half]
                x2 = src3[:, :, half:n_rot]
                if Dh > n_rot:
                    op("act", lambda: A.mul(out=dst3[:, :, n_rot:Dh], in_=src3[:, :, n_rot:Dh], mul=float(scale)),
                       reads, writes)
                op("dve", lambda: V.tensor_tensor(out=v[0], in0=x1, in1=cb, op=ALU.mult), reads + [tab], [r[0]])
                op("dve", lambda: V.tensor_tensor(out=v[1], in0=x2, in1=sbc, op=ALU.mult), reads + [tab], [r[1]])
                op("dve", lambda: V.tensor_tensor(out=v[2], in0=x2, in1=cb, op=ALU.mult), reads + [tab], [r[2]])
                op("dve", lambda: V.tensor_tensor(out=v[3], in0=x1, in1=sbc, op=ALU.mult), reads + [tab], [r[3]])
                op("dve", lambda: V.tensor_tensor(out=dst3[:, :, 0:half], in0=v[0], in1=v[1], op=ALU.subtract),
                   [r[0], r[1]], writes)
                op("dve", lambda: V.tensor_tensor(out=dst3[:, :, half:n_rot], in0=v[2], in1=v[3], op=ALU.add),
                   [r[2], r[3]], writes)

            def fm_store(src, src3, n, dst, t):
                for g0 in range(0, n, 8):
                    g1 = min(n, g0 + 8)
                    pt = nxt("ptr", ptr)
                    sg_ = nxt("stage", stage)
                    for k in range(g0, g1):
                        op("pe", lambda: PE.transpose(out=pt.ap[:, k - g0, :], in_=src3[:, k, :], identity=ident.ap),
                           [src, ident], [pt])
                    op("act", lambda: A.copy(out=sg_.ap[:, 0:g1 - g0, :], in_=pt.ap[:, 0:g1 - g0, :]), [pt], [sg_])
                    dma("sp", dst.rearrange("n p s -> p n s")[:, g0:g1, t * 128:(t + 1) * 128],
                        sg_.ap[:, 0:g1 - g0, :], reads=[sg_])

            def rmsnorm_to(src_ap, n, gb, dst, reads):
                rms_rstd(es, src_ap, n, junkf, st, reads)
                op("dve", lambda: V.scalar_tensor_tensor(out=dst.ap, in0=src_ap, scalar=st.ap[:, 0:1], in1=gb.ap,
                                                         op0=ALU.mult, op1=ALU.mult), reads + [st, gb], [dst])

            def epilogue(ci, z, t):
                col0, width, kind, meta = CHUNKS[ci]
                zs = z.ap[:, 0:width]
                r0 = t * 128
                if kind == "cq":
                    rmsnorm_to(zs, 384, qgb, cqn, [z])
                    pt = nxt("ptr", ptr)
                    for c in range(3):
                        op("pe", lambda: PE.transpose(out=pt.ap[:, c, :], in_=cqn.ap[:, c * 128:(c + 1) * 128],
                                                      identity=ident.ap), [cqn, ident], [pt])
                    op("act", lambda: A.copy(out=cqT.ap, in_=pt.ap[:, 0:3, :]), [pt], [cqT])
                    for c in range(3):
                        z2 = nxt("pz", pz)
                        for kc in range(3):
                            op("pe", lambda: PE.matmul(z2.ap[:, 0:384], lhsT=cqT.ap[:, kc, :],
                                                       rhs=wuq.ap[:, kc, c * 384:(c + 1) * 384],
                                                       start=(kc == 0), stop=(kc == 2)), [cqT, wuq], [z2])
                        v3 = z2.ap[:, 0:384].rearrange("p (h d) -> p h d", h=2)
                        op("act", lambda: A.mul(out=qn.ap[:, 2 * c:2 * c + 2, :], in_=v3[:, :, 0:128], mul=float(SA)),
                           [z2], [qn])
                        rope_evac(v3[:, :, 128:192], qr.ap[:, 2 * c:2 * c + 2, :], 2, 64, 64, "aq", t, SA, [z2], [qr])
                    fm_store(qn, qn.ap, 6, qaTn, t)
                    fm_store(qr, qr.ap.rearrange("p (a b) d -> p a (b d)", b=2), 3, qaTr, t)
                elif kind == "ckv":
                    LVL = int(os.environ.get("MK_LVL", 99))
                    rmsnorm_to(z.ap[:, 0:256], 256, kvgb, ckvn, [z])
                    if LVL < 2:
                        return
                    pt = nxt("ptr", ptr)
                    for c in range(2):
                        op("pe", lambda: PE.transpose(out=pt.ap[:, c, :], in_=ckvn.ap[:, c * 128:(c + 1) * 128],
                                                      identity=ident.ap), [ckvn, ident], [pt])
                    op("act", lambda: A.copy(out=ckvT.ap, in_=pt.ap[:, 0:2, :]), [pt], [ckvT])
                    if LVL < 3:
                        return
                    rope_evac(z.ap[:, 256:320].rearrange("p (h d) -> p h d", h=1), kr.ap[:, 0:1, :], 1, 64, 64, "ak", t,
                              1.0, [z], [kr])
                    if LVL < 4:
                        return
                    op("act", lambda: A.copy(out=kr.ap[:, 1, :], in_=kr.ap[:, 0, :]), [kr], [kr])
                    if LVL < 5:
                        return
                    for c in range(3):
                        z2 = nxt("pz", pz)
                        for kc in range(2):
                            op("pe", lambda: PE.matmul(z2.ap, lhsT=ckvT.ap[:, kc, :],
                                                       rhs=wukv.ap[:, kc, c * 512:(c + 1) * 512],
                                                       start=(kc == 0), stop=(kc == 1)), [ckvT, wukv], [z2])
                        v3 = z2.ap.rearrange("p (h d) -> p h d", h=2)
                        SUB = int(os.environ.get("MK_SUB", 3))
                        if SUB & 1:
                            op("act", lambda: A.copy(out=kn.ap[:, 2 * c:2 * c + 2, :], in_=v3[:, :, 0:128]), [z2], [kn])
                        if SUB & 2:
                            op("dve", lambda: V.tensor_copy(out=vx6.ap[:, 2 * c:2 * c + 2, 0:128], in_=v3[:, :, 128:256]),
                               [z2], [vx6])
                    if LVL < 6:
                        return
                    fm_store(kn, kn.ap, 6, kaTn, t)
                    if LVL < 7:
                        return
                    fm_store(kr, kr.ap.rearrange("p (a b) d -> p a (b d)", b=2), 1, kropeT, t)
                    if LVL < 8:
                        return
                    dma("sp", Va[r0:r0 + 128, :], vx6.ap.rearrange("p h d -> p (h d)"), reads=[vx6])
                elif kind == "gate":
                    s_ = nxt("sgt", sgt)
                    op("act", lambda: A.activation(out=s_.ap[:, 0:width], in_=zs, func=AF.Silu), [z], [s_])
                    dma("sp", SG[r0:r0 + 128, meta:meta + width], s_.ap[:, 0:width], reads=[s_])
                elif kind in ("qb", "kb"):
                    h0, nh = meta
                    v3 = zs.rearrange("p (h d) -> p h d", h=nh)
                    rope_evac(v3, hd5.ap[:, h0:h0 + nh, :], nh, 128, 32, "bq" if kind == "qb" else "bk", t,
                              SB if kind == "qb" else 1.0, [z], [hd5])
                    if h0 + nh == 5:
                        fm_store(hd5, hd5.ap, 5, qbT if kind == "qb" else kbT, t)
                elif kind in ("vb", "vc"):
                    h0, nh = meta
                    v3 = zs.rearrange("p (h d) -> p h d", h=nh)
                    op("dve", lambda: V.tensor_copy(out=vx5.ap[:, h0:h0 + nh, 0:128], in_=v3), [z], [vx5])
                    if h0 + nh == 5:
                        dst = Vb if kind == "vb" else Vc
                        dma("sp", dst[r0:r0 + 128, :], vx5.ap.rearrange("p h d -> p (h d)"), reads=[vx5])
                elif kind == "qi":
                    v3 = zs.rearrange("p (h d) -> p h d", h=8)
                    rope_evac(v3, qis.ap, 8, 64, 16, "i", t, 1.0, [z], [qis])
                    fm_store(qis, qis.ap.rearrange("p (a b) d -> p a (b d)", b=2), 4, qiT, t)
                elif kind == "kiw":
                    rope_evac(z.ap[:, 0:64].rearrange("p (h d) -> p h d", h=1), kis.ap[:, 0:1, :], 1, 64, 16, "i", t,
                              1.0, [z], [kis])
                    op("act", lambda: A.copy(out=kis.ap[:, 1, :], in_=kis.ap[:, 0, :]), [kis], [kis])
                    fm_store(kis, kis.ap.rearrange("p (a b) d -> p a (b d)", b=2), 1, kiT, t)
                    op("act", lambda: A.activation(out=wi.ap[:, 0:8], in_=z.ap[:, 64:72], func=AF.Abs), [z], [wi])
                    op("dve", lambda: V.tensor_scalar(out=wi.ap[:, 8:16], in0=z.ap[:, 64:72], scalar1=0.0, scalar2=2.0,
                                                      op0=ALU.is_ge, op1=ALU.mult), [z], [wi])
                    op("dve", lambda: V.tensor_scalar(out=wi.ap[:, 8:16], in0=wi.ap[:, 8:16], scalar1=-1.0,
                                                      scalar2=None, op0=ALU.add), [wi], [wi])
                    dma("sp", WI[r0:r0 + 128, :], wi.ap, reads=[wi])
                elif kind in ("qc", "kc"):
                    h0, nh = meta
                    v3 = zs.rearrange("p (h d) -> p h d", h=nh)
                    d3 = hd5.ap.rearrange("p a (b d) -> p (a b) d", b=2)
                    rope_evac(v3, d3[:, h0:h0 + nh, :], nh, 64, 16, "cq" if kind == "qc" else "i", t,
                              SC if kind == "qc" else 1.0, [z], [hd5])
                    if h0 + nh == 10:
                        fm_store(hd5, hd5.ap, 5, qcT if kind == "qc" else kcT, t)
                else:
                    raise AssertionError(kind)

            NSEG = int(os.environ.get("MK_NSEG", len(SEGS)))
            EPI = os.environ.get("MK_EPI", "1") == "1"
            EPK = os.environ.get("MK_EPK")
            EPK = set(EPK.split(",")) if EPK else None
            if NSEG > 0:
                load_w(0)
            for si in range(NSEG):
                if si + 1 < NSEG:
                    load_w(si + 1)
                wb = wbuf[si % 2]
                c0 = CHUNKS[SEGS[si][0]][0]
                dma("sp", uTs[0].ap, uT_d[0], writes=[uTs[0]])
                for t in range(NT):
                    u = uTs[t % 3]
                    if t + 1 < NT:
                        dma("sp", uTs[(t + 1) % 3].ap, uT_d[t + 1], writes=[uTs[(t + 1) % 3]])
                    zl = []
                    for ci in SEGS[si]:
                        col0, width, kind, meta = CHUNKS[ci]
                        z = nxt("pz", pz)
                        for kc in range(16):
                            op("pe", lambda: PE.matmul(z.ap[:, 0:width], lhsT=u.ap[:, kc, :],
                                                       rhs=wb.ap[:, kc, col0 - c0:col0 - c0 + width],
                                                       start=(kc == 0), stop=(kc == 15)), [u, wb], [z])
                        zl.append((ci, z))
                    if CHUNKS[SEGS[si][0]][2] == "qi":
                        zl = zl[::-1]
                    for (ci, z) in zl:
                        if EPI and (EPK is None or CHUNKS[ci][2] in EPK):
                            epilogue(ci, z, t)
            sch.barrier()
            maybe_stop("A2")

        with ExitStack() as es:
            GQ = 4
            kiTs = sb(es, "kiTs", [128, S], BF16)
            dvl = sb(es, "dvl", [128, 4, 512], F32)
            dng = sb(es, "dng", [128, 4, 512], F32)
            pw2 = sb(es, "pw2", [128, NIT], F32)
            Sc = [sb(es, "Sc", [128, S], F32) for _ in range(GQ)]
            rl = [sb(es, "rl", [128, 512], F32) for _ in range(4)]
            qit = [sb(es, "qit", [128, 4, 128], BF16) for _ in range(GQ)]
            wit = [sb(es, "wit", [128, 16], F32) for _ in range(GQ)]
            sel = [sb(es, "sel", [128, S], BF16) for _ in range(2)]
            junkb = [sb(es, "junkb", [128, S], BF16) for _ in range(GQ)]
            nms = [sb(es, "nms", [128, 4 * NQB, 128], BF16) for _ in range(2)]
            bs = [sb(es, "bs", [128, 8], F32) for _ in range(GQ)]
            halves = [sb(es, "halves", [128, NIT], F32) for _ in range(GQ)]
            psI = [ps(es, "psI", [128, 512], F32) for _ in range(4)]
            pst = [ps(es, "pst", [128, 8, 128], BF16) for _ in range(2)]
            nI = [0, 0, 0]
            dma("sp", kiTs.ap, kiT[0], writes=[kiTs])
            dma("sp", dvl.ap, dvalid_d.rearrange("j p q -> p j q"), writes=[dvl])
            dma("sp", dng.ap, dneg_d.rearrange("j p q -> p j q"), writes=[dng])
            dma("sp", pw2.ap, pow2_d, writes=[pw2])
            for qb in range(NQB):
                nkb = qb + 1
                Lk = 512 * nkb
                tiles = list(range(GQ))
                for qs in tiles:
                    it = 4 * qb + qs
                    dma("sp", qit[qs].ap, qiT.rearrange("n p s -> p n s")[:, :, it * 128:(it + 1) * 128], writes=[qit[qs]])
                    dma("sp", wit[qs].ap, WI[it * 128:(it + 1) * 128, :], writes=[wit[qs]])
                for kb in range(nkb):
                    for h in range(8):
                        for qs in tiles:
                            q_, w_, sc = qit[qs], wit[qs], Sc[qs]
                            pI = psI[nI[0] % 4]
                            nI[0] += 1
                            r_ = rl[nI[1] % 4]
                            nI[1] += 1
                            lo_p = 64 * (h % 2)
                            op("pe", lambda: PE.matmul(pI.ap, lhsT=q_.ap[lo_p:lo_p + 64, h // 2, :],
                                                       rhs=kiTs.ap[lo_p:lo_p + 64, kb * 512:(kb + 1) * 512],
                                                       start=True, stop=True), [q_, kiTs], [pI])
                            op("act", lambda: A.activation(out=r_.ap, in_=pI.ap, func=AF.Relu, scale=w_.ap[:, h:h + 1]),
                               [pI, w_], [r_])
                            scs = sc.ap[:, kb * 512:(kb + 1) * 512]
                            if h == 0:
                                op("dve", lambda: V.tensor_scalar(out=scs, in0=r_.ap, scalar1=w_.ap[:, 8:9], scalar2=None,
                                                                  op0=ALU.mult), [r_, w_], [sc])
                            else:
                                op("dve", lambda: V.scalar_tensor_tensor(out=scs, in0=r_.ap, scalar=w_.ap[:, 8 + h:9 + h],
                                                                         in1=scs, op0=ALU.mult, op1=ALU.add),
                                   [r_, w_, sc], [sc])
                bis = [qs for qs in tiles if 4 * qb + qs >= 2]

                def each(fn, lst=tiles):
                    for qs in lst:
                        fn(qs)

                def dgv(qs):
                    return Sc[qs].ap[:, qb * 512:(qb + 1) * 512]

                each(lambda qs: op("dve", lambda: V.tensor_tensor(out=dgv(qs), in0=dgv(qs), in1=dvl.ap[:, qs, :],
                                                                   op=ALU.mult), [Sc[qs], dvl], [Sc[qs]]))
                each(lambda qs: op("dve", lambda: V.tensor_reduce(out=bs[qs].ap[:, 0:1], in_=Sc[qs].ap[:, 0:Lk], axis=AX.X,
                                                                   op=ALU.max), [Sc[qs]], [bs[qs]]), bis)
                each(lambda qs: op("dve", lambda: V.tensor_reduce(out=bs[qs].ap[:, 1:2], in_=Sc[qs].ap[:, 0:Lk], axis=AX.X,
                                                                   op=ALU.min), [Sc[qs]], [bs[qs]]), bis)
                each(lambda qs: op("dve", lambda: V.tensor_tensor(out=dgv(qs), in0=dgv(qs), in1=dng.ap[:, qs, :],
                                                                   op=ALU.add), [Sc[qs], dng], [Sc[qs]]))
                each(lambda qs: op("dve", lambda: V.tensor_tensor(out=bs[qs].ap[:, 2:3], in0=bs[qs].ap[:, 0:1],
                                                                   in1=bs[qs].ap[:, 1:2], op=ALU.subtract),
                                   [bs[qs]], [bs[qs]]), bis)
                each(lambda qs: op("dve", lambda: V.tensor_scalar(out=halves[qs].ap, in0=pw2.ap, scalar1=bs[qs].ap[:, 2:3],
                                                                   scalar2=None, op0=ALU.mult), [pw2, bs[qs]], [halves[qs]]),
                     bis)
                for k in range(NIT):
                    each(lambda qs: op("dve", lambda: V.tensor_tensor(out=bs[qs].ap[:, 3:4], in0=bs[qs].ap[:, 1:2],
                                                                       in1=halves[qs].ap[:, k:k + 1], op=ALU.add),
                                       [bs[qs], halves[qs]], [bs[qs]]), bis)
                    each(lambda qs: op("dve", lambda: V.tensor_scalar(out=junkb[qs].ap[:, 0:Lk], in0=Sc[qs].ap[:, 0:Lk],
                                                                       scalar1=bs[qs].ap[:, 3:4], scalar2=None,
                                                                       op0=ALU.is_ge, op1=ALU.add,
                                                                       accum_out=bs[qs].ap[:, 4:5]),
                                       [Sc[qs], bs[qs]], [junkb[qs], bs[qs]]), bis)
                    each(lambda qs: op("dve", lambda: V.tensor_scalar(out=bs[qs].ap[:, 5:6], in0=bs[qs].ap[:, 4:5],
                                                                       scalar1=255.5, scalar2=halves[qs].ap[:, k:k + 1],
                                                                       op0=ALU.is_ge, op1=ALU.mult),
                                       [bs[qs], halves[qs]], [bs[qs]]), bis)
                    each(lambda qs: op("dve", lambda: V.tensor_tensor(out=bs[qs].ap[:, 1:2], in0=bs[qs].ap[:, 1:2],
                                                                       in1=bs[qs].ap[:, 5:6], op=ALU.add),
                                       [bs[qs]], [bs[qs]]), bis)
                for qs in tiles:
                    if qs not in bis:
                        op("dve", lambda: V.memset(bs[qs].ap[:, 1:2], -1e29), [], [bs[qs]])
                for qs in tiles:
                    sl_ = sel[qs % 2]
                    op("dve", lambda: V.tensor_scalar(out=sl_.ap[:, 0:Lk], in0=Sc[qs].ap[:, 0:Lk], scalar1=bs[qs].ap[:, 1:2],
                                                      scalar2=None, op0=ALU.is_ge), [Sc[qs], bs[qs]], [sl_])
                    nm_ = nms[qs % 2]
                    for g in range(nkb):
                        pt = pst[nI[2] % 2]
                        nI[2] += 1
                        for k in range(4):
                            kt = 4 * g + k
                            op("pe", lambda: PE.transpose(out=pt.ap[:, k, :], in_=sl_.ap[:, kt * 128:(kt + 1) * 128],
                                                          identity=ident.ap), [sl_, ident], [pt])
                        op("act", lambda: A.activation(out=nm_.ap[:, 4 * g:4 * g + 4, :], in_=pt.ap[:, 0:4, :],
                                                       func=AF.Identity, bias=float(NEGM), scale=float(-NEGM)), [pt], [nm_])
                    dma("sp", NM[qb][:, 0:4 * nkb, qs * 128:(qs + 1) * 128], nm_.ap[:, 0:4 * nkb, :], reads=[nm_])
            sch.barrier()
            maybe_stop("B1")

        def attention(mixer):
            with ExitStack() as es:
                if mixer == "a":
                    H, width, col0 = 6, 768, 0
                    Vd, kTd, nkt_extra = Va, kaTn, True
                elif mixer == "b":
                    H, width, col0 = 5, 640, 768
                    Vd, kTd, nkt_extra = Vb, kbT, False
                else:
                    H, width, col0 = 5, 640, 1408
                    Vd, kTd, nkt_extra = Vc, kcT, False
                kTs = sb(es, "kTs", [128, H, S], BF16)
                Vs = sb(es, "Vs", [128, NT, H * VW], BF16)
                for h in range(H):
                    dma("sp", kTs.ap[:, h, :], kTd[h], writes=[kTs])
                for t0 in range(0, NT, 8):
                    t1 = min(NT, t0 + 8)
                    dma("sp", Vs.ap[:, t0:t1, :], Vd[t0 * 128:t1 * 128, :].rearrange("(t p) c -> p t c", p=128),
                        writes=[Vs])
                if mixer == "a":
                    krs = sb(es, "krs", [128, S], BF16)
                    dma("sp", krs.ap, kropeT[0], writes=[krs])
                    qrb = [sb(es, "qrb", [128, 3, 512], BF16) for _ in range(2)]
                if mixer == "b":
                    nmT = [sb(es, "nmT", [128, 4 * NQB, 512], BF16) for _ in range(1)]
                if mixer == "c":
                    subg = sb(es, "subg", [128, 128], F32)
                    lamt = sb(es, "lamt", [128, 4, 64], F32)
                    lams = sb(es, "lams", [128, 8], F32)
                    dma("sp", subg.ap, bcast_rows(subln_g_d[L:L + 1, :], 128), writes=[subg])
                    for i, n in enumerate(("lam_q1", "lam_k1", "lam_q2", "lam_k2")):
                        dma("sp", lamt.ap[:, i, :], bcast_rows(lam_d[n][L:L + 1, :], 64), writes=[lamt])
                    op("dve", lambda: V.tensor_scalar(out=subg.ap, in0=subg.ap, scalar1=float(1.0 - lam_init),
                                                      scalar2=None, op0=ALU.mult), [subg], [subg])
                    for j in range(2):
                        op("dve", lambda: V.tensor_tensor(out=lamt.ap[:, 2 * j, :], in0=lamt.ap[:, 2 * j, :],
                                                          in1=lamt.ap[:, 2 * j + 1, :], op=ALU.mult), [lamt], [lamt])
                        op("dve", lambda: V.reduce_sum(out=lams.ap[:, j:j + 1], in_=lamt.ap[:, 2 * j, :], axis=AX.X),
                           [lamt], [lams])
                    op("act", lambda: A.activation(out=lams.ap[:, 2:4], in_=lams.ap[:, 0:2], func=AF.Exp), [lams], [lams])
                    op("dve", lambda: V.tensor_tensor(out=lams.ap[:, 4:5], in0=lams.ap[:, 3:4], in1=lams.ap[:, 2:3],
                                                      op=ALU.subtract), [lams], [lams])
                    op("dve", lambda: V.tensor_scalar(out=lams.ap[:, 5:6], in0=lams.ap[:, 4:5], scalar1=float(-lam_init),
                                                      scalar2=None, op0=ALU.add), [lams], [lams])
                    t1s = [sb(es, "t1s", [128, 4, 128], F32) for _ in range(1)]
                    osb = [sb(es, "osb", [128, 128], F32) for _ in range(2)]
                    junkc = sb(es, "junkc", [128, 128], F32)
                qnb = [sb(es, "qnb", [128, H, 512], BF16) for _ in range(2)]
                sgb = [sb(es, "sgb", [128, 4, width], F32) for _ in range(2)]
                mxo = [sb(es, "mxo", [128, 4, width], BF16) for _ in range(2)]
                PTs = [sb(es, "PT", [128, 512], BF16) for _ in range(4)]
                rd = [sb(es, "rd", [128, 4], F32) for _ in range(4)]
                pS = [ps(es, "pS", [128, 512], F32) for _ in range(3)]
                pO = [[ps(es, "pO", [128, 512], F32) for _ in range(2)] for _ in range(2)]
                n = {"S": 0, "P": 0, "O": 0, "rd": 0, "osb": 0}

                def units_of(h):
                    if mixer == "c":
                        return [(h, 1), (h, 2)]
                    return [(h, 0)]

                for qb in range(NQB):
                    qq = qnb[qb % 2]
                    qsl = slice(qb * 512, (qb + 1) * 512)
                    qsrc = {"a": qaTn, "b": qbT, "c": qcT}[mixer]
                    dma("sp", qq.ap, qsrc.rearrange("n p s -> p n s")[:, :, qsl], writes=[qq])
                    if mixer == "a":
                        qr_ = qrb[qb % 2]
                        dma("sp", qr_.ap, qaTr.rearrange("n p s -> p n s")[:, :, qsl], writes=[qr_])
                    if mixer == "b":
                        nm_ = nmT[0]
                        dma("sp", nm_.ap[:, 0:4 * (qb + 1), :], NM[qb][:, 0:4 * (qb + 1), :], writes=[nm_])
                    sg_ = sgb[qb % 2]
                    dma("sp", sg_.ap, SG[qsl, col0:col0 + width].rearrange("(a p) c -> p a c", p=128), writes=[sg_])
                    mo = mxo[qb % 2]
                    nkt = 4 * (qb + 1)
                    for h in range(H):
                        for (hh, u) in units_of(h):
                            po = pO[n["O"] % 2]
                            n["O"] += 1
                            for kt in range(nkt):
                                j = kt - 4 * qb
                                c_lo = 128 * j if j > 0 else 0
                                cs_ = slice(c_lo, 512)
                                ksl = slice(kt * 128, (kt + 1) * 128)
                                s_ = pS[n["S"] % 3]
                                n["S"] += 1
                                need_mask = (mixer == "b") or (j >= 0)
                                if mixer == "a":
                                    lo_p = 64 * (h % 2)
                                    op("pe", lambda: PE.matmul(s_.ap[:, cs_], lhsT=kTs.ap[:, h, ksl], rhs=qq.ap[:, h, cs_],
                                                               start=True, stop=False), [kTs, qq], [s_])
                                    op("pe", lambda: PE.matmul(s_.ap[:, cs_], lhsT=krs.ap[lo_p:lo_p + 64, ksl],
                                                               rhs=qr_.ap[lo_p:lo_p + 64, h // 2, cs_],
                                                               start=False, stop=not need_mask), [krs, qr_], [s_])
                                elif mixer == "b":
                                    op("pe", lambda: PE.matmul(s_.ap[:, cs_], lhsT=kTs.ap[:, h, ksl], rhs=qq.ap[:, h, cs_],
                                                               start=True, stop=False), [kTs, qq], [s_])
                                else:
                                    lo_p = 64 * (u - 1)
                                    op("pe", lambda: PE.matmul(s_.ap[:, cs_], lhsT=kTs.ap[lo_p:lo_p + 64, h, ksl],
                                                               rhs=qq.ap[lo_p:lo_p + 64, h, cs_],
                                                               start=True, stop=not need_mask), [kTs, qq], [s_])
                                if need_mask:
                                    if mixer == "b":
                                        op("pe", lambda: PE.matmul(s_.ap[:, cs_], lhsT=ident.ap, rhs=nm_.ap[:, kt, cs_],
                                                                   start=False, stop=True), [ident, nm_], [s_])
                                    else:
                                        op("pe", lambda: PE.matmul(s_.ap[:, cs_], lhsT=ident.ap, rhs=dmask.ap[:, j, cs_],
                                                                   start=False, stop=True), [ident, dmask], [s_])
                                P_ = PTs[n["P"] % 4]
                                n["P"] += 1
                                op("act", lambda: A.activation(out=P_.ap[:, cs_], in_=s_.ap[:, cs_], func=AF.Exp),
                                   [s_], [P_])
                                for qs in range(max(j, 0), 4):
                                    bank = po[qs // 2]
                                    oc = (qs % 2) * 256
                                    first = (kt == 0 and qs % 2 == 0)
                                    op("pe", lambda: PE.matmul(bank.ap[:, oc:oc + 129], lhsT=P_.ap[:, qs * 128:(qs + 1) * 128],
                                                               rhs=Vs.ap[:, kt, h * VW:h * VW + 129],
                                                               start=first, stop=(kt == 4 * qb + qs),
                                                               skip_group_check=True), [P_, Vs], [bank])
                            for qs in range(4):
                                bank = po[qs // 2]
                                oc = (qs % 2) * 256
                                r_ = rd[n["rd"] % 4]
                                n["rd"] += 1
                                op("dve", lambda: V.reciprocal(out=r_.ap[:, 0:1], in_=bank.ap[:, oc + 128:oc + 129]),
                                   [bank], [r_])
                                dst = mo.ap[:, qs, h * 128:(h + 1) * 128]
                                gsl = sg_.ap[:, qs, h * 128:(h + 1) * 128]
                                if mixer != "c":
                                    op("dve", lambda: V.scalar_tensor_tensor(out=dst, in0=bank.ap[:, oc:oc + 128],
                                                                             scalar=r_.ap[:, 0:1], in1=gsl,
                                                                             op0=ALU.mult, op1=ALU.mult),
                                       [bank, r_, sg_], [mo])
                                elif u == 1:
                                    t1_ = t1s[0]
                                    op("act", lambda: A.mul(out=t1_.ap[:, qs, :], in_=bank.ap[:, oc:oc + 128],
                                                            mul=r_.ap[:, 0:1]), [bank, r_], [t1_])
                                else:
                                    t1_ = t1s[0]
                                    o_ = osb[n["osb"] % 2]
                                    n["osb"] += 1
                                    op("dve", lambda: V.tensor_tensor(out=r_.ap[:, 1:2], in0=r_.ap[:, 0:1],
                                                                      in1=lams.ap[:, 5:6], op=ALU.mult), [r_, lams], [r_])
                                    op("dve", lambda: V.scalar_tensor_tensor(out=o_.ap, in0=bank.ap[:, oc:oc + 128],
                                                                             scalar=r_.ap[:, 1:2], in1=t1_.ap[:, qs, :],
                                                                             op0=ALU.mult, op1=ALU.add),
                                       [bank, r_, t1_], [o_])
                                    op("act", lambda: A.activation(out=junkc.ap, in_=o_.ap, func=AF.Square,
                                                                   accum_out=r_.ap[:, 2:3]), [o_], [junkc, r_])
                                    op("act", lambda: A.activation(out=r_.ap[:, 3:4], in_=r_.ap[:, 2:3], func=AF.Sqrt,
                                                                   bias=EPS, scale=1.0 / 128.0), [r_], [r_])
                                    op("dve", lambda: V.reciprocal(out=r_.ap[:, 2:3], in_=r_.ap[:, 3:4]), [r_], [r_])
                                    op("dve", lambda: V.scalar_tensor_tensor(out=o_.ap, in0=o_.ap, scalar=r_.ap[:, 2:3],
                                                                             in1=subg.ap, op0=ALU.mult, op1=ALU.mult),
                                       [o_, r_, subg], [o_])
                                    op("dve", lambda: V.tensor_tensor(out=dst, in0=o_.ap, in1=gsl, op=ALU.mult),
                                       [o_, sg_], [mo])
                    dma("sp", MX[qsl, col0:col0 + width].rearrange("(a p) c -> p a c", p=128), mo.ap, reads=[mo])
                sch.barrier()
                maybe_stop("B" + mixer)

        attention("a")
        attention("b")
        attention("c")

        with ExitStack() as es:
            wo = sb(es, "wo", [128, 16, D], BF16)
            for c in range(4):
                for kc0 in range(0, 16, 8):
                    dma("pool", wo.ap[:, kc0:kc0 + 8, c * 512:(c + 1) * 512],
                        w_o_d[L].rearrange("(c p) n -> p c n", p=128)[:, kc0:kc0 + 8, c * 512:(c + 1) * 512], writes=[wo])
            mxt = [sb(es, "mxt", [128, D], BF16) for _ in range(2)]
            mT = [sb(es, "mT", [128, 16, 128], BF16) for _ in range(2)]
            hin = [sb(es, "hin", [128, D], F32) for _ in range(2)]
            h1 = [sb(es, "h1", [128, D], F32) for _ in range(2)]
            h1b = [sb(es, "h1b", [128, D], BF16) for _ in range(2)]
            h1T = [sb(es, "h1T", [128, 16, 128], BF16) for _ in range(2)]
            pT = [ps(es, "pT", [128, 8, 128], BF16) for _ in range(4)]
            pz = [ps(es, "pz", [128, 512], F32) for _ in range(4)]
            npT = 0
            for t in range(NT):
                i = t % 2
                rs = slice(t * 128, (t + 1) * 128)
                dma("sp", mxt[i].ap, MX[rs, :], writes=[mxt[i]])
                dma("sp", hin[i].ap, h_src[rs, :], writes=[hin[i]])
                for g in range(2):
                    pt = pT[npT % 4]
                    npT += 1
                    for c in range(8):
                        cc = 8 * g + c
                        op("pe", lambda: PE.transpose(out=pt.ap[:, c, :], in_=mxt[i].ap[:, cc * 128:(cc + 1) * 128],
                                                      identity=ident.ap), [mxt[i], ident], [pt])
                    op("act", lambda: A.copy(out=mT[i].ap[:, 8 * g:8 * g + 8, :], in_=pt.ap), [pt], [mT[i]])
                for c in range(4):
                    z = pz[c]
                    for kc in range(16):
                        op("pe", lambda: PE.matmul(z.ap, lhsT=mT[i].ap[:, kc, :], rhs=wo.ap[:, kc, c * 512:(c + 1) * 512],
                                                   start=(kc == 0), stop=(kc == 15)), [mT[i], wo], [z])
                    op("dve", lambda: V.tensor_tensor(out=h1[i].ap[:, c * 512:(c + 1) * 512], in0=z.ap,
                                                      in1=hin[i].ap[:, c * 512:(c + 1) * 512], op=ALU.add),
                       [z, hin[i]], [h1[i]])
                dma("sp", hbuf[rs, :], h1[i].ap, reads=[h1[i]])
                op("dve", lambda: V.tensor_copy(out=h1b[i].ap, in_=h1[i].ap), [h1[i]], [h1b[i]])
                for g in range(2):
                    pt = pT[npT % 4]
                    npT += 1
                    for c in range(8):
                        cc = 8 * g + c
                        op("pe", lambda: PE.transpose(out=pt.ap[:, c, :], in_=h1b[i].ap[:, cc * 128:(cc + 1) * 128],
                                                      identity=ident.ap), [h1b[i], ident], [pt])
                    op("act", lambda: A.copy(out=h1T[i].ap[:, 8 * g:8 * g + 8, :], in_=pt.ap), [pt], [h1T[i]])
                dma("sp", uT_d[t], h1T[i].ap, reads=[h1T[i]])
            sch.barrier()
            maybe_stop("C1")

        with ExitStack() as es:
            wpg = sb(es, "wpg", [128, 16, D], BF16)
            wple = sb(es, "wple", [128, 2, D], BF16)
            for c in range(4):
                for kc0 in range(0, 16, 8):
                    dma("pool", wpg.ap[:, kc0:kc0 + 8, c * 512:(c + 1) * 512],
                        w_pg_d[L].rearrange("(c p) n -> p c n", p=128)[:, kc0:kc0 + 8, c * 512:(c + 1) * 512], writes=[wpg])
            dma("pool", wple.ap, w_ple_d[L].rearrange("(c p) n -> p c n", p=128), writes=[wple])
            if last:
                fgb = sb(es, "fgb", [128, D], F32)
                dma("sp", fgb.ap, bcast_rows(final_g_d[0:1, :], D), writes=[fgb])
                junk = sb(es, "junk", [128, D], BF16)
                st = [sb(es, "st", [128, 2], F32) for _ in range(2)]
            h1 = [sb(es, "h1", [128, D], F32) for _ in range(2)]
            h1T = [sb(es, "h1T", [128, 16, 128], BF16) for _ in range(2)]
            pt_ = [sb(es, "pt_", [128, PLE], F32) for _ in range(2)]
            pb = [sb(es, "pb", [128, PLE], BF16) for _ in range(2)]
            pTs = [sb(es, "pTs", [128, 2, 128], BF16) for _ in range(2)]
            sg = [sb(es, "sg", [128, 512], F32) for _ in range(2)]
            tm = [sb(es, "tm", [128, 512], F32) for _ in range(2)]
            h2 = [sb(es, "h2", [128, D], F32) for _ in range(2)]
            pz = [ps(es, "pz", [128, 512], F32) for _ in range(6)]
            pT = [ps(es, "pT", [128, 8, 128], BF16) for _ in range(2)]
            nz = 0
            ns = 0
            for t in range(NT):
                i = t % 2
                rs = slice(t * 128, (t + 1) * 128)
                dma("sp", h1[i].ap, hbuf[rs, :], writes=[h1[i]])
                dma("sp", h1T[i].ap, uT_d[t], writes=[h1T[i]])
                dma("sp", pt_[i].ap, p_d[L][rs, :], writes=[pt_[i]])
                op("dve", lambda: V.tensor_copy(out=pb[i].ap, in_=pt_[i].ap), [pt_[i]], [pb[i]])
                for c in range(2):
                    op("pe", lambda: PE.transpose(out=pT[i].ap[:, c, :], in_=pb[i].ap[:, c * 128:(c + 1) * 128],
                                                  identity=ident.ap), [pb[i], ident], [pT[i]])
                op("act", lambda: A.copy(out=pTs[i].ap, in_=pT[i].ap[:, 0:2, :]), [pT[i]], [pTs[i]])
                for c in range(4):
                    cs_ = slice(c * 512, (c + 1) * 512)
                    zg = pz[nz % 6]
                    nz += 1
                    for kc in range(16):
                        op("pe", lambda: PE.matmul(zg.ap, lhsT=h1T[i].ap[:, kc, :], rhs=wpg.ap[:, kc, cs_],
                                                   start=(kc == 0), stop=(kc == 15)), [h1T[i], wpg], [zg])
                    zp = pz[nz % 6]
                    nz += 1
                    for kc in range(2):
                        op("pe", lambda: PE.matmul(zp.ap, lhsT=pTs[i].ap[:, kc, :], rhs=wple.ap[:, kc, cs_],
                                                   start=(kc == 0), stop=(kc == 1)), [pTs[i], wple], [zp])
                    s_ = sg[ns % 2]
                    t_ = tm[ns % 2]
                    ns += 1
                    op("act", lambda: A.activation(out=s_.ap, in_=zg.ap, func=AF.Sigmoid), [zg], [s_])
                    op("dve", lambda: V.tensor_tensor(out=t_.ap, in0=zp.ap, in1=s_.ap, op=ALU.mult), [zp, s_], [t_])
                    op("dve", lambda: V.tensor_tensor(out=h2[i].ap[:, cs_], in0=t_.ap, in1=h1[i].ap[:, cs_], op=ALU.add),
                       [t_, h1[i]], [h2[i]])
                if not last:
                    dma("sp", hbuf[rs, :], h2[i].ap, reads=[h2[i]])
                else:
                    rms_rstd(es, h2[i].ap, D, junk, st[i], [h2[i]])
                    op("dve", lambda: V.scalar_tensor_tensor(out=h1[i].ap, in0=h2[i].ap, scalar=st[i].ap[:, 0:1],
                                                             in1=fgb.ap, op0=ALU.mult, op1=ALU.mult),
                       [h2[i], st[i], fgb], [h1[i]])
                    dma("sp", y_d[rs, :], h1[i].ap, reads=[h1[i]])
            sch.barrier()
            maybe_stop("C2")

    except _StopBuild:
        return nc, sch
    top.close()
    return nc, sch


_CACHE = {}


def kernel(x, p, positions, w_in, w_uq, w_ukv, w_o, norm_g, q_norm_g, kv_norm_g,
           lam_q1, lam_k1, lam_q2, lam_k2, subln_g, w_ple, w_pg, final_g):
    x = np.asarray(x)
    B, S, _ = x.shape
    DEPTH = int(np.asarray(w_in).shape[0])
    key = (S, DEPTH)
    if key not in _CACHE:
        _CACHE[key] = build_program(S, DEPTH)[0]
    nc = _CACHE[key]
    consts = host_consts()
    f32 = lambda a: np.ascontiguousarray(np.asarray(a), dtype=np.float32)
    shared = {
        "w_in": f32(w_in), "w_uq": f32(w_uq), "w_ukv": f32(w_ukv), "w_o": f32(w_o),
        "norm_g": f32(norm_g), "q_norm_g": f32(q_norm_g), "kv_norm_g": f32(kv_norm_g),
        "lam_q1": f32(lam_q1), "lam_k1": f32(lam_k1), "lam_q2": f32(lam_q2), "lam_k2": f32(lam_k2),
        "subln_g": f32(subln_g), "w_ple": f32(w_ple), "w_pg": f32(w_pg),
        "final_g": f32(final_g).reshape(1, D),
    }
    shared.update(consts)
    p = np.asarray(p)
    positions = np.asarray(positions)
    in_maps = []
    for c in range(8):
        b = c % B
        m = dict(shared)
        m["x"] = f32(x[b])
        m["p"] = f32(p[:, b])
        m["positions"] = np.ascontiguousarray(positions[b].astype(np.int32).reshape(S // 128, 128).T)
        in_maps.append(m)
    res = run_bass_kernel_spmd(nc, in_maps, core_ids=list(range(8)))
    out = np.stack([np.asarray(res.results[b]["y"], dtype=np.float32) for b in range(B)], axis=0)
    return out
```

```python
import math
import os
from contextlib import ExitStack

import numpy as np
import ml_dtypes
import concourse.bass as bass
import concourse.mybir as mybir
from concourse.bass_utils import run_bass_kernel_spmd

F32 = mybir.dt.float32
BF16 = mybir.dt.bfloat16
I32 = mybir.dt.int32
ALU = mybir.AluOpType
AF = mybir.ActivationFunctionType
AX = mybir.AxisListType

D = 2048
PLE = 256
D_IN = 7176
EPS = 1e-6
THETA = 500000.0
NIT = 22
NEGM = -30000.0
SA = 192.0 ** -0.5
SB = 128.0 ** -0.5
SC = 64.0 ** -0.5
VW = 132


class Buf:
    __slots__ = ("name", "writers", "readers", "ap", "psum")

    def __init__(self, name="", ap=None, psum=False):
        self.psum = psum
        self.name = name
        self.writers = {}
        self.readers = {}
        self.ap = ap


class Sched:
    NDMA = 8

    def __init__(self, nc):
        self.nc = nc
        self.eng = {"pe": nc.tensor, "act": nc.scalar, "dve": nc.vector,
                    "pool": nc.gpsimd, "sp": nc.sync}
        self.sems = {}
        self.cnt = {}
        for e in ("pe", "act", "dve", "pool"):
            self.sems[e] = nc.alloc_semaphore("s_" + e)
            self.cnt[e] = 0
        self.dq = {}
        for q in ("sp", "pool"):
            for i in range(self.NDMA):
                k = "d_%s%d" % (q, i)
                self.sems[k] = nc.alloc_semaphore(k)
                self.cnt[k] = 0
            self.dq[q] = 0
        self.seen = {e: {} for e in self.eng}
        self.nins = 0
        self.nwait = 0

    def _wait(self, e, evs):
        best = {}
        for (k, v) in evs:
            if v > best.get(k, 0):
                best[k] = v
        seen = self.seen[e]
        for k, v in best.items():
            if k == "pe" and e == "pe":
                continue
            if seen.get(k, 0) < v:
                self.eng[e].wait_ge(self.sems[k], v)
                seen[k] = v
                self.nwait += 1

    @staticmethod
    def _deps(reads, writes, e=None):
        evs = []
        for b in reads:
            evs.extend(b.writers.items())
            if b.psum:
                evs.extend((k, v) for k, v in b.readers.items() if k != e)
        for b in writes:
            evs.extend(b.writers.items())
            evs.extend(b.readers.items())
        return evs

    @staticmethod
    def _commit(ev, reads, writes):
        k, v = ev
        for b in reads:
            if b.readers.get(k, 0) < v:
                b.readers[k] = v
        for b in writes:
            b.writers = {k: v}
            b.readers = {}

    def op(self, e, ins_fn, reads=(), writes=()):
        self._wait(e, self._deps(reads, writes, e))
        ins = ins_fn()
        self.cnt[e] += 1
        ins.then_inc(self.sems[e], 1)
        self._commit((e, self.cnt[e]), reads, writes)
        self.nins += 1

    def dma(self, q, out, in_, reads=(), writes=()):
        i = self.dq[q]
        self.dq[q] = (i + 1) % self.NDMA
        k = "d_%s%d" % (q, i)
        evs = self._deps(reads, writes)
        if self.cnt[k] > 0:
            evs.append((k, self.cnt[k]))
        self._wait(q, evs)
        ins = self.eng[q].dma_start(out=out, in_=in_)
        self.cnt[k] += 16
        ins.then_inc(self.sems[k], 16)
        self._commit((k, self.cnt[k]), reads, writes)
        self.nins += 1

    def barrier(self):
        evs = [(k, v) for k, v in self.cnt.items() if v > 0]
        for e in self.eng:
            self._wait(e, list(evs))


def host_consts():
    c = {}
    c["ident"] = np.eye(128, dtype=np.float32).astype(ml_dtypes.bfloat16)
    invf = np.zeros((128, 56), np.float32)
    off = 0
    for n_rot in (64, 32, 16):
        half = n_rot // 2
        f = 1.0 / (np.float32(THETA) ** (np.arange(half, dtype=np.float32) * np.float32(2.0 / n_rot)))
        invf[:, off:off + half] = f.astype(np.float32)[None, :]
        off += half
    c["invf"] = invf
    dm = np.zeros((4, 128, 512), np.float32)
    for j in range(4):
        kk = 128 * j + np.arange(128)[:, None]
        qq = np.arange(512)[None, :]
        dm[j] = np.where((kk // 64) <= (qq // 64), 0.0, NEGM)
    c["dmask"] = dm.astype(ml_dtypes.bfloat16)
    dv = np.zeros((4, 128, 512), np.float32)
    for qs in range(4):
        qq = 128 * qs + np.arange(128)[:, None]
        kk = np.arange(512)[None, :]
        dv[qs] = ((kk // 64) <= (qq // 64)).astype(np.float32)
    c["dvalid"] = dv
    c["dneg"] = ((dv - 1.0) * 1e30).astype(np.float32)
    c["pow2"] = np.tile((0.5 ** np.arange(1, NIT + 1, dtype=np.float64)).astype(np.float32)[None, :], (128, 1))
    return c


CHUNKS = [
    (0, 384, "cq", None),
    (384, 320, "ckv", None),
    (704, 512, "gate", 0),
    (1216, 256, "gate", 512),
    (1472, 512, "qb", (0, 4)),
    (1984, 128, "qb", (4, 1)),
    (2112, 512, "kb", (0, 4)),
    (2624, 128, "kb", (4, 1)),
    (2752, 512, "vb", (0, 4)),
    (3264, 128, "vb", (4, 1)),
    (3392, 512, "qi", None),
    (3904, 72, "kiw", None),
    (3976, 512, "gate", 768),
    (4488, 128, "gate", 1280),
    (4616, 512, "qc", (0, 8)),
    (5128, 128, "qc", (8, 2)),
    (5256, 512, "kc", (0, 8)),
    (5768, 128, "kc", (8, 2)),
    (5896, 512, "vc", (0, 4)),
    (6408, 128, "vc", (4, 1)),
    (6536, 512, "gate", 1408),
    (7048, 128, "gate", 1920),
]
SEGS = [(2 * i, 2 * i + 1) for i in range(11)]


class _StopBuild(Exception):
    pass


def build_program(S, DEPTH, debug_layers=None):
    assert S % 512 == 0
    STOP = os.environ.get("MK_STOP", "")

    def maybe_stop(tag):
        if STOP == tag:
            raise _StopBuild()
    NT = S // 128
    NQB = S // 512
    nc = bass.Bass("TRN2", target_bir_lowering=False)
    sch = Sched(nc)
    uid = [0]

    def din(name, shape, dt):
        return nc.dram_tensor(name, list(shape), dt, kind="ExternalInput").ap()

    def dscr(name, shape, dt):
        return nc.dram_tensor(name, list(shape), dt).ap()

    x_d = din("x", [S, D], F32)
    p_d = din("p", [DEPTH, S, PLE], F32)
    pos_d = din("positions", [128, S // 128], I32)
    w_in_d = din("w_in", [DEPTH, D, D_IN], F32)
    w_uq_d = din("w_uq", [DEPTH, 384, 1152], F32)
    w_ukv_d = din("w_ukv", [DEPTH, 256, 1536], F32)
    w_o_d = din("w_o", [DEPTH, D, D], F32)
    norm_g_d = din("norm_g", [DEPTH, D], F32)
    q_norm_g_d = din("q_norm_g", [DEPTH, 384], F32)
    kv_norm_g_d = din("kv_norm_g", [DEPTH, 256], F32)
    lam_d = {n: din(n, [DEPTH, 64], F32) for n in ("lam_q1", "lam_k1", "lam_q2", "lam_k2")}
    subln_g_d = din("subln_g", [DEPTH, 128], F32)
    w_ple_d = din("w_ple", [DEPTH, PLE, D], F32)
    w_pg_d = din("w_pg", [DEPTH, D, D], F32)
    final_g_d = din("final_g", [1, D], F32)
    ident_d = din("ident", [128, 128], BF16)
    invf_d = din("invf", [128, 56], F32)
    dmask_d = din("dmask", [4, 128, 512], BF16)
    dvalid_d = din("dvalid", [4, 128, 512], F32)
    dneg_d = din("dneg", [4, 128, 512], F32)
    pow2_d = din("pow2", [128, NIT], F32)
    y_d = nc.dram_tensor("y", [S, D], F32, kind="ExternalOutput").ap()

    hbuf = dscr("hbuf", [S, D], F32)
    uT_d = dscr("uT_d", [NT, 128, 16, 128], BF16)
    rope_d = dscr("rope_d", [128, NT, 224], F32)
    qaTn = dscr("qaTn", [6, 128, S], BF16)
    qaTr = dscr("qaTr", [3, 128, S], BF16)
    kaTn = dscr("kaTn", [6, 128, S], BF16)
    kropeT = dscr("kropeT", [1, 128, S], BF16)
    Va = dscr("Va", [S, 6 * VW], BF16)
    qbT = dscr("qbT", [5, 128, S], BF16)
    kbT = dscr("kbT", [5, 128, S], BF16)
    Vb = dscr("Vb", [S, 5 * VW], BF16)
    qiT = dscr("qiT", [4, 128, S], BF16)
    kiT = dscr("kiT", [1, 128, S], BF16)
    WI = dscr("WI", [S, 16], F32)
    qcT = dscr("qcT", [5, 128, S], BF16)
    kcT = dscr("kcT", [5, 128, S], BF16)
    Vc = dscr("Vc", [S, 5 * VW], BF16)
    SG = dscr("SG", [S, D], F32)
    MX = dscr("MX", [S, D], BF16)
    NM = dscr("NM", [NQB, 128, 4 * NQB, 512], BF16)

    def sb(es, name, shape, dt):
        uid[0] += 1
        h = es.enter_context(nc.sbuf_tensor("%s_%d" % (name, uid[0]), list(shape), dt))
        return Buf(name, h.ap())

    def ps(es, name, shape, dt):
        uid[0] += 1
        h = es.enter_context(nc.psum_tensor("%s_%d" % (name, uid[0]), list(shape), dt))
        return Buf(name, h.ap(), psum=True)

    V = nc.vector
    G = nc.gpsimd
    A = nc.scalar
    PE = nc.tensor
    op = sch.op
    dma = sch.dma

    top = ExitStack()
    ident = sb(top, "ident", [128, 128], BF16)
    dmask = sb(top, "dmask", [128, 4, 512], BF16)
    dma("sp", ident.ap, ident_d, writes=[ident])
    dma("sp", dmask.ap, dmask_d.rearrange("j p q -> p j q"), writes=[dmask])

    def bcast_rows(src_row_ap, n):
        return src_row_ap.to_broadcast([128, n])

    ROFF = {"aq": (0, 32), "ak": (64, 32), "bq": (128, 16), "bk": (160, 16), "i": (192, 8), "cq": (208, 8)}
    with ExitStack() as es:
        posi = sb(es, "posi", [128, NT], I32)
        posf = sb(es, "posf", [128, NT], F32)
        invf = sb(es, "invf", [128, 56], F32)
        ang = sb(es, "ang", [128, NT, 56], F32)
        r1 = sb(es, "r1", [128, NT, 56], F32)
        cs = sb(es, "cs", [128, NT, 56], F32)
        sn = sb(es, "sn", [128, NT, 56], F32)
        tab = sb(es, "tab", [128, NT, 224], F32)
        dma("sp", posi.ap, pos_d, writes=[posi])
        dma("sp", invf.ap, invf_d, writes=[invf])
        op("dve", lambda: V.tensor_copy(out=posf.ap, in_=posi.ap), [posi], [posf])
        for t in range(NT):
            op("dve", lambda: V.tensor_scalar(out=ang.ap[:, t, :], in0=invf.ap, scalar1=posf.ap[:, t:t + 1],
                                              scalar2=None, op0=ALU.mult), [invf, posf], [ang])
        twopi = 2.0 * math.pi
        ki = sb(es, "ki", [128, NT, 56], I32)
        kf = sb(es, "kf", [128, NT, 56], F32)

        def sin_of(shift, dst):
            op("dve", lambda: V.tensor_scalar(out=r1.ap, in0=ang.ap, scalar1=float(shift), scalar2=None, op0=ALU.add),
               [ang], [r1])
            op("dve", lambda: V.tensor_scalar(out=kf.ap, in0=r1.ap, scalar1=1.0 / twopi, scalar2=None, op0=ALU.mult),
               [r1], [kf])
            op("dve", lambda: V.tensor_copy(out=ki.ap, in_=kf.ap), [kf], [ki])
            op("dve", lambda: V.tensor_copy(out=kf.ap, in_=ki.ap), [ki], [kf])
            op("dve", lambda: V.scalar_tensor_tensor(out=r1.ap, in0=kf.ap, scalar=-twopi, in1=r1.ap,
                                                     op0=ALU.mult, op1=ALU.add), [kf, r1], [r1])
            op("dve", lambda: V.tensor_scalar(out=kf.ap, in0=r1.ap, scalar1=math.pi, scalar2=-twopi,
                                              op0=ALU.is_gt, op1=ALU.mult), [r1], [kf])
            op("dve", lambda: V.tensor_tensor(out=r1.ap, in0=r1.ap, in1=kf.ap, op=ALU.add), [r1, kf], [r1])
            op("dve", lambda: V.tensor_scalar(out=kf.ap, in0=r1.ap, scalar1=-math.pi, scalar2=twopi,
                                              op0=ALU.is_lt, op1=ALU.mult), [r1], [kf])
            op("dve", lambda: V.tensor_tensor(out=r1.ap, in0=r1.ap, in1=kf.ap, op=ALU.add), [r1, kf], [r1])
            op("act", lambda: A.activation(out=dst.ap, in_=r1.ap, func=AF.Sin), [r1], [dst])

        sin_of(0.0, sn)
        sin_of(0.5 * math.pi, cs)
        specs = [("aq", 0, SA), ("ak", 0, 1.0), ("bq", 32, SB), ("bk", 32, 1.0), ("i", 48, 1.0), ("cq", 48, SC)]
        for (nm, so, scl) in specs:
            o, hf = ROFF[nm]
            op("dve", lambda: V.tensor_scalar(out=tab.ap[:, :, o:o + hf], in0=cs.ap[:, :, so:so + hf],
                                              scalar1=float(scl), scalar2=None, op0=ALU.mult), [cs], [tab])
            op("dve", lambda: V.tensor_scalar(out=tab.ap[:, :, o + hf:o + 2 * hf], in0=sn.ap[:, :, so:so + hf],
                                              scalar1=float(scl), scalar2=None, op0=ALU.mult), [sn], [tab])
        dma("sp", rope_d, tab.ap, reads=[tab])
        sch.barrier()

    def rms_rstd(es_tmp, src_ap, n, junk, st, reads):
        op("act", lambda: A.activation(out=junk.ap[:, 0:n], in_=src_ap, func=AF.Square, accum_out=st.ap[:, 0:1]),
           reads, [junk, st])
        op("act", lambda: A.activation(out=st.ap[:, 1:2], in_=st.ap[:, 0:1], func=AF.Sqrt, bias=EPS, scale=1.0 / n),
           [st], [st])
        op("dve", lambda: V.reciprocal(out=st.ap[:, 0:1], in_=st.ap[:, 1:2]), [st], [st])

    maybe_stop_holder = [None]
    try:
      maybe_stop("P")
      for L in range(DEPTH):
        h_src = x_d if L == 0 else hbuf
        last = (L == DEPTH - 1)
        lam_init = 0.8 - 0.6 * math.exp(-0.3 * L)

        with ExitStack() as es:
            gbc = sb(es, "gbc", [128, D], F32)
            dma("sp", gbc.ap, bcast_rows(norm_g_d[L:L + 1, :], D), writes=[gbc])
            hin = [sb(es, "hin", [128, D], F32) for _ in range(2)]
            ub = [sb(es, "ub", [128, D], BF16) for _ in range(2)]
            uTs = [sb(es, "uTs", [128, 16, 128], BF16) for _ in range(2)]
            junk = sb(es, "junk", [128, D], BF16)
            st = [sb(es, "st", [128, 2], F32) for _ in range(2)]
            pT = [ps(es, "pT", [128, 8, 128], BF16) for _ in range(4)]
            npT = 0
            dma("sp", hin[0].ap, h_src[0:128, :], writes=[hin[0]])
            for t in range(NT):
                i = t % 2
                if t + 1 < NT:
                    dma("sp", hin[1 - i].ap, h_src[(t + 1) * 128:(t + 2) * 128, :], writes=[hin[1 - i]])
                rms_rstd(es, hin[i].ap, D, junk, st[i], [hin[i]])
                op("dve", lambda: V.scalar_tensor_tensor(out=ub[i].ap, in0=hin[i].ap, scalar=st[i].ap[:, 0:1],
                                                         in1=gbc.ap, op0=ALU.mult, op1=ALU.mult),
                   [hin[i], st[i], gbc], [ub[i]])
                for g in range(2):
                    pt = pT[npT % 4]
                    npT += 1
                    for c in range(8):
                        cc = 8 * g + c
                        op("pe", lambda: PE.transpose(out=pt.ap[:, c, :], in_=ub[i].ap[:, cc * 128:(cc + 1) * 128],
                                                      identity=ident.ap), [ub[i], ident], [pt])
                    op("act", lambda: A.copy(out=uTs[i].ap[:, 8 * g:8 * g + 8, :], in_=pt.ap), [pt], [uTs[i]])
                dma("sp", uT_d[t], uTs[i].ap, reads=[uTs[i]])
            sch.barrier()
            maybe_stop("A1")

        with ExitStack() as es:
            WMAX = 768
            wbuf = [sb(es, "wbuf", [128, 16, WMAX], BF16) for _ in range(2)]
            wuq = sb(es, "wuq", [128, 3, 1152], BF16)
            wukv = sb(es, "wukv", [128, 2, 1536], BF16)
            qgb = sb(es, "qgb", [128, 384], F32)
            kvgb = sb(es, "kvgb", [128, 256], F32)
            tab = sb(es, "tab", [128, NT, 224], F32)
            uTs = [sb(es, "uTs", [128, 16, 128], BF16) for _ in range(3)]
            pz = [ps(es, "pz", [128, 512], F32) for _ in range(4)]
            ptr = [ps(es, "ptr", [128, 8, 128], BF16) for _ in range(2)]
            junkf = sb(es, "junkf", [128, 512], F32)
            st = sb(es, "st", [128, 2], F32)
            cqn = sb(es, "cqn", [128, 384], BF16)
            cqT = sb(es, "cqT", [128, 3, 128], BF16)
            ckvn = sb(es, "ckvn", [128, 256], BF16)
            ckvT = sb(es, "ckvT", [128, 2, 128], BF16)
            qn = sb(es, "qn", [128, 6, 128], BF16)
            qr = sb(es, "qr", [128, 6, 64], BF16)
            kn = sb(es, "kn", [128, 6, 128], BF16)
            kr = sb(es, "kr", [128, 2, 64], BF16)
            vx6 = sb(es, "vx6", [128, 6, VW], BF16)
            vx5 = sb(es, "vx5", [128, 5, VW], BF16)
            hd5 = sb(es, "hd5", [128, 5, 128], BF16)
            qis = sb(es, "qis", [128, 8, 64], BF16)
            kis = sb(es, "kis", [128, 2, 64], BF16)
            wi = sb(es, "wi", [128, 16], F32)
            sgt = [sb(es, "sgt", [128, 512], F32) for _ in range(2)]
            rt = [[sb(es, "rt", [128, 128], F32) for _ in range(4)] for _ in range(2)]
            stage = [sb(es, "stage", [128, 8, 128], BF16) for _ in range(2)]
            cnt = {"pz": 0, "ptr": 0, "rt": 0, "stage": 0, "sgt": 0}

            dma("sp", tab.ap, rope_d, writes=[tab])
            dma("sp", qgb.ap, bcast_rows(q_norm_g_d[L:L + 1, :], 384), writes=[qgb])
            dma("sp", kvgb.ap, bcast_rows(kv_norm_g_d[L:L + 1, :], 256), writes=[kvgb])
            dma("pool", wuq.ap, w_uq_d[L].rearrange("(c p) n -> p c n", p=128), writes=[wuq])
            dma("pool", wukv.ap, w_ukv_d[L].rearrange("(c p) n -> p c n", p=128), writes=[wukv])
            op("dve", lambda: V.memset(vx6.ap, 1.0), [], [vx6])
            op("dve", lambda: V.memset(vx5.ap, 1.0), [], [vx5])

            def load_w(si):
                c0 = CHUNKS[SEGS[si][0]][0]
                c1 = CHUNKS[SEGS[si][1]][0] + CHUNKS[SEGS[si][1]][1]
                wb = wbuf[si % 2]
                src = w_in_d[L].rearrange("(c p) n -> p c n", p=128)
                for kc0 in range(0, 16, 4):
                    dma("pool", wb.ap[:, kc0:kc0 + 4, 0:c1 - c0], src[:, kc0:kc0 + 4, c0:c1], writes=[wb])

            def nxt(key, lst):
                cnt[key] += 1
                return lst[cnt[key] % len(lst)]

            def rope_evac(src3, dst3, H, Dh, n_rot, tname, t, scale, reads, writes):
                half = n_rot // 2
                o, hf = ROFF[tname]
                assert hf == half
                cb = tab.ap[:, t, o:o + half].unsqueeze(1).to_broadcast([128, H, half])
                sbc = tab.ap[:, t, o + half:o + 2 * half].unsqueeze(1).to_broadcast([128, H, half])
                r = nxt("rt", rt)
                n = H * half
                v = [r[k].ap[:, 0:n].rearrange("p (h d) -> p h d", h=H) for k in range(4)]
                x1 = src3[:, :, 0:half]
                x2 = src3[:, :, half:n_rot]
                if Dh > n_rot:
                    op("act", lambda: A.mul(out=dst3[:, :, n_rot:Dh], in_=src3[:, :, n_rot:Dh], mul=float(scale)),
                       reads, writes)
                op("dve", lambda: V.tensor_tensor(out=v[0], in0=x1, in1=cb, op=ALU.mult), reads + [tab], [r[0]])
                op("dve", lambda: V.tensor_tensor(out=v[1], in0=x2, in1=sbc, op=ALU.mult), reads + [tab], [r[1]])
                op("dve", lambda: V.tensor_tensor(out=v[2], in0=x2, in1=cb, op=ALU.mult), reads + [tab], [r[2]])
                op("dve", lambda: V.tensor_tensor(out=v[3], in0=x1, in1=sbc, op=ALU.mult), reads + [tab], [r[3]])
                op("dve", lambda: V.tensor_tensor(out=dst3[:, :, 0:half], in0=v[0], in1=v[1], op=ALU.subtract),
                   [r[0], r[1]], writes)
                op("dve", lambda: V.tensor_tensor(out=dst3[:, :, half:n_rot], in0=v[2], in1=v[3], op=ALU.add),
                   [r[2], r[3]], writes)

            def fm_store(src, src3, n, dst, t):
                for g0 in range(0, n, 8):
                    g1 = min(n, g0 + 8)
                    pt = nxt("ptr", ptr)
                    sg_ = nxt("stage", stage)
                    for k in range(g0, g1):
                        op("pe", lambda: PE.transpose(out=pt.ap[:, k - g0, :], in_=src3[:, k, :], identity=ident.ap),
                           [src, ident], [pt])
                    op("act", lambda: A.copy(out=sg_.ap[:, 0:g1 - g0, :], in_=pt.ap[:, 0:g1 - g0, :]), [pt], [sg_])
                    dma("sp", dst.rearrange("n p s -> p n s")[:, g0:g1, t * 128:(t + 1) * 128],
                        sg_.ap[:, 0:g1 - g0, :], reads=[sg_])

            def rmsnorm_to(src_ap, n, gb, dst, reads):
                rms_rstd(es, src_ap, n, junkf, st, reads)
                op("dve", lambda: V.scalar_tensor_tensor(out=dst.ap, in0=src_ap, scalar=st.ap[:, 0:1], in1=gb.ap,
                                                         op0=ALU.mult, op1=ALU.mult), reads + [st, gb], [dst])

            def epilogue(ci, z, t):
                col0, width, kind, meta = CHUNKS[ci]
                zs = z.ap[:, 0:width]
                r0 = t * 128
                if kind == "cq":
                    rmsnorm_to(zs, 384, qgb, cqn, [z])
                    pt = nxt("ptr", ptr)
                    for c in range(3):
                        op("pe", lambda: PE.transpose(out=pt.ap[:, c, :], in_=cqn.ap[:, c * 128:(c + 1) * 128],
                                                      identity=ident.ap), [cqn, ident], [pt])
                    op("act", lambda: A.copy(out=cqT.ap, in_=pt.ap[:, 0:3, :]), [pt], [cqT])
                    for c in range(3):
                        z2 = nxt("pz", pz)
                        for kc in range(3):
                            op("pe", lambda: PE.matmul(z2.ap[:, 0:384], lhsT=cqT.ap[:, kc, :],
                                                       rhs=wuq.ap[:, kc, c * 384:(c + 1) * 384],
                                                       start=(kc == 0), stop=(kc == 2)), [cqT, wuq], [z2])
                        v3 = z2.ap[:, 0:384].rearrange("p (h d) -> p h d", h=2)
                        op("act", lambda: A.mul(out=qn.ap[:, 2 * c:2 * c + 2, :], in_=v3[:, :, 0:128], mul=float(SA)),
                           [z2], [qn])
                        rope_evac(v3[:, :, 128:192], qr.ap[:, 2 * c:2 * c + 2, :], 2, 64, 64, "aq", t, SA, [z2], [qr])
                    fm_store(qn, qn.ap, 6, qaTn, t)
                    fm_store(qr, qr.ap.rearrange("p (a b) d -> p a (b d)", b=2), 3, qaTr, t)
                elif kind == "ckv":
                    LVL = int(os.environ.get("MK_LVL", 99))
                    rmsnorm_to(z.ap[:, 0:256], 256, kvgb, ckvn, [z])
                    if LVL < 2:
                        return
                    pt = nxt("ptr", ptr)
                    for c in range(2):
                        op("pe", lambda: PE.transpose(out=pt.ap[:, c, :], in_=ckvn.ap[:, c * 128:(c + 1) * 128],
                                                      identity=ident.ap), [ckvn, ident], [pt])
                    op("act", lambda: A.copy(out=ckvT.ap, in_=pt.ap[:, 0:2, :]), [pt], [ckvT])
                    if LVL < 3:
                        return
                    rope_evac(z.ap[:, 256:320].rearrange("p (h d) -> p h d", h=1), kr.ap[:, 0:1, :], 1, 64, 64, "ak", t,
                              1.0, [z], [kr])
                    if LVL < 4:
                        return
                    op("act", lambda: A.copy(out=kr.ap[:, 1, :], in_=kr.ap[:, 0, :]), [kr], [kr])
                    if LVL < 5:
                        return
                    for c in range(3):
                        z2 = nxt("pz", pz)
                        for kc in range(2):
                            op("pe", lambda: PE.matmul(z2.ap, lhsT=ckvT.ap[:, kc, :],
                                                       rhs=wukv.ap[:, kc, c * 512:(c + 1) * 512],
                                                       start=(kc == 0), stop=(kc == 1)), [ckvT, wukv], [z2])
                        v3 = z2.ap.rearrange("p (h d) -> p h d", h=2)
                        SUB = int(os.environ.get("MK_SUB", 3))
                        if SUB & 1:
                            op("act", lambda: A.copy(out=kn.ap[:, 2 * c:2 * c + 2, :], in_=v3[:, :, 0:128]), [z2], [kn])
                        if SUB & 2:
                            op("dve", lambda: V.tensor_copy(out=vx6.ap[:, 2 * c:2 * c + 2, 0:128], in_=v3[:, :, 128:256]),
                               [z2], [vx6])
                    if LVL < 6:
                        return
                    fm_store(kn, kn.ap, 6, kaTn, t)
                    if LVL < 7:
                        return
                    fm_store(kr, kr.ap.rearrange("p (a b) d -> p a (b d)", b=2), 1, kropeT, t)
                    if LVL < 8:
                        return
                    dma("sp", Va[r0:r0 + 128, :], vx6.ap.rearrange("p h d -> p (h d)"), reads=[vx6])
                elif kind == "gate":
                    s_ = nxt("sgt", sgt)
                    op("act", lambda: A.activation(out=s_.ap[:, 0:width], in_=zs, func=AF.Silu), [z], [s_])
                    dma("sp", SG[r0:r0 + 128, meta:meta + width], s_.ap[:, 0:width], reads=[s_])
                elif kind in ("qb", "kb"):
                    h0, nh = meta
                    v3 = zs.rearrange("p (h d) -> p h d", h=nh)
                    rope_evac(v3, hd5.ap[:, h0:h0 + nh, :], nh, 128, 32, "bq" if kind == "qb" else "bk", t,
                              SB if kind == "qb" else 1.0, [z], [hd5])
                    if h0 + nh == 5:
                        fm_store(hd5, hd5.ap, 5, qbT if kind == "qb" else kbT, t)
                elif kind in ("vb", "vc"):
                    h0, nh = meta
                    v3 = zs.rearrange("p (h d) -> p h d", h=nh)
                    op("dve", lambda: V.tensor_copy(out=vx5.ap[:, h0:h0 + nh, 0:128], in_=v3), [z], [vx5])
                    if h0 + nh == 5:
                        dst = Vb if kind == "vb" else Vc
                        dma("sp", dst[r0:r0 + 128, :], vx5.ap.rearrange("p h d -> p (h d)"), reads=[vx5])
                elif kind == "qi":
                    v3 = zs.rearrange("p (h d) -> p h d", h=8)
                    rope_evac(v3, qis.ap, 8, 64, 16, "i", t, 1.0, [z], [qis])
                    fm_store(qis, qis.ap.rearrange("p (a b) d -> p a (b d)", b=2), 4, qiT, t)
                elif kind == "kiw":
                    rope_evac(z.ap[:, 0:64].rearrange("p (h d) -> p h d", h=1), kis.ap[:, 0:1, :], 1, 64, 16, "i", t,
                              1.0, [z], [kis])
                    op("act", lambda: A.copy(out=kis.ap[:, 1, :], in_=kis.ap[:, 0, :]), [kis], [kis])
                    fm_store(kis, kis.ap.rearrange("p (a b) d -> p a (b d)", b=2), 1, kiT, t)
                    op("act", lambda: A.activation(out=wi.ap[:, 0:8], in_=z.ap[:, 64:72], func=AF.Abs), [z], [wi])
                    op("dve", lambda: V.tensor_scalar(out=wi.ap[:, 8:16], in0=z.ap[:, 64:72], scalar1=0.0, scalar2=2.0,
                                                      op0=ALU.is_ge, op1=ALU.mult), [z], [wi])
                    op("dve", lambda: V.tensor_scalar(out=wi.ap[:, 8:16], in0=wi.ap[:, 8:16], scalar1=-1.0,
                                                      scalar2=None, op0=ALU.add), [wi], [wi])
                    dma("sp", WI[r0:r0 + 128, :], wi.ap, reads=[wi])
                elif kind in ("qc", "kc"):
                    h0, nh = meta
                    v3 = zs.rearrange("p (h d) -> p h d", h=nh)
                    d3 = hd5.ap.rearrange("p a (b d) -> p (a b) d", b=2)
                    rope_evac(v3, d3[:, h0:h0 + nh, :], nh, 64, 16, "cq" if kind == "qc" else "i", t,
                              SC if kind == "qc" else 1.0, [z], [hd5])
                    if h0 + nh == 10:
                        fm_store(hd5, hd5.ap, 5, qcT if kind == "qc" else kcT, t)
                else:
                    raise AssertionError(kind)

            NSEG = int(os.environ.get("MK_NSEG", len(SEGS)))
            EPI = os.environ.get("MK_EPI", "1") == "1"
            EPK = os.environ.get("MK_EPK")
            EPK = set(EPK.split(",")) if EPK else None
            if NSEG > 0:
                load_w(0)
            for si in range(NSEG):
                if si + 1 < NSEG:
                    load_w(si + 1)
                wb = wbuf[si % 2]
                c0 = CHUNKS[SEGS[si][0]][0]
                dma("sp", uTs[0].ap, uT_d[0], writes=[uTs[0]])
                for t in range(NT):
                    u = uTs[t % 3]
                    if t + 1 < NT:
                        dma("sp", uTs[(t + 1) % 3].ap, uT_d[t + 1], writes=[uTs[(t + 1) % 3]])
                    zl = []
                    for ci in SEGS[si]:
                        col0, width, kind, meta = CHUNKS[ci]
                        z = nxt("pz", pz)
                        for kc in range(16):
                            op("pe", lambda: PE.matmul(z.ap[:, 0:width], lhsT=u.ap[:, kc, :],
                                                       rhs=wb.ap[:, kc, col0 - c0:col0 - c0 + width],
                                                       start=(kc == 0), stop=(kc == 15)), [u, wb], [z])
                        zl.append((ci, z))
                    if CHUNKS[SEGS[si][0]][2] == "qi":
                        zl = zl[::-1]
                    for (ci, z) in zl:
                        if EPI and (EPK is None or CHUNKS[ci][2] in EPK):
                            epilogue(ci, z, t)
            sch.barrier()
            maybe_stop("A2")

        with ExitStack() as es:
            GQ = 4
            kiTs = sb(es, "kiTs", [128, S], BF16)
            dvl = sb(es, "dvl", [128, 4, 512], F32)
            dng = sb(es, "dng", [128, 4, 512], F32)
            pw2 = sb(es, "pw2", [128, NIT], F32)
            Sc = [sb(es, "Sc", [128, S], F32) for _ in range(GQ)]
            rl = [sb(es, "rl", [128, 512], F32) for _ in range(4)]
            qit = [sb(es, "qit", [128, 4, 128], BF16) for _ in range(GQ)]
            wit = [sb(es, "wit", [128, 16], F32) for _ in range(GQ)]
            sel = [sb(es, "sel", [128, S], BF16) for _ in range(2)]
            junkb = [sb(es, "junkb", [128, S], BF16) for _ in range(GQ)]
            nms = [sb(es, "nms", [128, 4 * NQB, 128], BF16) for _ in range(2)]
            bs = [sb(es, "bs", [128, 8], F32) for _ in range(GQ)]
            halves = [sb(es, "halves", [128, NIT], F32) for _ in range(GQ)]
            psI = [ps(es, "psI", [128, 512], F32) for _ in range(4)]
            pst = [ps(es, "pst", [128, 8, 128], BF16) for _ in range(2)]
            nI = [0, 0, 0]
            dma("sp", kiTs.ap, kiT[0], writes=[kiTs])
            dma("sp", dvl.ap, dvalid_d.rearrange("j p q -> p j q"), writes=[dvl])
            dma("sp", dng.ap, dneg_d.rearrange("j p q -> p j q"), writes=[dng])
            dma("sp", pw2.ap, pow2_d, writes=[pw2])
            for qb in range(NQB):
                nkb = qb + 1
                Lk = 512 * nkb
                tiles = list(range(GQ))
                for qs in tiles:
                    it = 4 * qb + qs
                    dma("sp", qit[qs].ap, qiT.rearrange("n p s -> p n s")[:, :, it * 128:(it + 1) * 128], writes=[qit[qs]])
                    dma("sp", wit[qs].ap, WI[it * 128:(it + 1) * 128, :], writes=[wit[qs]])
                for kb in range(nkb):
                    for h in range(8):
                        for qs in tiles:
                            q_, w_, sc = qit[qs], wit[qs], Sc[qs]
                            pI = psI[nI[0] % 4]
                            nI[0] += 1
                            r_ = rl[nI[1] % 4]
                            nI[1] += 1
                            lo_p = 64 * (h % 2)
                            op("pe", lambda: PE.matmul(pI.ap, lhsT=q_.ap[lo_p:lo_p + 64, h // 2, :],
                                                       rhs=kiTs.ap[lo_p:lo_p + 64, kb * 512:(kb + 1) * 512],
                                                       start=True, stop=True), [q_, kiTs], [pI])
                            op("act", lambda: A.activation(out=r_.ap, in_=pI.ap, func=AF.Relu, scale=w_.ap[:, h:h + 1]),
                               [pI, w_], [r_])
                            scs = sc.ap[:, kb * 512:(kb + 1) * 512]
                            if h == 0:
                                op("dve", lambda: V.tensor_scalar(out=scs, in0=r_.ap, scalar1=w_.ap[:, 8:9], scalar2=None,
                                                                  op0=ALU.mult), [r_, w_], [sc])
                            else:
                                op("dve", lambda: V.scalar_tensor_tensor(out=scs, in0=r_.ap, scalar=w_.ap[:, 8 + h:9 + h],
                                                                         in1=scs, op0=ALU.mult, op1=ALU.add),
                                   [r_, w_, sc], [sc])
                bis = [qs for qs in tiles if 4 * qb + qs >= 2]

                def each(fn, lst=tiles):
                    for qs in lst:
                        fn(qs)

                def dgv(qs):
                    return Sc[qs].ap[:, qb * 512:(qb + 1) * 512]

                each(lambda qs: op("dve", lambda: V.tensor_tensor(out=dgv(qs), in0=dgv(qs), in1=dvl.ap[:, qs, :],
                                                                   op=ALU.mult), [Sc[qs], dvl], [Sc[qs]]))
                each(lambda qs: op("dve", lambda: V.tensor_reduce(out=bs[qs].ap[:, 0:1], in_=Sc[qs].ap[:, 0:Lk], axis=AX.X,
                                                                   op=ALU.max), [Sc[qs]], [bs[qs]]), bis)
                each(lambda qs: op("dve", lambda: V.tensor_reduce(out=bs[qs].ap[:, 1:2], in_=Sc[qs].ap[:, 0:Lk], axis=AX.X,
                                                                   op=ALU.min), [Sc[qs]], [bs[qs]]), bis)
                each(lambda qs: op("dve", lambda: V.tensor_tensor(out=dgv(qs), in0=dgv(qs), in1=dng.ap[:, qs, :],
                                                                   op=ALU.add), [Sc[qs], dng], [Sc[qs]]))
                each(lambda qs: op("dve", lambda: V.tensor_tensor(out=bs[qs].ap[:, 2:3], in0=bs[qs].ap[:, 0:1],
                                                                   in1=bs[qs].ap[:, 1:2], op=ALU.subtract),
                                   [bs[qs]], [bs[qs]]), bis)
                each(lambda qs: op("dve", lambda: V.tensor_scalar(out=halves[qs].ap, in0=pw2.ap, scalar1=bs[qs].ap[:, 2:3],
                                                                   scalar2=None, op0=ALU.mult), [pw2, bs[qs]], [halves[qs]]),
                     bis)
                for k in range(NIT):
                    each(lambda qs: op("dve", lambda: V.tensor_tensor(out=bs[qs].ap[:, 3:4], in0=bs[qs].ap[:, 1:2],
                                                                       in1=halves[qs].ap[:, k:k + 1], op=ALU.add),
                                       [bs[qs], halves[qs]], [bs[qs]]), bis)
                    each(lambda qs: op("dve", lambda: V.tensor_scalar(out=junkb[qs].ap[:, 0:Lk], in0=Sc[qs].ap[:, 0:Lk],
                                                                       scalar1=bs[qs].ap[:, 3:4], scalar2=None,
                                                                       op0=ALU.is_ge, op1=ALU.add,
                                                                       accum_out=bs[qs].ap[:, 4:5]),
                                       [Sc[qs], bs[qs]], [junkb[qs], bs[qs]]), bis)
                    each(lambda qs: op("dve", lambda: V.tensor_scalar(out=bs[qs].ap[:, 5:6], in0=bs[qs].ap[:, 4:5],
                                                                       scalar1=255.5, scalar2=halves[qs].ap[:, k:k + 1],
                                                                       op0=ALU.is_ge, op1=ALU.mult),
                                       [bs[qs], halves[qs]], [bs[qs]]), bis)
                    each(lambda qs: op("dve", lambda: V.tensor_tensor(out=bs[qs].ap[:, 1:2], in0=bs[qs].ap[:, 1:2],
                                                                       in1=bs[qs].ap[:, 5:6], op=ALU.add),
                                       [bs[qs]], [bs[qs]]), bis)
                for qs in tiles:
                    if qs not in bis:
                        op("dve", lambda: V.memset(bs[qs].ap[:, 1:2], -1e29), [], [bs[qs]])
                for qs in tiles:
                    sl_ = sel[qs % 2]
                    op("dve", lambda: V.tensor_scalar(out=sl_.ap[:, 0:Lk], in0=Sc[qs].ap[:, 0:Lk], scalar1=bs[qs].ap[:, 1:2],
                                                      scalar2=None, op0=ALU.is_ge), [Sc[qs], bs[qs]], [sl_])
                    nm_ = nms[qs % 2]
                    for g in range(nkb):
                        pt = pst[nI[2] % 2]
                        nI[2] += 1
                        for k in range(4):
                            kt = 4 * g + k
                            op("pe", lambda: PE.transpose(out=pt.ap[:, k, :], in_=sl_.ap[:, kt * 128:(kt + 1) * 128],
                                                          identity=ident.ap), [sl_, ident], [pt])
                        op("act", lambda: A.activation(out=nm_.ap[:, 4 * g:4 * g + 4, :], in_=pt.ap[:, 0:4, :],
                                                       func=AF.Identity, bias=float(NEGM), scale=float(-NEGM)), [pt], [nm_])
                    dma("sp", NM[qb][:, 0:4 * nkb, qs * 128:(qs + 1) * 128], nm_.ap[:, 0:4 * nkb, :], reads=[nm_])
            sch.barrier()
            maybe_stop("B1")

        def attention(mixer):
            with ExitStack() as es:
                if mixer == "a":
                    H, width, col0 = 6, 768, 0
                    Vd, kTd, nkt_extra = Va, kaTn, True
                elif mixer == "b":
                    H, width, col0 = 5, 640, 768
                    Vd, kTd, nkt_extra = Vb, kbT, False
                else:
                    H, width, col0 = 5, 640, 1408
                    Vd, kTd, nkt_extra = Vc, kcT, False
                kTs = sb(es, "kTs", [128, H, S], BF16)
                Vs = sb(es, "Vs", [128, NT, H * VW], BF16)
                for h in range(H):
                    dma("sp", kTs.ap[:, h, :], kTd[h], writes=[kTs])
                for t0 in range(0, NT, 8):
                    t1 = min(NT, t0 + 8)
                    dma("sp", Vs.ap[:, t0:t1, :], Vd[t0 * 128:t1 * 128, :].rearrange("(t p) c -> p t c", p=128),
                        writes=[Vs])
                if mixer == "a":
                    krs = sb(es, "krs", [128, S], BF16)
                    dma("sp", krs.ap, kropeT[0], writes=[krs])
                    qrb = [sb(es, "qrb", [128, 3, 512], BF16) for _ in range(2)]
                if mixer == "b":
                    nmT = [sb(es, "nmT", [128, 4 * NQB, 512], BF16) for _ in range(1)]
                if mixer == "c":
                    subg = sb(es, "subg", [128, 128], F32)
                    lamt = sb(es, "lamt", [128, 4, 64], F32)
                    lams = sb(es, "lams", [128, 8], F32)
                    dma("sp", subg.ap, bcast_rows(subln_g_d[L:L + 1, :], 128), writes=[subg])
                    for i, n in enumerate(("lam_q1", "lam_k1", "lam_q2", "lam_k2")):
                        dma("sp", lamt.ap[:, i, :], bcast_rows(lam_d[n][L:L + 1, :], 64), writes=[lamt])
                    op("dve", lambda: V.tensor_scalar(out=subg.ap, in0=subg.ap, scalar1=float(1.0 - lam_init),
                                                      scalar2=None, op0=ALU.mult), [subg], [subg])
                    for j in range(2):
                        op("dve", lambda: V.tensor_tensor(out=lamt.ap[:, 2 * j, :], in0=lamt.ap[:, 2 * j, :],
                                                          in1=lamt.ap[:, 2 * j + 1, :], op=ALU.mult), [lamt], [lamt])
                        op("dve", lambda: V.reduce_sum(out=lams.ap[:, j:j + 1], in_=lamt.ap[:, 2 * j, :], axis=AX.X),
                           [lamt], [lams])
                    op("act", lambda: A.activation(out=lams.ap[:, 2:4], in_=lams.ap[:, 0:2], func=AF.Exp), [lams], [lams])
                    op("dve", lambda: V.tensor_tensor(out=lams.ap[:, 4:5], in0=lams.ap[:, 3:4], in1=lams.ap[:, 2:3],
                                                      op=ALU.subtract), [lams], [lams])
                    op("dve", lambda: V.tensor_scalar(out=lams.ap[:, 5:6], in0=lams.ap[:, 4:5], scalar1=float(-lam_init),
                                                      scalar2=None, op0=ALU.add), [lams], [lams])
                    t1s = [sb(es, "t1s", [128, 4, 128], F32) for _ in range(1)]
                    osb = [sb(es, "osb", [128, 128], F32) for _ in range(4)]
                    junkc = [sb(es, "junkc", [128, 128], F32) for _ in range(4)]
                qnb = [sb(es, "qnb", [128, H, 512], BF16) for _ in range(2)]
                sgb = [sb(es, "sgb", [128, 4, width], F32) for _ in range(2)]
                mxo = [sb(es, "mxo", [128, 4, width], BF16) for _ in range(2)]
                PTs = [sb(es, "PT", [128, 512], BF16) for _ in range(4)]
                rd = [sb(es, "rd", [128, 4], F32) for _ in range(4)]
                pS = [ps(es, "pS", [128, 512], F32) for _ in range(3)]
                pO = [[ps(es, "pO", [128, 512], F32) for _ in range(2)] for _ in range(2)]
                n = {"S": 0, "P": 0, "O": 0, "rd": 0, "osb": 0}

                def units_of(h):
                    if mixer == "c":
                        return [(h, 1), (h, 2)]
                    return [(h, 0)]

                for qb in range(NQB):
                    qq = qnb[qb % 2]
                    qsl = slice(qb * 512, (qb + 1) * 512)
                    qsrc = {"a": qaTn, "b": qbT, "c": qcT}[mixer]
                    dma("sp", qq.ap, qsrc.rearrange("n p s -> p n s")[:, :, qsl], writes=[qq])
                    if mixer == "a":
                        qr_ = qrb[qb % 2]
                        dma("sp", qr_.ap, qaTr.rearrange("n p s -> p n s")[:, :, qsl], writes=[qr_])
                    if mixer == "b":
                        nm_ = nmT[0]
                        dma("sp", nm_.ap[:, 0:4 * (qb + 1), :], NM[qb][:, 0:4 * (qb + 1), :], writes=[nm_])
                    sg_ = sgb[qb % 2]
                    dma("sp", sg_.ap, SG[qsl, col0:col0 + width].rearrange("(a p) c -> p a c", p=128), writes=[sg_])
                    mo = mxo[qb % 2]
                    nkt = 4 * (qb + 1)
                    for h in range(H):
                        for (hh, u) in units_of(h):
                            po = pO[n["O"] % 2]
                            n["O"] += 1
                            for kt in range(nkt):
                                j = kt - 4 * qb
                                c_lo = 128 * j if j > 0 else 0
                                cs_ = slice(c_lo, 512)
                                ksl = slice(kt * 128, (kt + 1) * 128)
                                s_ = pS[n["S"] % 3]
                                n["S"] += 1
                                need_mask = (mixer == "b") or (j >= 0)
                                if mixer == "a":
                                    lo_p = 64 * (h % 2)
                                    op("pe", lambda: PE.matmul(s_.ap[:, cs_], lhsT=kTs.ap[:, h, ksl], rhs=qq.ap[:, h, cs_],
                                                               start=True, stop=False), [kTs, qq], [s_])
                                    op("pe", lambda: PE.matmul(s_.ap[:, cs_], lhsT=krs.ap[lo_p:lo_p + 64, ksl],
                                                               rhs=qr_.ap[lo_p:lo_p + 64, h // 2, cs_],
                                                               start=False, stop=not need_mask), [krs, qr_], [s_])
                                elif mixer == "b":
                                    op("pe", lambda: PE.matmul(s_.ap[:, cs_], lhsT=kTs.ap[:, h, ksl], rhs=qq.ap[:, h, cs_],
                                                               start=True, stop=False), [kTs, qq], [s_])
                                else:
                                    lo_p = 64 * (u - 1)
                                    op("pe", lambda: PE.matmul(s_.ap[:, cs_], lhsT=kTs.ap[lo_p:lo_p + 64, h, ksl],
                                                               rhs=qq.ap[lo_p:lo_p + 64, h, cs_],
                                                               start=True, stop=not need_mask), [kTs, qq], [s_])
                                if need_mask:
                                    if mixer == "b":
                                        op("pe", lambda: PE.matmul(s_.ap[:, cs_], lhsT=ident.ap, rhs=nm_.ap[:, kt, cs_],
                                                                   start=False, stop=True), [ident, nm_], [s_])
                                    else:
                                        op("pe", lambda: PE.matmul(s_.ap[:, cs_], lhsT=ident.ap, rhs=dmask.ap[:, j, cs_],
                                                                   start=False, stop=True), [ident, dmask], [s_])
                                P_ = PTs[n["P"] % 4]
                                n["P"] += 1
                                op("act", lambda: A.activation(out=P_.ap[:, cs_], in_=s_.ap[:, cs_], func=AF.Exp),
                                   [s_], [P_])
                                for qs in range(max(j, 0), 4):
                                    bank = po[qs // 2]
                                    oc = (qs % 2) * 256
                                    first = (kt == 0 and qs % 2 == 0)
                                    op("pe", lambda: PE.matmul(bank.ap[:, oc:oc + 129], lhsT=P_.ap[:, qs * 128:(qs + 1) * 128],
                                                               rhs=Vs.ap[:, kt, h * VW:h * VW + 129],
                                                               start=first, stop=(kt == 4 * qb + qs),
                                                               skip_group_check=True), [P_, Vs], [bank])
                            QS = range(4)
                            bk = [po[qs // 2] for qs in QS]
                            ocs = [(qs % 2) * 256 for qs in QS]
                            rr = [rd[qs] for qs in QS]
                            dsts = [mo.ap[:, qs, h * 128:(h + 1) * 128] for qs in QS]
                            gsls = [sg_.ap[:, qs, h * 128:(h + 1) * 128] for qs in QS]
                            for qs in QS:
                                op("dve", lambda: V.reciprocal(out=rr[qs].ap[:, 0:1], in_=bk[qs].ap[:, ocs[qs] + 128:ocs[qs] + 129]),
                                   [bk[qs]], [rr[qs]])
                            if mixer != "c":
                                for qs in QS:
                                    op("dve", lambda: V.scalar_tensor_tensor(out=dsts[qs], in0=bk[qs].ap[:, ocs[qs]:ocs[qs] + 128],
                                                                             scalar=rr[qs].ap[:, 0:1], in1=gsls[qs],
                                                                             op0=ALU.mult, op1=ALU.mult),
                                       [bk[qs], rr[qs], sg_], [mo])
                            elif u == 1:
                                t1_ = t1s[0]
                                for qs in QS:
                                    op("act", lambda: A.mul(out=t1_.ap[:, qs, :], in_=bk[qs].ap[:, ocs[qs]:ocs[qs] + 128],
                                                            mul=rr[qs].ap[:, 0:1]), [bk[qs], rr[qs]], [t1_])
                            else:
                                t1_ = t1s[0]
                                oo = [osb[qs] for qs in QS]
                                for qs in QS:
                                    op("dve", lambda: V.tensor_tensor(out=rr[qs].ap[:, 1:2], in0=rr[qs].ap[:, 0:1],
                                                                      in1=lams.ap[:, 5:6], op=ALU.mult), [rr[qs], lams], [rr[qs]])
                                for qs in QS:
                                    op("dve", lambda: V.scalar_tensor_tensor(out=oo[qs].ap, in0=bk[qs].ap[:, ocs[qs]:ocs[qs] + 128],
                                                                             scalar=rr[qs].ap[:, 1:2], in1=t1_.ap[:, qs, :],
                                                                             op0=ALU.mult, op1=ALU.add),
                                       [bk[qs], rr[qs], t1_], [oo[qs]])
                                for qs in QS:
                                    op("act", lambda: A.activation(out=junkc[qs].ap, in_=oo[qs].ap, func=AF.Square,
                                                                   accum_out=rr[qs].ap[:, 2:3]), [oo[qs]], [junkc[qs], rr[qs]])
                                for qs in QS:
                                    op("act", lambda: A.activation(out=rr[qs].ap[:, 3:4], in_=rr[qs].ap[:, 2:3], func=AF.Sqrt,
                                                                   bias=EPS, scale=1.0 / 128.0), [rr[qs]], [rr[qs]])
                                for qs in QS:
                                    op("dve", lambda: V.reciprocal(out=rr[qs].ap[:, 2:3], in_=rr[qs].ap[:, 3:4]), [rr[qs]], [rr[qs]])
                                for qs in QS:
                                    op("dve", lambda: V.scalar_tensor_tensor(out=oo[qs].ap, in0=oo[qs].ap, scalar=rr[qs].ap[:, 2:3],
                                                                             in1=subg.ap, op0=ALU.mult, op1=ALU.mult),
                                       [oo[qs], rr[qs], subg], [oo[qs]])
                                for qs in QS:
                                    op("dve", lambda: V.tensor_tensor(out=dsts[qs], in0=oo[qs].ap, in1=gsls[qs], op=ALU.mult),
                                       [oo[qs], sg_], [mo])
                    dma("sp", MX[qsl, col0:col0 + width].rearrange("(a p) c -> p a c", p=128), mo.ap, reads=[mo])
                sch.barrier()
                maybe_stop("B" + mixer)

        attention("a")
        attention("b")
        attention("c")

        with ExitStack() as es:
            wo = sb(es, "wo", [128, 16, D], BF16)
            for c in range(4):
                for kc0 in range(0, 16, 8):
                    dma("pool", wo.ap[:, kc0:kc0 + 8, c * 512:(c + 1) * 512],
                        w_o_d[L].rearrange("(c p) n -> p c n", p=128)[:, kc0:kc0 + 8, c * 512:(c + 1) * 512], writes=[wo])
            mxt = [sb(es, "mxt", [128, D], BF16) for _ in range(2)]
            mT = [sb(es, "mT", [128, 16, 128], BF16) for _ in range(2)]
            hin = [sb(es, "hin", [128, D], F32) for _ in range(2)]
            h1 = [sb(es, "h1", [128, D], F32) for _ in range(2)]
            h1b = [sb(es, "h1b", [128, D], BF16) for _ in range(2)]
            h1T = [sb(es, "h1T", [128, 16, 128], BF16) for _ in range(2)]
            pT = [ps(es, "pT", [128, 8, 128], BF16) for _ in range(4)]
            pz = [ps(es, "pz", [128, 512], F32) for _ in range(4)]
            npT = 0
            for t in range(NT):
                i = t % 2
                rs = slice(t * 128, (t + 1) * 128)
                dma("sp", mxt[i].ap, MX[rs, :], writes=[mxt[i]])
                dma("sp", hin[i].ap, h_src[rs, :], writes=[hin[i]])
                for g in range(2):
                    pt = pT[npT % 4]
                    npT += 1
                    for c in range(8):
                        cc = 8 * g + c
                        op("pe", lambda: PE.transpose(out=pt.ap[:, c, :], in_=mxt[i].ap[:, cc * 128:(cc + 1) * 128],
                                                      identity=ident.ap), [mxt[i], ident], [pt])
                    op("act", lambda: A.copy(out=mT[i].ap[:, 8 * g:8 * g + 8, :], in_=pt.ap), [pt], [mT[i]])
                for c in range(4):
                    z = pz[c]
                    for kc in range(16):
                        op("pe", lambda: PE.matmul(z.ap, lhsT=mT[i].ap[:, kc, :], rhs=wo.ap[:, kc, c * 512:(c + 1) * 512],
                                                   start=(kc == 0), stop=(kc == 15)), [mT[i], wo], [z])
                    op("dve", lambda: V.tensor_tensor(out=h1[i].ap[:, c * 512:(c + 1) * 512], in0=z.ap,
                                                      in1=hin[i].ap[:, c * 512:(c + 1) * 512], op=ALU.add),
                       [z, hin[i]], [h1[i]])
                dma("sp", hbuf[rs, :], h1[i].ap, reads=[h1[i]])
                op("dve", lambda: V.tensor_copy(out=h1b[i].ap, in_=h1[i].ap), [h1[i]], [h1b[i]])
                for g in range(2):
                    pt = pT[npT % 4]
                    npT += 1
                    for c in range(8):
                        cc = 8 * g + c
                        op("pe", lambda: PE.transpose(out=pt.ap[:, c, :], in_=h1b[i].ap[:, cc * 128:(cc + 1) * 128],
                                                      identity=ident.ap), [h1b[i], ident], [pt])
                    op("act", lambda: A.copy(out=h1T[i].ap[:, 8 * g:8 * g + 8, :], in_=pt.ap), [pt], [h1T[i]])
                dma("sp", uT_d[t], h1T[i].ap, reads=[h1T[i]])
            sch.barrier()
            maybe_stop("C1")

        with ExitStack() as es:
            wpg = sb(es, "wpg", [128, 16, D], BF16)
            wple = sb(es, "wple", [128, 2, D], BF16)
            for c in range(4):
                for kc0 in range(0, 16, 8):
                    dma("pool", wpg.ap[:, kc0:kc0 + 8, c * 512:(c + 1) * 512],
                        w_pg_d[L].rearrange("(c p) n -> p c n", p=128)[:, kc0:kc0 + 8, c * 512:(c + 1) * 512], writes=[wpg])
            dma("pool", wple.ap, w_ple_d[L].rearrange("(c p) n -> p c n", p=128), writes=[wple])
            if last:
                fgb = sb(es, "fgb", [128, D], F32)
                dma("sp", fgb.ap, bcast_rows(final_g_d[0:1, :], D), writes=[fgb])
                junk = sb(es, "junk", [128, D], BF16)
                st = [sb(es, "st", [128, 2], F32) for _ in range(2)]
            h1 = [sb(es, "h1", [128, D], F32) for _ in range(2)]
            h1T = [sb(es, "h1T", [128, 16, 128], BF16) for _ in range(2)]
            pt_ = [sb(es, "pt_", [128, PLE], F32) for _ in range(2)]
            pb = [sb(es, "pb", [128, PLE], BF16) for _ in range(2)]
            pTs = [sb(es, "pTs", [128, 2, 128], BF16) for _ in range(2)]
            sg = [sb(es, "sg", [128, 512], F32) for _ in range(2)]
            tm = [sb(es, "tm", [128, 512], F32) for _ in range(2)]
            h2 = [sb(es, "h2", [128, D], F32) for _ in range(2)]
            pz = [ps(es, "pz", [128, 512], F32) for _ in range(6)]
            pT = [ps(es, "pT", [128, 8, 128], BF16) for _ in range(2)]
            nz = 0
            ns = 0
            for t in range(NT):
                i = t % 2
                rs = slice(t * 128, (t + 1) * 128)
                dma("sp", h1[i].ap, hbuf[rs, :], writes=[h1[i]])
                dma("sp", h1T[i].ap, uT_d[t], writes=[h1T[i]])
                dma("sp", pt_[i].ap, p_d[L][rs, :], writes=[pt_[i]])
                op("dve", lambda: V.tensor_copy(out=pb[i].ap, in_=pt_[i].ap), [pt_[i]], [pb[i]])
                for c in range(2):
                    op("pe", lambda: PE.transpose(out=pT[i].ap[:, c, :], in_=pb[i].ap[:, c * 128:(c + 1) * 128],
                                                  identity=ident.ap), [pb[i], ident], [pT[i]])
                op("act", lambda: A.copy(out=pTs[i].ap, in_=pT[i].ap[:, 0:2, :]), [pT[i]], [pTs[i]])
                for c in range(4):
                    cs_ = slice(c * 512, (c + 1) * 512)
                    zg = pz[nz % 6]
                    nz += 1
                    for kc in range(16):
                        op("pe", lambda: PE.matmul(zg.ap, lhsT=h1T[i].ap[:, kc, :], rhs=wpg.ap[:, kc, cs_],
                                                   start=(kc == 0), stop=(kc == 15)), [h1T[i], wpg], [zg])
                    zp = pz[nz % 6]
                    nz += 1
                    for kc in range(2):
                        op("pe", lambda: PE.matmul(zp.ap, lhsT=pTs[i].ap[:, kc, :], rhs=wple.ap[:, kc, cs_],
                                                   start=(kc == 0), stop=(kc == 1)), [pTs[i], wple], [zp])
                    s_ = sg[ns % 2]
                    t_ = tm[ns % 2]
                    ns += 1
                    op("act", lambda: A.activation(out=s_.ap, in_=zg.ap, func=AF.Sigmoid), [zg], [s_])
                    op("dve", lambda: V.tensor_tensor(out=t_.ap, in0=zp.ap, in1=s_.ap, op=ALU.mult), [zp, s_], [t_])
                    op("dve", lambda: V.tensor_tensor(out=h2[i].ap[:, cs_], in0=t_.ap, in1=h1[i].ap[:, cs_], op=ALU.add),
                       [t_, h1[i]], [h2[i]])
                if not last:
                    dma("sp", hbuf[rs, :], h2[i].ap, reads=[h2[i]])
                else:
                    rms_rstd(es, h2[i].ap, D, junk, st[i], [h2[i]])
                    op("dve", lambda: V.scalar_tensor_tensor(out=h1[i].ap, in0=h2[i].ap, scalar=st[i].ap[:, 0:1],
                                                             in1=fgb.ap, op0=ALU.mult, op1=ALU.mult),
                       [h2[i], st[i], fgb], [h1[i]])
                    dma("sp", y_d[rs, :], h1[i].ap, reads=[h1[i]])
            sch.barrier()
            maybe_stop("C2")

    except _StopBuild:
        return nc, sch
    top.close()
    return nc, sch


_CACHE = {}


def kernel(x, p, positions, w_in, w_uq, w_ukv, w_o, norm_g, q_norm_g, kv_norm_g,
           lam_q1, lam_k1, lam_q2, lam_k2, subln_g, w_ple, w_pg, final_g):
    x = np.asarray(x)
    B, S, _ = x.shape
    DEPTH = int(np.asarray(w_in).shape[0])
    key = (S, DEPTH)
    if key not in _CACHE:
        _CACHE[key] = build_program(S, DEPTH)[0]
    nc = _CACHE[key]
    consts = host_consts()
    f32 = lambda a: np.ascontiguousarray(np.asarray(a), dtype=np.float32)
    shared = {
        "w_in": f32(w_in), "w_uq": f32(w_uq), "w_ukv": f32(w_ukv), "w_o": f32(w_o),
        "norm_g": f32(norm_g), "q_norm_g": f32(q_norm_g), "kv_norm_g": f32(kv_norm_g),
        "lam_q1": f32(lam_q1), "lam_k1": f32(lam_k1), "lam_q2": f32(lam_q2), "lam_k2": f32(lam_k2),
        "subln_g": f32(subln_g), "w_ple": f32(w_ple), "w_pg": f32(w_pg),
        "final_g": f32(final_g).reshape(1, D),
    }
    shared.update(consts)
    p = np.asarray(p)
    positions = np.asarray(positions)
    in_maps = []
    for c in range(8):
        b = c % B
        m = dict(shared)
        m["x"] = f32(x[b])
        m["p"] = f32(p[:, b])
        m["positions"] = np.ascontiguousarray(positions[b].astype(np.int32).reshape(S // 128, 128).T)
        in_maps.append(m)
    res = run_bass_kernel_spmd(nc, in_maps, core_ids=list(range(8)))
    out = np.stack([np.asarray(res.results[b]["y"], dtype=np.float32) for b in range(B)], axis=0)
    return out
```

```python
import math
import os
from contextlib import ExitStack

import numpy as np
import ml_dtypes
import concourse.bass as bass
import concourse.mybir as mybir
from concourse.bass_utils import run_bass_kernel_spmd

F32 = mybir.dt.float32
BF16 = mybir.dt.bfloat16
I32 = mybir.dt.int32
ALU = mybir.AluOpType
AF = mybir.ActivationFunctionType
AX = mybir.AxisListType

D = 2048
PLE = 256
D_IN = 7176
EPS = 1e-6
THETA = 500000.0
NIT = 22
NEGM = -30000.0
SA = 192.0 ** -0.5
SB = 128.0 ** -0.5
SC = 64.0 ** -0.5
VW = 132


class Buf:
    __slots__ = ("name", "writers", "readers", "ap", "psum")

    def __init__(self, name="", ap=None, psum=False):
        self.psum = psum
        self.name = name
        self.writers = {}
        self.readers = {}
        self.ap = ap


class Sched:
    NDMA = 8

    def __init__(self, nc):
        self.nc = nc
        self.eng = {"pe": nc.tensor, "act": nc.scalar, "dve": nc.vector,
                    "pool": nc.gpsimd, "sp": nc.sync}
        self.sems = {}
        self.cnt = {}
        for e in ("pe", "act", "dve", "pool"):
            self.sems[e] = nc.alloc_semaphore("s_" + e)
            self.cnt[e] = 0
        self.dq = {}
        for q in ("sp", "pool"):
            for i in range(self.NDMA):
                k = "d_%s%d" % (q, i)
                self.sems[k] = nc.alloc_semaphore(k)
                self.cnt[k] = 0
            self.dq[q] = 0
        self.seen = {e: {} for e in self.eng}
        self.nins = 0
        self.nwait = 0

    def _wait(self, e, evs):
        best = {}
        for (k, v) in evs:
            if v > best.get(k, 0):
                best[k] = v
        seen = self.seen[e]
        for k, v in best.items():
            if k == "pe" and e == "pe":
                continue
            if seen.get(k, 0) < v:
                self.eng[e].wait_ge(self.sems[k], v)
                seen[k] = v
                self.nwait += 1

    @staticmethod
    def _deps(reads, writes, e=None):
        evs = []
        for b in reads:
            evs.extend(b.writers.items())
            if b.psum:
                evs.extend((k, v) for k, v in b.readers.items() if k != e)
        for b in writes:
            evs.extend(b.writers.items())
            evs.extend(b.readers.items())
        return evs

    @staticmethod
    def _commit(ev, reads, writes):
        k, v = ev
        for b in reads:
            if b.readers.get(k, 0) < v:
                b.readers[k] = v
        for b in writes:
            b.writers = {k: v}
            b.readers = {}

    def op(self, e, ins_fn, reads=(), writes=()):
        self._wait(e, self._deps(reads, writes, e))
        ins = ins_fn()
        self.cnt[e] += 1
        ins.then_inc(self.sems[e], 1)
        self._commit((e, self.cnt[e]), reads, writes)
        self.nins += 1

    def dma(self, q, out, in_, reads=(), writes=()):
        i = self.dq[q]
        self.dq[q] = (i + 1) % self.NDMA
        k = "d_%s%d" % (q, i)
        evs = self._deps(reads, writes)
        if self.cnt[k] > 0:
            evs.append((k, self.cnt[k]))
        self._wait(q, evs)
        ins = self.eng[q].dma_start(out=out, in_=in_)
        self.cnt[k] += 16
        ins.then_inc(self.sems[k], 16)
        self._commit((k, self.cnt[k]), reads, writes)
        self.nins += 1

    def barrier(self):
        evs = [(k, v) for k, v in self.cnt.items() if v > 0]
        for e in self.eng:
            self._wait(e, list(evs))


def host_consts():
    c = {}
    c["ident"] = np.eye(128, dtype=np.float32).astype(ml_dtypes.bfloat16)
    invf = np.zeros((128, 56), np.float32)
    off = 0
    for n_rot in (64, 32, 16):
        half = n_rot // 2
        f = 1.0 / (np.float32(THETA) ** (np.arange(half, dtype=np.float32) * np.float32(2.0 / n_rot)))
        invf[:, off:off + half] = f.astype(np.float32)[None, :]
        off += half
    c["invf"] = invf
    dm = np.zeros((4, 128, 512), np.float32)
    for j in range(4):
        kk = 128 * j + np.arange(128)[:, None]
        qq = np.arange(512)[None, :]
        dm[j] = np.where((kk // 64) <= (qq // 64), 0.0, NEGM)
    c["dmask"] = dm.astype(ml_dtypes.bfloat16)
    dv = np.zeros((4, 128, 512), np.float32)
    for qs in range(4):
        qq = 128 * qs + np.arange(128)[:, None]
        kk = np.arange(512)[None, :]
        dv[qs] = ((kk // 64) <= (qq // 64)).astype(np.float32)
    c["dvalid"] = dv
    c["dneg"] = ((dv - 1.0) * 1e30).astype(np.float32)
    c["pow2"] = np.tile((0.5 ** np.arange(1, NIT + 1, dtype=np.float64)).astype(np.float32)[None, :], (128, 1))
    return c


CHUNKS = [
    (0, 384, "cq", None),
    (384, 320, "ckv", None),
    (704, 512, "gate", 0),
    (1216, 256, "gate", 512),
    (1472, 512, "qb", (0, 4)),
    (1984, 128, "qb", (4, 1)),
    (2112, 512, "kb", (0, 4)),
    (2624, 128, "kb", (4, 1)),
    (2752, 512, "vb", (0, 4)),
    (3264, 128, "vb", (4, 1)),
    (3392, 512, "qi", None),
    (3904, 72, "kiw", None),
    (3976, 512, "gate", 768),
    (4488, 128, "gate", 1280),
    (4616, 512, "qc", (0, 8)),
    (5128, 128, "qc", (8, 2)),
    (5256, 512, "kc", (0, 8)),
    (5768, 128, "kc", (8, 2)),
    (5896, 512, "vc", (0, 4)),
    (6408, 128, "vc", (4, 1)),
    (6536, 512, "gate", 1408),
    (7048, 128, "gate", 1920),
]
SEGS = [(2 * i, 2 * i + 1) for i in range(11)]


class _StopBuild(Exception):
    pass


def build_program(S, DEPTH, debug_layers=None):
    assert S % 512 == 0
    STOP = os.environ.get("MK_STOP", "")

    def maybe_stop(tag):
        if STOP == tag:
            raise _StopBuild()
    NT = S // 128
    NQB = S // 512
    nc = bass.Bass("TRN2", target_bir_lowering=False)
    sch = Sched(nc)
    uid = [0]

    def din(name, shape, dt):
        return nc.dram_tensor(name, list(shape), dt, kind="ExternalInput").ap()

    def dscr(name, shape, dt):
        return nc.dram_tensor(name, list(shape), dt).ap()

    x_d = din("x", [S, D], F32)
    p_d = din("p", [DEPTH, S, PLE], F32)
    pos_d = din("positions", [128, S // 128], I32)
    w_in_d = din("w_in", [DEPTH, D, D_IN], F32)
    w_uq_d = din("w_uq", [DEPTH, 384, 1152], F32)
    w_ukv_d = din("w_ukv", [DEPTH, 256, 1536], F32)
    w_o_d = din("w_o", [DEPTH, D, D], F32)
    norm_g_d = din("norm_g", [DEPTH, D], F32)
    q_norm_g_d = din("q_norm_g", [DEPTH, 384], F32)
    kv_norm_g_d = din("kv_norm_g", [DEPTH, 256], F32)
    lam_d = {n: din(n, [DEPTH, 64], F32) for n in ("lam_q1", "lam_k1", "lam_q2", "lam_k2")}
    subln_g_d = din("subln_g", [DEPTH, 128], F32)
    w_ple_d = din("w_ple", [DEPTH, PLE, D], F32)
    w_pg_d = din("w_pg", [DEPTH, D, D], F32)
    final_g_d = din("final_g", [1, D], F32)
    ident_d = din("ident", [128, 128], BF16)
    invf_d = din("invf", [128, 56], F32)
    dmask_d = din("dmask", [4, 128, 512], BF16)
    dvalid_d = din("dvalid", [4, 128, 512], F32)
    dneg_d = din("dneg", [4, 128, 512], F32)
    pow2_d = din("pow2", [128, NIT], F32)
    y_d = nc.dram_tensor("y", [S, D], F32, kind="ExternalOutput").ap()

    hbuf = dscr("hbuf", [S, D], F32)
    uT_d = dscr("uT_d", [NT, 128, 16, 128], BF16)
    rope_d = dscr("rope_d", [128, NT, 224], F32)
    qaTn = dscr("qaTn", [6, 128, S], BF16)
    qaTr = dscr("qaTr", [3, 128, S], BF16)
    kaTn = dscr("kaTn", [6, 128, S], BF16)
    kropeT = dscr("kropeT", [1, 128, S], BF16)
    Va = dscr("Va", [S, 6 * VW], BF16)
    qbT = dscr("qbT", [5, 128, S], BF16)
    kbT = dscr("kbT", [5, 128, S], BF16)
    Vb = dscr("Vb", [S, 5 * VW], BF16)
    qiT = dscr("qiT", [4, 128, S], BF16)
    kiT = dscr("kiT", [1, 128, S], BF16)
    WI = dscr("WI", [S, 16], F32)
    qcT = dscr("qcT", [5, 128, S], BF16)
    kcT = dscr("kcT", [5, 128, S], BF16)
    Vc = dscr("Vc", [S, 5 * VW], BF16)
    SG = dscr("SG", [S, D], F32)
    MX = dscr("MX", [S, D], BF16)
    NM = dscr("NM", [NQB, 128, 4 * NQB, 512], BF16)

    def sb(es, name, shape, dt):
        uid[0] += 1
        h = es.enter_context(nc.sbuf_tensor("%s_%d" % (name, uid[0]), list(shape), dt))
        return Buf(name, h.ap())

    def ps(es, name, shape, dt):
        uid[0] += 1
        h = es.enter_context(nc.psum_tensor("%s_%d" % (name, uid[0]), list(shape), dt))
        return Buf(name, h.ap(), psum=True)

    V = nc.vector
    G = nc.gpsimd
    A = nc.scalar
    PE = nc.tensor
    op = sch.op
    dma = sch.dma

    top = ExitStack()
    ident = sb(top, "ident", [128, 128], BF16)
    dmask = sb(top, "dmask", [128, 4, 512], BF16)
    dma("sp", ident.ap, ident_d, writes=[ident])
    dma("sp", dmask.ap, dmask_d.rearrange("j p q -> p j q"), writes=[dmask])

    def bcast_rows(src_row_ap, n):
        return src_row_ap.to_broadcast([128, n])

    ROFF = {"aq": (0, 32), "ak": (64, 32), "bq": (128, 16), "bk": (160, 16), "i": (192, 8), "cq": (208, 8)}
    with ExitStack() as es:
        posi = sb(es, "posi", [128, NT], I32)
        posf = sb(es, "posf", [128, NT], F32)
        invf = sb(es, "invf", [128, 56], F32)
        ang = sb(es, "ang", [128, NT, 56], F32)
        r1 = sb(es, "r1", [128, NT, 56], F32)
        cs = sb(es, "cs", [128, NT, 56], F32)
        sn = sb(es, "sn", [128, NT, 56], F32)
        tab = sb(es, "tab", [128, NT, 224], F32)
        dma("sp", posi.ap, pos_d, writes=[posi])
        dma("sp", invf.ap, invf_d, writes=[invf])
        op("dve", lambda: V.tensor_copy(out=posf.ap, in_=posi.ap), [posi], [posf])
        for t in range(NT):
            op("dve", lambda: V.tensor_scalar(out=ang.ap[:, t, :], in0=invf.ap, scalar1=posf.ap[:, t:t + 1],
                                              scalar2=None, op0=ALU.mult), [invf, posf], [ang])
        twopi = 2.0 * math.pi
        ki = sb(es, "ki", [128, NT, 56], I32)
        kf = sb(es, "kf", [128, NT, 56], F32)

        def sin_of(shift, dst):
            op("dve", lambda: V.tensor_scalar(out=r1.ap, in0=ang.ap, scalar1=float(shift), scalar2=None, op0=ALU.add),
               [ang], [r1])
            op("dve", lambda: V.tensor_scalar(out=kf.ap, in0=r1.ap, scalar1=1.0 / twopi, scalar2=None, op0=ALU.mult),
               [r1], [kf])
            op("dve", lambda: V.tensor_copy(out=ki.ap, in_=kf.ap), [kf], [ki])
            op("dve", lambda: V.tensor_copy(out=kf.ap, in_=ki.ap), [ki], [kf])
            op("dve", lambda: V.scalar_tensor_tensor(out=r1.ap, in0=kf.ap, scalar=-twopi, in1=r1.ap,
                                                     op0=ALU.mult, op1=ALU.add), [kf, r1], [r1])
            op("dve", lambda: V.tensor_scalar(out=kf.ap, in0=r1.ap, scalar1=math.pi, scalar2=-twopi,
                                              op0=ALU.is_gt, op1=ALU.mult), [r1], [kf])
            op("dve", lambda: V.tensor_tensor(out=r1.ap, in0=r1.ap, in1=kf.ap, op=ALU.add), [r1, kf], [r1])
            op("dve", lambda: V.tensor_scalar(out=kf.ap, in0=r1.ap, scalar1=-math.pi, scalar2=twopi,
                                              op0=ALU.is_lt, op1=ALU.mult), [r1], [kf])
            op("dve", lambda: V.tensor_tensor(out=r1.ap, in0=r1.ap, in1=kf.ap, op=ALU.add), [r1, kf], [r1])
            op("act", lambda: A.activation(out=dst.ap, in_=r1.ap, func=AF.Sin), [r1], [dst])

        sin_of(0.0, sn)
        sin_of(0.5 * math.pi, cs)
        specs = [("aq", 0, SA), ("ak", 0, 1.0), ("bq", 32, SB), ("bk", 32, 1.0), ("i", 48, 1.0), ("cq", 48, SC)]
        for (nm, so, scl) in specs:
            o, hf = ROFF[nm]
            op("dve", lambda: V.tensor_scalar(out=tab.ap[:, :, o:o + hf], in0=cs.ap[:, :, so:so + hf],
                                              scalar1=float(scl), scalar2=None, op0=ALU.mult), [cs], [tab])
            op("dve", lambda: V.tensor_scalar(out=tab.ap[:, :, o + hf:o + 2 * hf], in0=sn.ap[:, :, so:so + hf],
                                              scalar1=float(scl), scalar2=None, op0=ALU.mult), [sn], [tab])
        dma("sp", rope_d, tab.ap, reads=[tab])
        sch.barrier()

    def rms_rstd(es_tmp, src_ap, n, junk, st, reads):
        op("act", lambda: A.activation(out=junk.ap[:, 0:n], in_=src_ap, func=AF.Square, accum_out=st.ap[:, 0:1]),
           reads, [junk, st])
        op("act", lambda: A.activation(out=st.ap[:, 1:2], in_=st.ap[:, 0:1], func=AF.Sqrt, bias=EPS, scale=1.0 / n),
           [st], [st])
        op("dve", lambda: V.reciprocal(out=st.ap[:, 0:1], in_=st.ap[:, 1:2]), [st], [st])

    maybe_stop_holder = [None]
    try:
      maybe_stop("P")
      for L in range(DEPTH):
        h_src = x_d if L == 0 else hbuf
        last = (L == DEPTH - 1)
        lam_init = 0.8 - 0.6 * math.exp(-0.3 * L)

        with ExitStack() as es:
            gbc = sb(es, "gbc", [128, D], F32)
            dma("sp", gbc.ap, bcast_rows(norm_g_d[L:L + 1, :], D), writes=[gbc])
            hin = [sb(es, "hin", [128, D], F32) for _ in range(2)]
            ub = [sb(es, "ub", [128, D], BF16) for _ in range(2)]
            uTs = [sb(es, "uTs", [128, 16, 128], BF16) for _ in range(2)]
            junk = sb(es, "junk", [128, D], BF16)
            st = [sb(es, "st", [128, 2], F32) for _ in range(2)]
            pT = [ps(es, "pT", [128, 8, 128], BF16) for _ in range(4)]
            npT = 0
            dma("sp", hin[0].ap, h_src[0:128, :], writes=[hin[0]])
            for t in range(NT):
                i = t % 2
                if t + 1 < NT:
                    dma("sp", hin[1 - i].ap, h_src[(t + 1) * 128:(t + 2) * 128, :], writes=[hin[1 - i]])
                rms_rstd(es, hin[i].ap, D, junk, st[i], [hin[i]])
                op("dve", lambda: V.scalar_tensor_tensor(out=ub[i].ap, in0=hin[i].ap, scalar=st[i].ap[:, 0:1],
                                                         in1=gbc.ap, op0=ALU.mult, op1=ALU.mult),
                   [hin[i], st[i], gbc], [ub[i]])
                for g in range(2):
                    pt = pT[npT % 4]
                    npT += 1
                    for c in range(8):
                        cc = 8 * g + c
                        op("pe", lambda: PE.transpose(out=pt.ap[:, c, :], in_=ub[i].ap[:, cc * 128:(cc + 1) * 128],
                                                      identity=ident.ap), [ub[i], ident], [pt])
                    op("act", lambda: A.copy(out=uTs[i].ap[:, 8 * g:8 * g + 8, :], in_=pt.ap), [pt], [uTs[i]])
                dma("sp", uT_d[t], uTs[i].ap, reads=[uTs[i]])
            sch.barrier()
            maybe_stop("A1")

        with ExitStack() as es:
            WMAX = 768
            wbuf = [sb(es, "wbuf", [128, 16, WMAX], BF16) for _ in range(2)]
            wuq = sb(es, "wuq", [128, 3, 1152], BF16)
            wukv = sb(es, "wukv", [128, 2, 1536], BF16)
            qgb = sb(es, "qgb", [128, 384], F32)
            kvgb = sb(es, "kvgb", [128, 256], F32)
            tab = sb(es, "tab", [128, NT, 224], F32)
            uTs = [sb(es, "uTs", [128, 16, 128], BF16) for _ in range(3)]
            pz = [ps(es, "pz", [128, 512], F32) for _ in range(4)]
            ptr = [ps(es, "ptr", [128, 8, 128], BF16) for _ in range(2)]
            junkf = sb(es, "junkf", [128, 512], F32)
            st = sb(es, "st", [128, 2], F32)
            cqn = sb(es, "cqn", [128, 384], BF16)
            cqT = sb(es, "cqT", [128, 3, 128], BF16)
            ckvn = sb(es, "ckvn", [128, 256], BF16)
            ckvT = sb(es, "ckvT", [128, 2, 128], BF16)
            qn = sb(es, "qn", [128, 6, 128], BF16)
            qr = sb(es, "qr", [128, 6, 64], BF16)
            kn = sb(es, "kn", [128, 6, 128], BF16)
            kr = sb(es, "kr", [128, 2, 64], BF16)
            vx6 = sb(es, "vx6", [128, 6, VW], BF16)
            vx5 = sb(es, "vx5", [128, 5, VW], BF16)
            hd5 = sb(es, "hd5", [128, 5, 128], BF16)
            qis = sb(es, "qis", [128, 8, 64], BF16)
            kis = sb(es, "kis", [128, 2, 64], BF16)
            wi = sb(es, "wi", [128, 16], F32)
            sgt = [sb(es, "sgt", [128, 512], F32) for _ in range(2)]
            rt = [[sb(es, "rt", [128, 128], F32) for _ in range(4)] for _ in range(2)]
            stage = [sb(es, "stage", [128, 8, 128], BF16) for _ in range(2)]
            cnt = {"pz": 0, "ptr": 0, "rt": 0, "stage": 0, "sgt": 0}

            dma("sp", tab.ap, rope_d, writes=[tab])
            dma("sp", qgb.ap, bcast_rows(q_norm_g_d[L:L + 1, :], 384), writes=[qgb])
            dma("sp", kvgb.ap, bcast_rows(kv_norm_g_d[L:L + 1, :], 256), writes=[kvgb])
            dma("pool", wuq.ap, w_uq_d[L].rearrange("(c p) n -> p c n", p=128), writes=[wuq])
            dma("pool", wukv.ap, w_ukv_d[L].rearrange("(c p) n -> p c n", p=128), writes=[wukv])
            op("dve", lambda: V.memset(vx6.ap, 1.0), [], [vx6])
            op("dve", lambda: V.memset(vx5.ap, 1.0), [], [vx5])

            def load_w(si):
                c0 = CHUNKS[SEGS[si][0]][0]
                c1 = CHUNKS[SEGS[si][1]][0] + CHUNKS[SEGS[si][1]][1]
                wb = wbuf[si % 2]
                src = w_in_d[L].rearrange("(c p) n -> p c n", p=128)
                for kc0 in range(0, 16, 4):
                    dma("pool", wb.ap[:, kc0:kc0 + 4, 0:c1 - c0], src[:, kc0:kc0 + 4, c0:c1], writes=[wb])

            def nxt(key, lst):
                cnt[key] += 1
                return lst[cnt[key] % len(lst)]

            def rope_evac(src3, dst3, H, Dh, n_rot, tname, t, scale, reads, writes):
                half = n_rot // 2
                o, hf = ROFF[tname]
                assert hf == half
                cb = tab.ap[:, t, o:o + half].unsqueeze(1).to_broadcast([128, H, half])
                sbc = tab.ap[:, t, o + half:o + 2 * half].unsqueeze(1).to_broadcast([128, H, half])
                r = nxt("rt", rt)
                n = H * half
                v = [r[k].ap[:, 0:n].rearrange("p (h d) -> p h d", h=H) for k in range(4)]
                x1 = src3[:, :, 0:half]
                x2 = src3[:, :, half:n_rot]
                if Dh > n_rot:
                    op("act", lambda: A.mul(out=dst3[:, :, n_rot:Dh], in_=src3[:, :, n_rot:Dh], mul=float(scale)),
                       reads, writes)
                op("dve", lambda: V.tensor_tensor(out=v[0], in0=x1, in1=cb, op=ALU.mult), reads + [tab], [r[0]])
                op("dve", lambda: V.tensor_tensor(out=v[1], in0=x2, in1=sbc, op=ALU.mult), reads + [tab], [r[1]])
                op("dve", lambda: V.tensor_tensor(out=v[2], in0=x2, in1=cb, op=ALU.mult), reads + [tab], [r[2]])
                op("dve", lambda: V.tensor_tensor(out=v[3], in0=x1, in1=sbc, op=ALU.mult), reads + [tab], [r[3]])
                op("dve", lambda: V.tensor_tensor(out=dst3[:, :, 0:half], in0=v[0], in1=v[1], op=ALU.subtract),
                   [r[0], r[1]], writes)
                op("dve", lambda: V.tensor_tensor(out=dst3[:, :, half:n_rot], in0=v[2], in1=v[3], op=ALU.add),
                   [r[2], r[3]], writes)

            def fm_store(src, src3, n, dst, t):
                for g0 in range(0, n, 8):
                    g1 = min(n, g0 + 8)
                    pt = nxt("ptr", ptr)
                    sg_ = nxt("stage", stage)
                    for k in range(g0, g1):
                        op("pe", lambda: PE.transpose(out=pt.ap[:, k - g0, :], in_=src3[:, k, :], identity=ident.ap),
                           [src, ident], [pt])
                    op("act", lambda: A.copy(out=sg_.ap[:, 0:g1 - g0, :], in_=pt.ap[:, 0:g1 - g0, :]), [pt], [sg_])
                    dma("sp", dst.rearrange("n p s -> p n s")[:, g0:g1, t * 128:(t + 1) * 128],
                        sg_.ap[:, 0:g1 - g0, :], reads=[sg_])

            def rmsnorm_to(src_ap, n, gb, dst, reads):
                rms_rstd(es, src_ap, n, junkf, st, reads)
                op("dve", lambda: V.scalar_tensor_tensor(out=dst.ap, in0=src_ap, scalar=st.ap[:, 0:1], in1=gb.ap,
                                                         op0=ALU.mult, op1=ALU.mult), reads + [st, gb], [dst])

            def epilogue(ci, z, t):
                col0, width, kind, meta = CHUNKS[ci]
                zs = z.ap[:, 0:width]
                r0 = t * 128
                if kind == "cq":
                    rmsnorm_to(zs, 384, qgb, cqn, [z])
                    pt = nxt("ptr", ptr)
                    for c in range(3):
                        op("pe", lambda: PE.transpose(out=pt.ap[:, c, :], in_=cqn.ap[:, c * 128:(c + 1) * 128],
                                                      identity=ident.ap), [cqn, ident], [pt])
                    op("act", lambda: A.copy(out=cqT.ap, in_=pt.ap[:, 0:3, :]), [pt], [cqT])
                    for c in range(3):
                        z2 = nxt("pz", pz)
                        for kc in range(3):
                            op("pe", lambda: PE.matmul(z2.ap[:, 0:384], lhsT=cqT.ap[:, kc, :],
                                                       rhs=wuq.ap[:, kc, c * 384:(c + 1) * 384],
                                                       start=(kc == 0), stop=(kc == 2)), [cqT, wuq], [z2])
                        v3 = z2.ap[:, 0:384].rearrange("p (h d) -> p h d", h=2)
                        op("act", lambda: A.mul(out=qn.ap[:, 2 * c:2 * c + 2, :], in_=v3[:, :, 0:128], mul=float(SA)),
                           [z2], [qn])
                        rope_evac(v3[:, :, 128:192], qr.ap[:, 2 * c:2 * c + 2, :], 2, 64, 64, "aq", t, SA, [z2], [qr])
                    fm_store(qn, qn.ap, 6, qaTn, t)
                    fm_store(qr, qr.ap.rearrange("p (a b) d -> p a (b d)", b=2), 3, qaTr, t)
                elif kind == "ckv":
                    LVL = int(os.environ.get("MK_LVL", 99))
                    rmsnorm_to(z.ap[:, 0:256], 256, kvgb, ckvn, [z])
                    if LVL < 2:
                        return
                    pt = nxt("ptr", ptr)
                    for c in range(2):
                        op("pe", lambda: PE.transpose(out=pt.ap[:, c, :], in_=ckvn.ap[:, c * 128:(c + 1) * 128],
                                                      identity=ident.ap), [ckvn, ident], [pt])
                    op("act", lambda: A.copy(out=ckvT.ap, in_=pt.ap[:, 0:2, :]), [pt], [ckvT])
                    if LVL < 3:
                        return
                    rope_evac(z.ap[:, 256:320].rearrange("p (h d) -> p h d", h=1), kr.ap[:, 0:1, :], 1, 64, 64, "ak", t,
                              1.0, [z], [kr])
                    if LVL < 4:
                        return
                    op("act", lambda: A.copy(out=kr.ap[:, 1, :], in_=kr.ap[:, 0, :]), [kr], [kr])
                    if LVL < 5:
                        return
                    for c in range(3):
                        z2 = nxt("pz", pz)
                        for kc in range(2):
                            op("pe", lambda: PE.matmul(z2.ap, lhsT=ckvT.ap[:, kc, :],
                                                       rhs=wukv.ap[:, kc, c * 512:(c + 1) * 512],
                                                       start=(kc == 0), stop=(kc == 1)), [ckvT, wukv], [z2])
                        v3 = z2.ap.rearrange("p (h d) -> p h d", h=2)
                        SUB = int(os.environ.get("MK_SUB", 3))
                        if SUB & 1:
                            op("act", lambda: A.copy(out=kn.ap[:, 2 * c:2 * c + 2, :], in_=v3[:, :, 0:128]), [z2], [kn])
                        if SUB & 2:
                            op("dve", lambda: V.tensor_copy(out=vx6.ap[:, 2 * c:2 * c + 2, 0:128], in_=v3[:, :, 128:256]),
                               [z2], [vx6])
                    if LVL < 6:
                        return
                    fm_store(kn, kn.ap, 6, kaTn, t)
                    if LVL < 7:
                        return
                    fm_store(kr, kr.ap.rearrange("p (a b) d -> p a (b d)", b=2), 1, kropeT, t)
                    if LVL < 8:
                        return
                    dma("sp", Va[r0:r0 + 128, :], vx6.ap.rearrange("p h d -> p (h d)"), reads=[vx6])
                elif kind == "gate":
                    s_ = nxt("sgt", sgt)
                    op("act", lambda: A.activation(out=s_.ap[:, 0:width], in_=zs, func=AF.Silu), [z], [s_])
                    dma("sp", SG[r0:r0 + 128, meta:meta + width], s_.ap[:, 0:width], reads=[s_])
                elif kind in ("qb", "kb"):
                    h0, nh = meta
                    v3 = zs.rearrange("p (h d) -> p h d", h=nh)
                    rope_evac(v3, hd5.ap[:, h0:h0 + nh, :], nh, 128, 32, "bq" if kind == "qb" else "bk", t,
                              SB if kind == "qb" else 1.0, [z], [hd5])
                    if h0 + nh == 5:
                        fm_store(hd5, hd5.ap, 5, qbT if kind == "qb" else kbT, t)
                elif kind in ("vb", "vc"):
                    h0, nh = meta
                    v3 = zs.rearrange("p (h d) -> p h d", h=nh)
                    op("dve", lambda: V.tensor_copy(out=vx5.ap[:, h0:h0 + nh, 0:128], in_=v3), [z], [vx5])
                    if h0 + nh == 5:
                        dst = Vb if kind == "vb" else Vc
                        dma("sp", dst[r0:r0 + 128, :], vx5.ap.rearrange("p h d -> p (h d)"), reads=[vx5])
                elif kind == "qi":
                    v3 = zs.rearrange("p (h d) -> p h d", h=8)
                    rope_evac(v3, qis.ap, 8, 64, 16, "i", t, 1.0, [z], [qis])
                    fm_store(qis, qis.ap.rearrange("p (a b) d -> p a (b d)", b=2), 4, qiT, t)
                elif kind == "kiw":
                    rope_evac(z.ap[:, 0:64].rearrange("p (h d) -> p h d", h=1), kis.ap[:, 0:1, :], 1, 64, 16, "i", t,
                              1.0, [z], [kis])
                    op("act", lambda: A.copy(out=kis.ap[:, 1, :], in_=kis.ap[:, 0, :]), [kis], [kis])
                    fm_store(kis, kis.ap.rearrange("p (a b) d -> p a (b d)", b=2), 1, kiT, t)
                    op("act", lambda: A.activation(out=wi.ap[:, 0:8], in_=z.ap[:, 64:72], func=AF.Abs), [z], [wi])
                    op("dve", lambda: V.tensor_scalar(out=wi.ap[:, 8:16], in0=z.ap[:, 64:72], scalar1=0.0, scalar2=2.0,
                                                      op0=ALU.is_ge, op1=ALU.mult), [z], [wi])
                    op("dve", lambda: V.tensor_scalar(out=wi.ap[:, 8:16], in0=wi.ap[:, 8:16], scalar1=-1.0,
                                                      scalar2=None, op0=ALU.add), [wi], [wi])
                    dma("sp", WI[r0:r0 + 128, :], wi.ap, reads=[wi])
                elif kind in ("qc", "kc"):
                    h0, nh = meta
                    v3 = zs.rearrange("p (h d) -> p h d", h=nh)
                    d3 = hd5.ap.rearrange("p a (b d) -> p (a b) d", b=2)
                    rope_evac(v3, d3[:, h0:h0 + nh, :], nh, 64, 16, "cq" if kind == "qc" else "i", t,
                              SC if kind == "qc" else 1.0, [z], [hd5])
                    if h0 + nh == 10:
                        fm_store(hd5, hd5.ap, 5, qcT if kind == "qc" else kcT, t)
                else:
                    raise AssertionError(kind)

            NSEG = int(os.environ.get("MK_NSEG", len(SEGS)))
            EPI = os.environ.get("MK_EPI", "1") == "1"
            EPK = os.environ.get("MK_EPK")
            EPK = set(EPK.split(",")) if EPK else None
            if NSEG > 0:
                load_w(0)
            for si in range(NSEG):
                if si + 1 < NSEG:
                    load_w(si + 1)
                wb = wbuf[si % 2]
                c0 = CHUNKS[SEGS[si][0]][0]
                dma("sp", uTs[0].ap, uT_d[0], writes=[uTs[0]])
                for t in range(NT):
                    u = uTs[t % 3]
                    if t + 1 < NT:
                        dma("sp", uTs[(t + 1) % 3].ap, uT_d[t + 1], writes=[uTs[(t + 1) % 3]])
                    zl = []
                    for ci in SEGS[si]:
                        col0, width, kind, meta = CHUNKS[ci]
                        z = nxt("pz", pz)
                        for kc in range(16):
                            op("pe", lambda: PE.matmul(z.ap[:, 0:width], lhsT=u.ap[:, kc, :],
                                                       rhs=wb.ap[:, kc, col0 - c0:col0 - c0 + width],
                                                       start=(kc == 0), stop=(kc == 15)), [u, wb], [z])
                        zl.append((ci, z))
                    if CHUNKS[SEGS[si][0]][2] == "qi":
                        zl = zl[::-1]
                    for (ci, z) in zl:
                        if EPI and (EPK is None or CHUNKS[ci][2] in EPK):
                            epilogue(ci, z, t)
            sch.barrier()
            maybe_stop("A2")

        with ExitStack() as es:
            GQ = 4
            kiTs = sb(es, "kiTs", [128, S], BF16)
            dvl = sb(es, "dvl", [128, 4, 512], F32)
            dng = sb(es, "dng", [128, 4, 512], F32)
            pw2 = sb(es, "pw2", [128, NIT], F32)
            Sc = [sb(es, "Sc", [128, S], F32) for _ in range(GQ)]
            rl = [sb(es, "rl", [128, 512], F32) for _ in range(4)]
            qit = [sb(es, "qit", [128, 4, 128], BF16) for _ in range(GQ)]
            wit = [sb(es, "wit", [128, 16], F32) for _ in range(GQ)]
            sel = [sb(es, "sel", [128, S], BF16) for _ in range(2)]
            junkb = [sb(es, "junkb", [128, S], BF16) for _ in range(GQ)]
            nms = [sb(es, "nms", [128, 4 * NQB, 128], BF16) for _ in range(2)]
            bs = [sb(es, "bs", [128, 8], F32) for _ in range(GQ)]
            halves = [sb(es, "halves", [128, NIT], F32) for _ in range(GQ)]
            psI = [ps(es, "psI", [128, 512], F32) for _ in range(4)]
            pst = [ps(es, "pst", [128, 8, 128], BF16) for _ in range(2)]
            nI = [0, 0, 0]
            dma("sp", kiTs.ap, kiT[0], writes=[kiTs])
            dma("sp", dvl.ap, dvalid_d.rearrange("j p q -> p j q"), writes=[dvl])
            dma("sp", dng.ap, dneg_d.rearrange("j p q -> p j q"), writes=[dng])
            dma("sp", pw2.ap, pow2_d, writes=[pw2])
            for qb in range(NQB):
                nkb = qb + 1
                Lk = 512 * nkb
                tiles = list(range(GQ))
                for qs in tiles:
                    it = 4 * qb + qs
                    dma("sp", qit[qs].ap, qiT.rearrange("n p s -> p n s")[:, :, it * 128:(it + 1) * 128], writes=[qit[qs]])
                    dma("sp", wit[qs].ap, WI[it * 128:(it + 1) * 128, :], writes=[wit[qs]])
                for kb in range(nkb):
                    for h in range(8):
                        for qs in tiles:
                            q_, w_, sc = qit[qs], wit[qs], Sc[qs]
                            pI = psI[nI[0] % 4]
                            nI[0] += 1
                            r_ = rl[nI[1] % 4]
                            nI[1] += 1
                            lo_p = 64 * (h % 2)
                            op("pe", lambda: PE.matmul(pI.ap, lhsT=q_.ap[lo_p:lo_p + 64, h // 2, :],
                                                       rhs=kiTs.ap[lo_p:lo_p + 64, kb * 512:(kb + 1) * 512],
                                                       start=True, stop=True), [q_, kiTs], [pI])
                            op("act", lambda: A.activation(out=r_.ap, in_=pI.ap, func=AF.Relu, scale=w_.ap[:, h:h + 1]),
                               [pI, w_], [r_])
                            scs = sc.ap[:, kb * 512:(kb + 1) * 512]
                            if h == 0:
                                op("dve", lambda: V.tensor_scalar(out=scs, in0=r_.ap, scalar1=w_.ap[:, 8:9], scalar2=None,
                                                                  op0=ALU.mult), [r_, w_], [sc])
                            else:
                                op("dve", lambda: V.scalar_tensor_tensor(out=scs, in0=r_.ap, scalar=w_.ap[:, 8 + h:9 + h],
                                                                         in1=scs, op0=ALU.mult, op1=ALU.add),
                                   [r_, w_, sc], [sc])
                bis = [qs for qs in tiles if 4 * qb + qs >= 2]

                def each(fn, lst=tiles):
                    for qs in lst:
                        fn(qs)

                def dgv(qs):
                    return Sc[qs].ap[:, qb * 512:(qb + 1) * 512]

                each(lambda qs: op("dve", lambda: V.tensor_tensor(out=dgv(qs), in0=dgv(qs), in1=dvl.ap[:, qs, :],
                                                                   op=ALU.mult), [Sc[qs], dvl], [Sc[qs]]))
                each(lambda qs: op("dve", lambda: V.tensor_reduce(out=bs[qs].ap[:, 0:1], in_=Sc[qs].ap[:, 0:Lk], axis=AX.X,
                                                                   op=ALU.max), [Sc[qs]], [bs[qs]]), bis)
                each(lambda qs: op("dve", lambda: V.tensor_reduce(out=bs[qs].ap[:, 1:2], in_=Sc[qs].ap[:, 0:Lk], axis=AX.X,
                                                                   op=ALU.min), [Sc[qs]], [bs[qs]]), bis)
                each(lambda qs: op("dve", lambda: V.tensor_tensor(out=dgv(qs), in0=dgv(qs), in1=dng.ap[:, qs, :],
                                                                   op=ALU.add), [Sc[qs], dng], [Sc[qs]]))
                each(lambda qs: op("dve", lambda: V.tensor_tensor(out=bs[qs].ap[:, 2:3], in0=bs[qs].ap[:, 0:1],
                                                                   in1=bs[qs].ap[:, 1:2], op=ALU.subtract),
                                   [bs[qs]], [bs[qs]]), bis)
                each(lambda qs: op("dve", lambda: V.tensor_scalar(out=halves[qs].ap, in0=pw2.ap, scalar1=bs[qs].ap[:, 2:3],
                                                                   scalar2=None, op0=ALU.mult), [pw2, bs[qs]], [halves[qs]]),
                     bis)
                for k in range(NIT):
                    each(lambda qs: op("dve", lambda: V.tensor_tensor(out=bs[qs].ap[:, 3:4], in0=bs[qs].ap[:, 1:2],
                                                                       in1=halves[qs].ap[:, k:k + 1], op=ALU.add),
                                       [bs[qs], halves[qs]], [bs[qs]]), bis)
                    each(lambda qs: op("dve", lambda: V.tensor_scalar(out=junkb[qs].ap[:, 0:Lk], in0=Sc[qs].ap[:, 0:Lk],
                                                                       scalar1=bs[qs].ap[:, 3:4], scalar2=None,
                                                                       op0=ALU.is_ge, op1=ALU.add,
                                                                       accum_out=bs[qs].ap[:, 4:5]),
                                       [Sc[qs], bs[qs]], [junkb[qs], bs[qs]]), bis)
                    each(lambda qs: op("dve", lambda: V.tensor_scalar(out=bs[qs].ap[:, 5:6], in0=bs[qs].ap[:, 4:5],
                                                                       scalar1=255.5, scalar2=halves[qs].ap[:, k:k + 1],
                                                                       op0=ALU.is_ge, op1=ALU.mult),
                                       [bs[qs], halves[qs]], [bs[qs]]), bis)
                    each(lambda qs: op("dve", lambda: V.tensor_tensor(out=bs[qs].ap[:, 1:2], in0=bs[qs].ap[:, 1:2],
                                                                       in1=bs[qs].ap[:, 5:6], op=ALU.add),
                                       [bs[qs]], [bs[qs]]), bis)
                for qs in tiles:
                    if qs not in bis:
                        op("dve", lambda: V.memset(bs[qs].ap[:, 1:2], -1e29), [], [bs[qs]])
                for qs in tiles:
                    sl_ = sel[qs % 2]
                    op("dve", lambda: V.tensor_scalar(out=sl_.ap[:, 0:Lk], in0=Sc[qs].ap[:, 0:Lk], scalar1=bs[qs].ap[:, 1:2],
                                                      scalar2=None, op0=ALU.is_ge), [Sc[qs], bs[qs]], [sl_])
                    nm_ = nms[qs % 2]
                    for g in range(nkb):
                        pt = pst[nI[2] % 2]
                        nI[2] += 1
                        for k in range(4):
                            kt = 4 * g + k
                            op("pe", lambda: PE.transpose(out=pt.ap[:, k, :], in_=sl_.ap[:, kt * 128:(kt + 1) * 128],
                                                          identity=ident.ap), [sl_, ident], [pt])
                        op("act", lambda: A.activation(out=nm_.ap[:, 4 * g:4 * g + 4, :], in_=pt.ap[:, 0:4, :],
                                                       func=AF.Identity, bias=float(NEGM), scale=float(-NEGM)), [pt], [nm_])
                    dma("sp", NM[qb][:, 0:4 * nkb, qs * 128:(qs + 1) * 128], nm_.ap[:, 0:4 * nkb, :], reads=[nm_])
            sch.barrier()
            maybe_stop("B1")

        def attention(mixer):
            with ExitStack() as es:
                if mixer == "a":
                    H, width, col0 = 6, 768, 0
                    Vd, kTd, nkt_extra = Va, kaTn, True
                elif mixer == "b":
                    H, width, col0 = 5, 640, 768
                    Vd, kTd, nkt_extra = Vb, kbT, False
                else:
                    H, width, col0 = 5, 640, 1408
                    Vd, kTd, nkt_extra = Vc, kcT, False
                kTs = sb(es, "kTs", [128, H, S], BF16)
                Vs = sb(es, "Vs", [128, NT, H * VW], BF16)
                for h in range(H):
                    dma("sp", kTs.ap[:, h, :], kTd[h], writes=[kTs])
                for t0 in range(0, NT, 8):
                    t1 = min(NT, t0 + 8)
                    dma("sp", Vs.ap[:, t0:t1, :], Vd[t0 * 128:t1 * 128, :].rearrange("(t p) c -> p t c", p=128),
                        writes=[Vs])
                if mixer == "a":
                    krs = sb(es, "krs", [128, S], BF16)
                    dma("sp", krs.ap, kropeT[0], writes=[krs])
                    qrb = [sb(es, "qrb", [128, 3, 512], BF16) for _ in range(2)]
                if mixer == "b":
                    nmT = [sb(es, "nmT", [128, 4 * NQB, 512], BF16) for _ in range(1)]
                if mixer == "c":
                    subg = sb(es, "subg", [128, 128], F32)
                    lamt = sb(es, "lamt", [128, 4, 64], F32)
                    lams = sb(es, "lams", [128, 8], F32)
                    dma("sp", subg.ap, bcast_rows(subln_g_d[L:L + 1, :], 128), writes=[subg])
                    for i, n in enumerate(("lam_q1", "lam_k1", "lam_q2", "lam_k2")):
                        dma("sp", lamt.ap[:, i, :], bcast_rows(lam_d[n][L:L + 1, :], 64), writes=[lamt])
                    op("dve", lambda: V.tensor_scalar(out=subg.ap, in0=subg.ap, scalar1=float(1.0 - lam_init),
                                                      scalar2=None, op0=ALU.mult), [subg], [subg])
                    for j in range(2):
                        op("dve", lambda: V.tensor_tensor(out=lamt.ap[:, 2 * j, :], in0=lamt.ap[:, 2 * j, :],
                                                          in1=lamt.ap[:, 2 * j + 1, :], op=ALU.mult), [lamt], [lamt])
                        op("dve", lambda: V.reduce_sum(out=lams.ap[:, j:j + 1], in_=lamt.ap[:, 2 * j, :], axis=AX.X),
                           [lamt], [lams])
                    op("act", lambda: A.activation(out=lams.ap[:, 2:4], in_=lams.ap[:, 0:2], func=AF.Exp), [lams], [lams])
                    op("dve", lambda: V.tensor_tensor(out=lams.ap[:, 4:5], in0=lams.ap[:, 3:4], in1=lams.ap[:, 2:3],
                                                      op=ALU.subtract), [lams], [lams])
                    op("dve", lambda: V.tensor_scalar(out=lams.ap[:, 5:6], in0=lams.ap[:, 4:5], scalar1=float(-lam_init),
                                                      scalar2=None, op0=ALU.add), [lams], [lams])
                    t1s = [sb(es, "t1s", [128, 4, 128], F32) for _ in range(1)]
                    osb = [sb(es, "osb", [128, 128], F32) for _ in range(4)]
                    junkc = [sb(es, "junkc", [128, 128], F32) for _ in range(4)]
                qnb = [sb(es, "qnb", [128, H, 512], BF16) for _ in range(2)]
                sgb = [sb(es, "sgb", [128, 4, width], F32) for _ in range(2)]
                mxo = [sb(es, "mxo", [128, 4, width], BF16) for _ in range(2)]
                PTs = [sb(es, "PT", [128, 512], BF16) for _ in range(4)]
                rd = [sb(es, "rd", [128, 4], F32) for _ in range(4)]
                pS = [ps(es, "pS", [128, 512], F32) for _ in range(3)]
                pO = [[ps(es, "pO", [128, 512], F32) for _ in range(2)] for _ in range(2)]
                n = {"S": 0, "P": 0, "O": 0, "rd": 0, "osb": 0}

                def units_of(h):
                    if mixer == "c":
                        return [(h, 1), (h, 2)]
                    return [(h, 0)]

                for qb in range(NQB):
                    qq = qnb[qb % 2]
                    qsl = slice(qb * 512, (qb + 1) * 512)
                    qsrc = {"a": qaTn, "b": qbT, "c": qcT}[mixer]
                    dma("sp", qq.ap, qsrc.rearrange("n p s -> p n s")[:, :, qsl], writes=[qq])
                    if mixer == "a":
                        qr_ = qrb[qb % 2]
                        dma("sp", qr_.ap, qaTr.rearrange("n p s -> p n s")[:, :, qsl], writes=[qr_])
                    if mixer == "b":
                        nm_ = nmT[0]
                        dma("sp", nm_.ap[:, 0:4 * (qb + 1), :], NM[qb][:, 0:4 * (qb + 1), :], writes=[nm_])
                    sg_ = sgb[qb % 2]
                    dma("sp", sg_.ap, SG[qsl, col0:col0 + width].rearrange("(a p) c -> p a c", p=128), writes=[sg_])
                    mo = mxo[qb % 2]
                    nkt = 4 * (qb + 1)
                    for h in range(H):
                        for (hh, u) in units_of(h):
                            po = pO[n["O"] % 2]
                            n["O"] += 1
                            for kt in range(nkt):
                                j = kt - 4 * qb
                                c_lo = 128 * j if j > 0 else 0
                                cs_ = slice(c_lo, 512)
                                ksl = slice(kt * 128, (kt + 1) * 128)
                                s_ = pS[n["S"] % 3]
                                n["S"] += 1
                                need_mask = (mixer == "b") or (j >= 0)
                                if mixer == "a":
                                    lo_p = 64 * (h % 2)
                                    op("pe", lambda: PE.matmul(s_.ap[:, cs_], lhsT=kTs.ap[:, h, ksl], rhs=qq.ap[:, h, cs_],
                                                               start=True, stop=False), [kTs, qq], [s_])
                                    op("pe", lambda: PE.matmul(s_.ap[:, cs_], lhsT=krs.ap[lo_p:lo_p + 64, ksl],
                                                               rhs=qr_.ap[lo_p:lo_p + 64, h // 2, cs_],
                                                               start=False, stop=not need_mask), [krs, qr_], [s_])
                                elif mixer == "b":
                                    op("pe", lambda: PE.matmul(s_.ap[:, cs_], lhsT=kTs.ap[:, h, ksl], rhs=qq.ap[:, h, cs_],
                                                               start=True, stop=False), [kTs, qq], [s_])
                                else:
                                    lo_p = 64 * (u - 1)
                                    op("pe", lambda: PE.matmul(s_.ap[:, cs_], lhsT=kTs.ap[lo_p:lo_p + 64, h, ksl],
                                                               rhs=qq.ap[lo_p:lo_p + 64, h, cs_],
                                                               start=True, stop=not need_mask), [kTs, qq], [s_])
                                if need_mask:
                                    if mixer == "b":
                                        op("pe", lambda: PE.matmul(s_.ap[:, cs_], lhsT=ident.ap, rhs=nm_.ap[:, kt, cs_],
                                                                   start=False, stop=True), [ident, nm_], [s_])
                                    else:
                                        op("pe", lambda: PE.matmul(s_.ap[:, cs_], lhsT=ident.ap, rhs=dmask.ap[:, j, cs_],
                                                                   start=False, stop=True), [ident, dmask], [s_])
                                P_ = PTs[n["P"] % 4]
                                n["P"] += 1
                                op("act", lambda: A.activation(out=P_.ap[:, cs_], in_=s_.ap[:, cs_], func=AF.Exp),
                                   [s_], [P_])
                                for qs in range(max(j, 0), 4):
                                    bank = po[qs // 2]
                                    oc = (qs % 2) * 256
                                    first = (kt == 0 and qs % 2 == 0)
                                    op("pe", lambda: PE.matmul(bank.ap[:, oc:oc + 129], lhsT=P_.ap[:, qs * 128:(qs + 1) * 128],
                                                               rhs=Vs.ap[:, kt, h * VW:h * VW + 129],
                                                               start=first, stop=(kt == 4 * qb + qs),
                                                               skip_group_check=True), [P_, Vs], [bank])
                            QS = range(4)
                            bk = [po[qs // 2] for qs in QS]
                            ocs = [(qs % 2) * 256 for qs in QS]
                            rr = [rd[qs] for qs in QS]
                            dsts = [mo.ap[:, qs, h * 128:(h + 1) * 128] for qs in QS]
                            gsls = [sg_.ap[:, qs, h * 128:(h + 1) * 128] for qs in QS]
                            for qs in QS:
                                op("dve", lambda: V.reciprocal(out=rr[qs].ap[:, 0:1], in_=bk[qs].ap[:, ocs[qs] + 128:ocs[qs] + 129]),
                                   [bk[qs]], [rr[qs]])
                            if mixer != "c":
                                for qs in QS:
                                    op("dve", lambda: V.scalar_tensor_tensor(out=dsts[qs], in0=bk[qs].ap[:, ocs[qs]:ocs[qs] + 128],
                                                                             scalar=rr[qs].ap[:, 0:1], in1=gsls[qs],
                                                                             op0=ALU.mult, op1=ALU.mult),
                                       [bk[qs], rr[qs], sg_], [mo])
                            elif u == 1:
                                t1_ = t1s[0]
                                for qs in QS:
                                    op("act", lambda: A.mul(out=t1_.ap[:, qs, :], in_=bk[qs].ap[:, ocs[qs]:ocs[qs] + 128],
                                                            mul=rr[qs].ap[:, 0:1]), [bk[qs], rr[qs]], [t1_])
                            else:
                                t1_ = t1s[0]
                                oo = [osb[qs] for qs in QS]
                                for qs in QS:
                                    op("dve", lambda: V.tensor_tensor(out=rr[qs].ap[:, 1:2], in0=rr[qs].ap[:, 0:1],
                                                                      in1=lams.ap[:, 5:6], op=ALU.mult), [rr[qs], lams], [rr[qs]])
                                for qs in QS:
                                    op("dve", lambda: V.scalar_tensor_tensor(out=oo[qs].ap, in0=bk[qs].ap[:, ocs[qs]:ocs[qs] + 128],
                                                                             scalar=rr[qs].ap[:, 1:2], in1=t1_.ap[:, qs, :],
                                                                             op0=ALU.mult, op1=ALU.add),
                                       [bk[qs], rr[qs], t1_], [oo[qs]])
                                for qs in QS:
                                    op("act", lambda: A.activation(out=junkc[qs].ap, in_=oo[qs].ap, func=AF.Square,
                                                                   accum_out=rr[qs].ap[:, 2:3]), [oo[qs]], [junkc[qs], rr[qs]])
                                for qs in QS:
                                    op("act", lambda: A.activation(out=rr[qs].ap[:, 3:4], in_=rr[qs].ap[:, 2:3], func=AF.Sqrt,
                                                                   bias=EPS, scale=1.0 / 128.0), [rr[qs]], [rr[qs]])
                                for qs in QS:
                                    op("dve", lambda: V.reciprocal(out=rr[qs].ap[:, 2:3], in_=rr[qs].ap[:, 3:4]), [rr[qs]], [rr[qs]])
                                for qs in QS:
                                    op("dve", lambda: V.scalar_tensor_tensor(out=oo[qs].ap, in0=oo[qs].ap, scalar=rr[qs].ap[:, 2:3],
                                                                             in1=subg.ap, op0=ALU.mult, op1=ALU.mult),
                                       [oo[qs], rr[qs], subg], [oo[qs]])
                                for qs in QS:
                                    op("dve", lambda: V.tensor_tensor(out=dsts[qs], in0=oo[qs].ap, in1=gsls[qs], op=ALU.mult),
                                       [oo[qs], sg_], [mo])
                    dma("sp", MX[qsl, col0:col0 + width].rearrange("(a p) c -> p a c", p=128), mo.ap, reads=[mo])
                sch.barrier()
                maybe_stop("B" + mixer)

        attention("a")
        attention("b")
        attention("c")

        with ExitStack() as es:
            wo = sb(es, "wo", [128, 16, D], BF16)
            for c in range(4):
                for kc0 in range(0, 16, 8):
                    dma("pool", wo.ap[:, kc0:kc0 + 8, c * 512:(c + 1) * 512],
                        w_o_d[L].rearrange("(c p) n -> p c n", p=128)[:, kc0:kc0 + 8, c * 512:(c + 1) * 512], writes=[wo])
            mxt = [sb(es, "mxt", [128, D], BF16) for _ in range(2)]
            mT = [sb(es, "mT", [128, 16, 128], BF16) for _ in range(2)]
            hin = [sb(es, "hin", [128, D], F32) for _ in range(2)]
            h1 = [sb(es, "h1", [128, D], F32) for _ in range(2)]
            h1b = [sb(es, "h1b", [128, D], BF16) for _ in range(2)]
            h1T = [sb(es, "h1T", [128, 16, 128], BF16) for _ in range(2)]
            pT = [ps(es, "pT", [128, 8, 128], BF16) for _ in range(4)]
            pz = [ps(es, "pz", [128, 512], F32) for _ in range(4)]
            npT = 0

            def c1_load(t):
                dma("sp", mxt[t % 2].ap, MX[t * 128:(t + 1) * 128, :], writes=[mxt[t % 2]])
                dma("sp", hin[t % 2].ap, h_src[t * 128:(t + 1) * 128, :], writes=[hin[t % 2]])

            c1_load(0)
            for t in range(NT):
                i = t % 2
                rs = slice(t * 128, (t + 1) * 128)
                if t + 1 < NT:
                    c1_load(t + 1)
                for g in range(2):
                    pt = pT[npT % 4]
                    npT += 1
                    for c in range(8):
                        cc = 8 * g + c
                        op("pe", lambda: PE.transpose(out=pt.ap[:, c, :], in_=mxt[i].ap[:, cc * 128:(cc + 1) * 128],
                                                      identity=ident.ap), [mxt[i], ident], [pt])
                    op("act", lambda: A.copy(out=mT[i].ap[:, 8 * g:8 * g + 8, :], in_=pt.ap), [pt], [mT[i]])
                for c in range(4):
                    z = pz[c]
                    for kc in range(16):
                        op("pe", lambda: PE.matmul(z.ap, lhsT=mT[i].ap[:, kc, :], rhs=wo.ap[:, kc, c * 512:(c + 1) * 512],
                                                   start=(kc == 0), stop=(kc == 15)), [mT[i], wo], [z])
                    op("dve", lambda: V.tensor_tensor(out=h1[i].ap[:, c * 512:(c + 1) * 512], in0=z.ap,
                                                      in1=hin[i].ap[:, c * 512:(c + 1) * 512], op=ALU.add),
                       [z, hin[i]], [h1[i]])
                dma("sp", hbuf[rs, :], h1[i].ap, reads=[h1[i]])
                op("dve", lambda: V.tensor_copy(out=h1b[i].ap, in_=h1[i].ap), [h1[i]], [h1b[i]])
                for g in range(2):
                    pt = pT[npT % 4]
                    npT += 1
                    for c in range(8):
                        cc = 8 * g + c
                        op("pe", lambda: PE.transpose(out=pt.ap[:, c, :], in_=h1b[i].ap[:, cc * 128:(cc + 1) * 128],
                                                      identity=ident.ap), [h1b[i], ident], [pt])
                    op("act", lambda: A.copy(out=h1T[i].ap[:, 8 * g:8 * g + 8, :], in_=pt.ap), [pt], [h1T[i]])
                dma("sp", uT_d[t], h1T[i].ap, reads=[h1T[i]])
            sch.barrier()
            maybe_stop("C1")

        with ExitStack() as es:
            wpg = sb(es, "wpg", [128, 16, D], BF16)
            wple = sb(es, "wple", [128, 2, D], BF16)
            for c in range(4):
                for kc0 in range(0, 16, 8):
                    dma("pool", wpg.ap[:, kc0:kc0 + 8, c * 512:(c + 1) * 512],
                        w_pg_d[L].rearrange("(c p) n -> p c n", p=128)[:, kc0:kc0 + 8, c * 512:(c + 1) * 512], writes=[wpg])
            dma("pool", wple.ap, w_ple_d[L].rearrange("(c p) n -> p c n", p=128), writes=[wple])
            if last:
                fgb = sb(es, "fgb", [128, D], F32)
                dma("sp", fgb.ap, bcast_rows(final_g_d[0:1, :], D), writes=[fgb])
                junk = sb(es, "junk", [128, D], BF16)
                st = [sb(es, "st", [128, 2], F32) for _ in range(2)]
            h1 = [sb(es, "h1", [128, D], F32) for _ in range(2)]
            h1T = [sb(es, "h1T", [128, 16, 128], BF16) for _ in range(2)]
            pt_ = [sb(es, "pt_", [128, PLE], F32) for _ in range(2)]
            pb = [sb(es, "pb", [128, PLE], BF16) for _ in range(2)]
            pTs = [sb(es, "pTs", [128, 2, 128], BF16) for _ in range(2)]
            sg = [sb(es, "sg", [128, 512], F32) for _ in range(2)]
            tm = [sb(es, "tm", [128, 512], F32) for _ in range(2)]
            h2 = [sb(es, "h2", [128, D], F32) for _ in range(2)]
            pz = [ps(es, "pz", [128, 512], F32) for _ in range(6)]
            pT = [ps(es, "pT", [128, 8, 128], BF16) for _ in range(2)]
            nz = 0
            ns = 0
            def c2_load(t):
                dma("sp", h1[t % 2].ap, hbuf[t * 128:(t + 1) * 128, :], writes=[h1[t % 2]])
                dma("sp", h1T[t % 2].ap, uT_d[t], writes=[h1T[t % 2]])
                dma("sp", pt_[t % 2].ap, p_d[L][t * 128:(t + 1) * 128, :], writes=[pt_[t % 2]])

            c2_load(0)
            for t in range(NT):
                i = t % 2
                rs = slice(t * 128, (t + 1) * 128)
                if t + 1 < NT:
                    c2_load(t + 1)
                op("dve", lambda: V.tensor_copy(out=pb[i].ap, in_=pt_[i].ap), [pt_[i]], [pb[i]])
                for c in range(2):
                    op("pe", lambda: PE.transpose(out=pT[i].ap[:, c, :], in_=pb[i].ap[:, c * 128:(c + 1) * 128],
                                                  identity=ident.ap), [pb[i], ident], [pT[i]])
                op("act", lambda: A.copy(out=pTs[i].ap, in_=pT[i].ap[:, 0:2, :]), [pT[i]], [pTs[i]])
                for c in range(4):
                    cs_ = slice(c * 512, (c + 1) * 512)
                    zg = pz[nz % 6]
                    nz += 1
                    for kc in range(16):
                        op("pe", lambda: PE.matmul(zg.ap, lhsT=h1T[i].ap[:, kc, :], rhs=wpg.ap[:, kc, cs_],
                                                   start=(kc == 0), stop=(kc == 15)), [h1T[i], wpg], [zg])
                    zp = pz[nz % 6]
                    nz += 1
                    for kc in range(2):
                        op("pe", lambda: PE.matmul(zp.ap, lhsT=pTs[i].ap[:, kc, :], rhs=wple.ap[:, kc, cs_],
                                                   start=(kc == 0), stop=(kc == 1)), [pTs[i], wple], [zp])
                    s_ = sg[ns % 2]
                    t_ = tm[ns % 2]
                    ns += 1
                    op("act", lambda: A.activation(out=s_.ap, in_=zg.ap, func=AF.Sigmoid), [zg], [s_])
                    op("dve", lambda: V.tensor_tensor(out=t_.ap, in0=zp.ap, in1=s_.ap, op=ALU.mult), [zp, s_], [t_])
                    op("dve", lambda: V.tensor_tensor(out=h2[i].ap[:, cs_], in0=t_.ap, in1=h1[i].ap[:, cs_], op=ALU.add),
                       [t_, h1[i]], [h2[i]])
                if not last:
                    dma("sp", hbuf[rs, :], h2[i].ap, reads=[h2[i]])
                else:
                    rms_rstd(es, h2[i].ap, D, junk, st[i], [h2[i]])
                    op("dve", lambda: V.scalar_tensor_tensor(out=h1[i].ap, in0=h2[i].ap, scalar=st[i].ap[:, 0:1],
                                                             in1=fgb.ap, op0=ALU.mult, op1=ALU.mult),
                       [h2[i], st[i], fgb], [h1[i]])
                    dma("sp", y_d[rs, :], h1[i].ap, reads=[h1[i]])
            sch.barrier()
            maybe_stop("C2")

    except _StopBuild:
        return nc, sch
    top.close()
    return nc, sch


_CACHE = {}


def kernel(x, p, positions, w_in, w_uq, w_ukv, w_o, norm_g, q_norm_g, kv_norm_g,
           lam_q1, lam_k1, lam_q2, lam_k2, subln_g, w_ple, w_pg, final_g):
    x = np.asarray(x)
    B, S, _ = x.shape
    DEPTH = int(np.asarray(w_in).shape[0])
    key = (S, DEPTH)
    if key not in _CACHE:
        _CACHE[key] = build_program(S, DEPTH)[0]
    nc = _CACHE[key]
    consts = host_consts()
    f32 = lambda a: np.ascontiguousarray(np.asarray(a), dtype=np.float32)
    shared = {
        "w_in": f32(w_in), "w_uq": f32(w_uq), "w_ukv": f32(w_ukv), "w_o": f32(w_o),
        "norm_g": f32(norm_g), "q_norm_g": f32(q_norm_g), "kv_norm_g": f32(kv_norm_g),
        "lam_q1": f32(lam_q1), "lam_k1": f32(lam_k1), "lam_q2": f32(lam_q2), "lam_k2": f32(lam_k2),
        "subln_g": f32(subln_g), "w_ple": f32(w_ple), "w_pg": f32(w_pg),
        "final_g": f32(final_g).reshape(1, D),
    }
    shared.update(consts)
    p = np.asarray(p)
    positions = np.asarray(positions)
    in_maps = []
    for c in range(8):
        b = c % B
        m = dict(shared)
        m["x"] = f32(x[b])
        m["p"] = f32(p[:, b])
        m["positions"] = np.ascontiguousarray(positions[b].astype(np.int32).reshape(S // 128, 128).T)
        in_maps.append(m)
    res = run_bass_kernel_spmd(nc, in_maps, core_ids=list(range(8)))
    out = np.stack([np.asarray(res.results[b]["y"], dtype=np.float32) for b in range(B)], axis=0)
    return out
```

```python
import math
import os
from contextlib import ExitStack

import numpy as np
import ml_dtypes
import concourse.bass as bass
import concourse.mybir as mybir
from concourse.bass_utils import run_bass_kernel_spmd

F32 = mybir.dt.float32
BF16 = mybir.dt.bfloat16
I32 = mybir.dt.int32
ALU = mybir.AluOpType
AF = mybir.ActivationFunctionType
AX = mybir.AxisListType

D = 2048
PLE = 256
D_IN = 7176
EPS = 1e-6
THETA = 500000.0
NIT = 22
NEGM = -30000.0
SA = 192.0 ** -0.5
SB = 128.0 ** -0.5
SC = 64.0 ** -0.5
VW = 132


class Buf:
    __slots__ = ("name", "writers", "readers", "ap", "psum")

    def __init__(self, name="", ap=None, psum=False):
        self.psum = psum
        self.name = name
        self.writers = {}
        self.readers = {}
        self.ap = ap


class Sched:
    NDMA = 8

    def __init__(self, nc):
        self.nc = nc
        self.eng = {"pe": nc.tensor, "act": nc.scalar, "dve": nc.vector,
                    "pool": nc.gpsimd, "sp": nc.sync}
        self.sems = {}
        self.cnt = {}
        for e in ("pe", "act", "dve", "pool"):
            self.sems[e] = nc.alloc_semaphore("s_" + e)
            self.cnt[e] = 0
        self.dq = {}
        for q in ("sp", "pool"):
            for i in range(self.NDMA):
                k = "d_%s%d" % (q, i)
                self.sems[k] = nc.alloc_semaphore(k)
                self.cnt[k] = 0
            self.dq[q] = 0
        self.seen = {e: {} for e in self.eng}
        self.nins = 0
        self.nwait = 0

    def _wait(self, e, evs):
        best = {}
        for (k, v) in evs:
            if v > best.get(k, 0):
                best[k] = v
        seen = self.seen[e]
        for k, v in best.items():
            if k == "pe" and e == "pe":
                continue
            if seen.get(k, 0) < v:
                self.eng[e].wait_ge(self.sems[k], v)
                seen[k] = v
                self.nwait += 1

    @staticmethod
    def _deps(reads, writes, e=None):
        evs = []
        for b in reads:
            evs.extend(b.writers.items())
            if b.psum:
                evs.extend((k, v) for k, v in b.readers.items() if k != e)
        for b in writes:
            evs.extend(b.writers.items())
            evs.extend(b.readers.items())
        return evs

    @staticmethod
    def _commit(ev, reads, writes):
        k, v = ev
        for b in reads:
            if b.readers.get(k, 0) < v:
                b.readers[k] = v
        for b in writes:
            b.writers = {k: v}
            b.readers = {}

    def op(self, e, ins_fn, reads=(), writes=()):
        self._wait(e, self._deps(reads, writes, e))
        ins = ins_fn()
        self.cnt[e] += 1
        ins.then_inc(self.sems[e], 1)
        self._commit((e, self.cnt[e]), reads, writes)
        self.nins += 1

    def dma(self, q, out, in_, reads=(), writes=()):
        i = self.dq[q]
        self.dq[q] = (i + 1) % self.NDMA
        k = "d_%s%d" % (q, i)
        evs = self._deps(reads, writes)
        if self.cnt[k] > 0:
            evs.append((k, self.cnt[k]))
        self._wait(q, evs)
        ins = self.eng[q].dma_start(out=out, in_=in_)
        self.cnt[k] += 16
        ins.then_inc(self.sems[k], 16)
        self._commit((k, self.cnt[k]), reads, writes)
        self.nins += 1

    def barrier(self):
        evs = [(k, v) for k, v in self.cnt.items() if v > 0]
        for e in self.eng:
            self._wait(e, list(evs))


def host_consts():
    c = {}
    c["ident"] = np.eye(128, dtype=np.float32).astype(ml_dtypes.bfloat16)
    invf = np.zeros((128, 56), np.float32)
    off = 0
    for n_rot in (64, 32, 16):
        half = n_rot // 2
        f = 1.0 / (np.float32(THETA) ** (np.arange(half, dtype=np.float32) * np.float32(2.0 / n_rot)))
        invf[:, off:off + half] = f.astype(np.float32)[None, :]
        off += half
    c["invf"] = invf
    dm = np.zeros((4, 128, 512), np.float32)
    for j in range(4):
        kk = 128 * j + np.arange(128)[:, None]
        qq = np.arange(512)[None, :]
        dm[j] = np.where((kk // 64) <= (qq // 64), 0.0, NEGM)
    c["dmask"] = dm.astype(ml_dtypes.bfloat16)
    dv = np.zeros((4, 128, 512), np.float32)
    for qs in range(4):
        qq = 128 * qs + np.arange(128)[:, None]
        kk = np.arange(512)[None, :]
        dv[qs] = ((kk // 64) <= (qq // 64)).astype(np.float32)
    c["dvalid"] = dv
    c["dneg"] = ((dv - 1.0) * 1e30).astype(np.float32)
    c["pow2"] = np.tile((0.5 ** np.arange(1, NIT + 1, dtype=np.float64)).astype(np.float32)[None, :], (128, 1))
    return c


CHUNKS = [
    (0, 384, "cq", None),
    (384, 320, "ckv", None),
    (704, 512, "gate", 0),
    (1216, 256, "gate", 512),
    (1472, 512, "qb", (0, 4)),
    (1984, 128, "qb", (4, 1)),
    (2112, 512, "kb", (0, 4)),
    (2624, 128, "kb", (4, 1)),
    (2752, 512, "vb", (0, 4)),
    (3264, 128, "vb", (4, 1)),
    (3392, 512, "qi", None),
    (3904, 72, "kiw", None),
    (3976, 512, "gate", 768),
    (4488, 128, "gate", 1280),
    (4616, 512, "qc", (0, 8)),
    (5128, 128, "qc", (8, 2)),
    (5256, 512, "kc", (0, 8)),
    (5768, 128, "kc", (8, 2)),
    (5896, 512, "vc", (0, 4)),
    (6408, 128, "vc", (4, 1)),
    (6536, 512, "gate", 1408),
    (7048, 128, "gate", 1920),
]
SEGS = [(2 * i, 2 * i + 1) for i in range(11)]


class _StopBuild(Exception):
    pass


def build_program(S, DEPTH, debug_layers=None):
    assert S % 512 == 0
    STOP = os.environ.get("MK_STOP", "")

    def maybe_stop(tag):
        if STOP == tag:
            raise _StopBuild()
    NT = S // 128
    NQB = S // 512
    nc = bass.Bass("TRN2", target_bir_lowering=False)
    sch = Sched(nc)
    uid = [0]

    def din(name, shape, dt):
        return nc.dram_tensor(name, list(shape), dt, kind="ExternalInput").ap()

    def dscr(name, shape, dt):
        return nc.dram_tensor(name, list(shape), dt).ap()

    x_d = din("x", [S, D], F32)
    p_d = din("p", [DEPTH, S, PLE], F32)
    pos_d = din("positions", [128, S // 128], I32)
    w_in_d = din("w_in", [DEPTH, D, D_IN], F32)
    w_uq_d = din("w_uq", [DEPTH, 384, 1152], F32)
    w_ukv_d = din("w_ukv", [DEPTH, 256, 1536], F32)
    w_o_d = din("w_o", [DEPTH, D, D], F32)
    norm_g_d = din("norm_g", [DEPTH, D], F32)
    q_norm_g_d = din("q_norm_g", [DEPTH, 384], F32)
    kv_norm_g_d = din("kv_norm_g", [DEPTH, 256], F32)
    lam_d = {n: din(n, [DEPTH, 64], F32) for n in ("lam_q1", "lam_k1", "lam_q2", "lam_k2")}
    subln_g_d = din("subln_g", [DEPTH, 128], F32)
    w_ple_d = din("w_ple", [DEPTH, PLE, D], F32)
    w_pg_d = din("w_pg", [DEPTH, D, D], F32)
    final_g_d = din("final_g", [1, D], F32)
    ident_d = din("ident", [128, 128], BF16)
    invf_d = din("invf", [128, 56], F32)
    dmask_d = din("dmask", [4, 128, 512], BF16)
    dvalid_d = din("dvalid", [4, 128, 512], F32)
    dneg_d = din("dneg", [4, 128, 512], F32)
    pow2_d = din("pow2", [128, NIT], F32)
    y_d = nc.dram_tensor("y", [S, D], F32, kind="ExternalOutput").ap()

    hbuf = dscr("hbuf", [S, D], F32)
    uT_d = dscr("uT_d", [NT, 128, 16, 128], BF16)
    rope_d = dscr("rope_d", [128, NT, 224], F32)
    qaTn = dscr("qaTn", [6, 128, S], BF16)
    qaTr = dscr("qaTr", [3, 128, S], BF16)
    kaTn = dscr("kaTn", [6, 128, S], BF16)
    kropeT = dscr("kropeT", [1, 128, S], BF16)
    Va = dscr("Va", [S, 6 * VW], BF16)
    qbT = dscr("qbT", [5, 128, S], BF16)
    kbT = dscr("kbT", [5, 128, S], BF16)
    Vb = dscr("Vb", [S, 5 * VW], BF16)
    qiT = dscr("qiT", [4, 128, S], BF16)
    kiT = dscr("kiT", [1, 128, S], BF16)
    WI = dscr("WI", [S, 16], F32)
    qcT = dscr("qcT", [5, 128, S], BF16)
    kcT = dscr("kcT", [5, 128, S], BF16)
    Vc = dscr("Vc", [S, 5 * VW], BF16)
    SG = dscr("SG", [S, D], F32)
    MX = dscr("MX", [S, D], BF16)
    NM = dscr("NM", [NQB, 128, 4 * NQB, 512], BF16)

    def sb(es, name, shape, dt):
        uid[0] += 1
        h = es.enter_context(nc.sbuf_tensor("%s_%d" % (name, uid[0]), list(shape), dt))
        return Buf(name, h.ap())

    def ps(es, name, shape, dt):
        uid[0] += 1
        h = es.enter_context(nc.psum_tensor("%s_%d" % (name, uid[0]), list(shape), dt))
        return Buf(name, h.ap(), psum=True)

    V = nc.vector
    G = nc.gpsimd
    A = nc.scalar
    PE = nc.tensor
    op = sch.op
    dma = sch.dma

    top = ExitStack()
    ident = sb(top, "ident", [128, 128], BF16)
    dmask = sb(top, "dmask", [128, 4, 512], BF16)
    dma("sp", ident.ap, ident_d, writes=[ident])
    dma("sp", dmask.ap, dmask_d.rearrange("j p q -> p j q"), writes=[dmask])

    def bcast_rows(src_row_ap, n):
        return src_row_ap.to_broadcast([128, n])

    ROFF = {"aq": (0, 32), "ak": (64, 32), "bq": (128, 16), "bk": (160, 16), "i": (192, 8), "cq": (208, 8)}
    with ExitStack() as es:
        posi = sb(es, "posi", [128, NT], I32)
        posf = sb(es, "posf", [128, NT], F32)
        invf = sb(es, "invf", [128, 56], F32)
        ang = sb(es, "ang", [128, NT, 56], F32)
        r1 = sb(es, "r1", [128, NT, 56], F32)
        cs = sb(es, "cs", [128, NT, 56], F32)
        sn = sb(es, "sn", [128, NT, 56], F32)
        tab = sb(es, "tab", [128, NT, 224], F32)
        dma("sp", posi.ap, pos_d, writes=[posi])
        dma("sp", invf.ap, invf_d, writes=[invf])
        op("dve", lambda: V.tensor_copy(out=posf.ap, in_=posi.ap), [posi], [posf])
        for t in range(NT):
            op("dve", lambda: V.tensor_scalar(out=ang.ap[:, t, :], in0=invf.ap, scalar1=posf.ap[:, t:t + 1],
                                              scalar2=None, op0=ALU.mult), [invf, posf], [ang])
        twopi = 2.0 * math.pi
        ki = sb(es, "ki", [128, NT, 56], I32)
        kf = sb(es, "kf", [128, NT, 56], F32)

        def sin_of(shift, dst):
            op("dve", lambda: V.tensor_scalar(out=r1.ap, in0=ang.ap, scalar1=float(shift), scalar2=None, op0=ALU.add),
               [ang], [r1])
            op("dve", lambda: V.tensor_scalar(out=kf.ap, in0=r1.ap, scalar1=1.0 / twopi, scalar2=None, op0=ALU.mult),
               [r1], [kf])
            op("dve", lambda: V.tensor_copy(out=ki.ap, in_=kf.ap), [kf], [ki])
            op("dve", lambda: V.tensor_copy(out=kf.ap, in_=ki.ap), [ki], [kf])
            op("dve", lambda: V.scalar_tensor_tensor(out=r1.ap, in0=kf.ap, scalar=-twopi, in1=r1.ap,
                                                     op0=ALU.mult, op1=ALU.add), [kf, r1], [r1])
            op("dve", lambda: V.tensor_scalar(out=kf.ap, in0=r1.ap, scalar1=math.pi, scalar2=-twopi,
                                              op0=ALU.is_gt, op1=ALU.mult), [r1], [kf])
            op("dve", lambda: V.tensor_tensor(out=r1.ap, in0=r1.ap, in1=kf.ap, op=ALU.add), [r1, kf], [r1])
            op("dve", lambda: V.tensor_scalar(out=kf.ap, in0=r1.ap, scalar1=-math.pi, scalar2=twopi,
                                              op0=ALU.is_lt, op1=ALU.mult), [r1], [kf])
            op("dve", lambda: V.tensor_tensor(out=r1.ap, in0=r1.ap, in1=kf.ap, op=ALU.add), [r1, kf], [r1])
            op("act", lambda: A.activation(out=dst.ap, in_=r1.ap, func=AF.Sin), [r1], [dst])

        sin_of(0.0, sn)
        sin_of(0.5 * math.pi, cs)
        specs = [("aq", 0, SA), ("ak", 0, 1.0), ("bq", 32, SB), ("bk", 32, 1.0), ("i", 48, 1.0), ("cq", 48, SC)]
        for (nm, so, scl) in specs:
            o, hf = ROFF[nm]
            op("dve", lambda: V.tensor_scalar(out=tab.ap[:, :, o:o + hf], in0=cs.ap[:, :, so:so + hf],
                                              scalar1=float(scl), scalar2=None, op0=ALU.mult), [cs], [tab])
            op("dve", lambda: V.tensor_scalar(out=tab.ap[:, :, o + hf:o + 2 * hf], in0=sn.ap[:, :, so:so + hf],
                                              scalar1=float(scl), scalar2=None, op0=ALU.mult), [sn], [tab])
        dma("sp", rope_d, tab.ap, reads=[tab])
        sch.barrier()

    def rms_rstd(es_tmp, src_ap, n, junk, st, reads):
        op("act", lambda: A.activation(out=junk.ap[:, 0:n], in_=src_ap, func=AF.Square, accum_out=st.ap[:, 0:1]),
           reads, [junk, st])
        op("act", lambda: A.activation(out=st.ap[:, 1:2], in_=st.ap[:, 0:1], func=AF.Sqrt, bias=EPS, scale=1.0 / n),
           [st], [st])
        op("dve", lambda: V.reciprocal(out=st.ap[:, 0:1], in_=st.ap[:, 1:2]), [st], [st])

    maybe_stop_holder = [None]
    try:
      maybe_stop("P")
      for L in range(DEPTH):
        h_src = x_d if L == 0 else hbuf
        last = (L == DEPTH - 1)
        lam_init = 0.8 - 0.6 * math.exp(-0.3 * L)

        with ExitStack() as es:
            gbc = sb(es, "gbc", [128, D], F32)
            dma("sp", gbc.ap, bcast_rows(norm_g_d[L:L + 1, :], D), writes=[gbc])
            hin = [sb(es, "hin", [128, D], F32) for _ in range(2)]
            ub = [sb(es, "ub", [128, D], BF16) for _ in range(2)]
            uTs = [sb(es, "uTs", [128, 16, 128], BF16) for _ in range(2)]
            junk = sb(es, "junk", [128, D], BF16)
            st = [sb(es, "st", [128, 2], F32) for _ in range(2)]
            pT = [ps(es, "pT", [128, 8, 128], BF16) for _ in range(4)]
            npT = 0
            dma("sp", hin[0].ap, h_src[0:128, :], writes=[hin[0]])
            for t in range(NT):
                i = t % 2
                if t + 1 < NT:
                    dma("sp", hin[1 - i].ap, h_src[(t + 1) * 128:(t + 2) * 128, :], writes=[hin[1 - i]])
                rms_rstd(es, hin[i].ap, D, junk, st[i], [hin[i]])
                op("dve", lambda: V.scalar_tensor_tensor(out=ub[i].ap, in0=hin[i].ap, scalar=st[i].ap[:, 0:1],
                                                         in1=gbc.ap, op0=ALU.mult, op1=ALU.mult),
                   [hin[i], st[i], gbc], [ub[i]])
                for g in range(2):
                    pt = pT[npT % 4]
                    npT += 1
                    for c in range(8):
                        cc = 8 * g + c
                        op("pe", lambda: PE.transpose(out=pt.ap[:, c, :], in_=ub[i].ap[:, cc * 128:(cc + 1) * 128],
                                                      identity=ident.ap), [ub[i], ident], [pt])
                    op("act", lambda: A.copy(out=uTs[i].ap[:, 8 * g:8 * g + 8, :], in_=pt.ap), [pt], [uTs[i]])
                dma("sp", uT_d[t], uTs[i].ap, reads=[uTs[i]])
            sch.barrier()
            maybe_stop("A1")

        with ExitStack() as es:
            WMAX = 768
            wbuf = [sb(es, "wbuf", [128, 16, WMAX], BF16) for _ in range(2)]
            wuq = sb(es, "wuq", [128, 3, 1152], BF16)
            wukv = sb(es, "wukv", [128, 2, 1536], BF16)
            qgb = sb(es, "qgb", [128, 384], F32)
            kvgb = sb(es, "kvgb", [128, 256], F32)
            tab = sb(es, "tab", [128, NT, 224], F32)
            uTs = [sb(es, "uTs", [128, 16, 128], BF16) for _ in range(3)]
            pz = [ps(es, "pz", [128, 512], F32) for _ in range(4)]
            ptr = [ps(es, "ptr", [128, 8, 128], BF16) for _ in range(2)]
            junkf = sb(es, "junkf", [128, 512], F32)
            st = sb(es, "st", [128, 2], F32)
            cqn = sb(es, "cqn", [128, 384], BF16)
            cqT = sb(es, "cqT", [128, 3, 128], BF16)
            ckvn = sb(es, "ckvn", [128, 256], BF16)
            ckvT = sb(es, "ckvT", [128, 2, 128], BF16)
            qn = sb(es, "qn", [128, 6, 128], BF16)
            qr = sb(es, "qr", [128, 6, 64], BF16)
            kn = sb(es, "kn", [128, 6, 128], BF16)
            kr = sb(es, "kr", [128, 2, 64], BF16)
            vx6 = sb(es, "vx6", [128, 6, VW], BF16)
            vx5 = sb(es, "vx5", [128, 5, VW], BF16)
            hd5 = sb(es, "hd5", [128, 5, 128], BF16)
            qis = sb(es, "qis", [128, 8, 64], BF16)
            kis = sb(es, "kis", [128, 2, 64], BF16)
            wi = sb(es, "wi", [128, 16], F32)
            sgt = [sb(es, "sgt", [128, 512], F32) for _ in range(2)]
            rt = [[sb(es, "rt", [128, 128], F32) for _ in range(4)] for _ in range(2)]
            stage = [sb(es, "stage", [128, 8, 128], BF16) for _ in range(2)]
            cnt = {"pz": 0, "ptr": 0, "rt": 0, "stage": 0, "sgt": 0}

            dma("sp", tab.ap, rope_d, writes=[tab])
            dma("sp", qgb.ap, bcast_rows(q_norm_g_d[L:L + 1, :], 384), writes=[qgb])
            dma("sp", kvgb.ap, bcast_rows(kv_norm_g_d[L:L + 1, :], 256), writes=[kvgb])
            dma("pool", wuq.ap, w_uq_d[L].rearrange("(c p) n -> p c n", p=128), writes=[wuq])
            dma("pool", wukv.ap, w_ukv_d[L].rearrange("(c p) n -> p c n", p=128), writes=[wukv])
            op("dve", lambda: V.memset(vx6.ap, 1.0), [], [vx6])
            op("dve", lambda: V.memset(vx5.ap, 1.0), [], [vx5])

            def load_w(si):
                c0 = CHUNKS[SEGS[si][0]][0]
                c1 = CHUNKS[SEGS[si][1]][0] + CHUNKS[SEGS[si][1]][1]
                wb = wbuf[si % 2]
                src = w_in_d[L].rearrange("(c p) n -> p c n", p=128)
                for kc0 in range(0, 16, 4):
                    dma("pool", wb.ap[:, kc0:kc0 + 4, 0:c1 - c0], src[:, kc0:kc0 + 4, c0:c1], writes=[wb])

            def nxt(key, lst):
                cnt[key] += 1
                return lst[cnt[key] % len(lst)]

            def rope_evac(src3, dst3, H, Dh, n_rot, tname, t, scale, reads, writes):
                half = n_rot // 2
                o, hf = ROFF[tname]
                assert hf == half
                cb = tab.ap[:, t, o:o + half].unsqueeze(1).to_broadcast([128, H, half])
                sbc = tab.ap[:, t, o + half:o + 2 * half].unsqueeze(1).to_broadcast([128, H, half])
                r = nxt("rt", rt)
                n = H * half
                v = [r[k].ap[:, 0:n].rearrange("p (h d) -> p h d", h=H) for k in range(4)]
                x1 = src3[:, :, 0:half]
                x2 = src3[:, :, half:n_rot]
                if Dh > n_rot:
                    op("act", lambda: A.mul(out=dst3[:, :, n_rot:Dh], in_=src3[:, :, n_rot:Dh], mul=float(scale)),
                       reads, writes)
                op("dve", lambda: V.tensor_tensor(out=v[0], in0=x1, in1=cb, op=ALU.mult), reads + [tab], [r[0]])
                op("dve", lambda: V.tensor_tensor(out=v[1], in0=x2, in1=sbc, op=ALU.mult), reads + [tab], [r[1]])
                op("dve", lambda: V.tensor_tensor(out=v[2], in0=x2, in1=cb, op=ALU.mult), reads + [tab], [r[2]])
                op("dve", lambda: V.tensor_tensor(out=v[3], in0=x1, in1=sbc, op=ALU.mult), reads + [tab], [r[3]])
                op("dve", lambda: V.tensor_tensor(out=dst3[:, :, 0:half], in0=v[0], in1=v[1], op=ALU.subtract),
                   [r[0], r[1]], writes)
                op("dve", lambda: V.tensor_tensor(out=dst3[:, :, half:n_rot], in0=v[2], in1=v[3], op=ALU.add),
                   [r[2], r[3]], writes)

            def fm_store(src, src3, n, dst, t):
                for g0 in range(0, n, 8):
                    g1 = min(n, g0 + 8)
                    pt = nxt("ptr", ptr)
                    sg_ = nxt("stage", stage)
                    for k in range(g0, g1):
                        op("pe", lambda: PE.transpose(out=pt.ap[:, k - g0, :], in_=src3[:, k, :], identity=ident.ap),
                           [src, ident], [pt])
                    op("act", lambda: A.copy(out=sg_.ap[:, 0:g1 - g0, :], in_=pt.ap[:, 0:g1 - g0, :]), [pt], [sg_])
                    dma("sp", dst.rearrange("n p s -> p n s")[:, g0:g1, t * 128:(t + 1) * 128],
                        sg_.ap[:, 0:g1 - g0, :], reads=[sg_])

            def rmsnorm_to(src_ap, n, gb, dst, reads):
                rms_rstd(es, src_ap, n, junkf, st, reads)
                op("dve", lambda: V.scalar_tensor_tensor(out=dst.ap, in0=src_ap, scalar=st.ap[:, 0:1], in1=gb.ap,
                                                         op0=ALU.mult, op1=ALU.mult), reads + [st, gb], [dst])

            def epilogue(ci, z, t):
                col0, width, kind, meta = CHUNKS[ci]
                zs = z.ap[:, 0:width]
                r0 = t * 128
                if kind == "cq":
                    rmsnorm_to(zs, 384, qgb, cqn, [z])
                    pt = nxt("ptr", ptr)
                    for c in range(3):
                        op("pe", lambda: PE.transpose(out=pt.ap[:, c, :], in_=cqn.ap[:, c * 128:(c + 1) * 128],
                                                      identity=ident.ap), [cqn, ident], [pt])
                    op("act", lambda: A.copy(out=cqT.ap, in_=pt.ap[:, 0:3, :]), [pt], [cqT])
                    for c in range(3):
                        z2 = nxt("pz", pz)
                        for kc in range(3):
                            op("pe", lambda: PE.matmul(z2.ap[:, 0:384], lhsT=cqT.ap[:, kc, :],
                                                       rhs=wuq.ap[:, kc, c * 384:(c + 1) * 384],
                                                       start=(kc == 0), stop=(kc == 2)), [cqT, wuq], [z2])
                        v3 = z2.ap[:, 0:384].rearrange("p (h d) -> p h d", h=2)
                        op("act", lambda: A.mul(out=qn.ap[:, 2 * c:2 * c + 2, :], in_=v3[:, :, 0:128], mul=float(SA)),
                           [z2], [qn])
                        rope_evac(v3[:, :, 128:192], qr.ap[:, 2 * c:2 * c + 2, :], 2, 64, 64, "aq", t, SA, [z2], [qr])
                    fm_store(qn, qn.ap, 6, qaTn, t)
                    fm_store(qr, qr.ap.rearrange("p (a b) d -> p a (b d)", b=2), 3, qaTr, t)
                elif kind == "ckv":
                    LVL = int(os.environ.get("MK_LVL", 99))
                    rmsnorm_to(z.ap[:, 0:256], 256, kvgb, ckvn, [z])
                    if LVL < 2:
                        return
                    pt = nxt("ptr", ptr)
                    for c in range(2):
                        op("pe", lambda: PE.transpose(out=pt.ap[:, c, :], in_=ckvn.ap[:, c * 128:(c + 1) * 128],
                                                      identity=ident.ap), [ckvn, ident], [pt])
                    op("act", lambda: A.copy(out=ckvT.ap, in_=pt.ap[:, 0:2, :]), [pt], [ckvT])
                    if LVL < 3:
                        return
                    rope_evac(z.ap[:, 256:320].rearrange("p (h d) -> p h d", h=1), kr.ap[:, 0:1, :], 1, 64, 64, "ak", t,
                              1.0, [z], [kr])
                    if LVL < 4:
                        return
                    op("act", lambda: A.copy(out=kr.ap[:, 1, :], in_=kr.ap[:, 0, :]), [kr], [kr])
                    if LVL < 5:
                        return
                    for c in range(3):
                        z2 = nxt("pz", pz)
                        for kc in range(2):
                            op("pe", lambda: PE.matmul(z2.ap, lhsT=ckvT.ap[:, kc, :],
                                                       rhs=wukv.ap[:, kc, c * 512:(c + 1) * 512],
                                                       start=(kc == 0), stop=(kc == 1)), [ckvT, wukv], [z2])
                        v3 = z2.ap.rearrange("p (h d) -> p h d", h=2)
                        SUB = int(os.environ.get("MK_SUB", 3))
                        if SUB & 1:
                            op("act", lambda: A.copy(out=kn.ap[:, 2 * c:2 * c + 2, :], in_=v3[:, :, 0:128]), [z2], [kn])
                        if SUB & 2:
                            op("dve", lambda: V.tensor_copy(out=vx6.ap[:, 2 * c:2 * c + 2, 0:128], in_=v3[:, :, 128:256]),
                               [z2], [vx6])
                    if LVL < 6:
                        return
                    fm_store(kn, kn.ap, 6, kaTn, t)
                    if LVL < 7:
                        return
                    fm_store(kr, kr.ap.rearrange("p (a b) d -> p a (b d)", b=2), 1, kropeT, t)
                    if LVL < 8:
                        return
                    dma("sp", Va[r0:r0 + 128, :], vx6.ap.rearrange("p h d -> p (h d)"), reads=[vx6])
                elif kind == "gate":
                    s_ = nxt("sgt", sgt)
                    op("act", lambda: A.activation(out=s_.ap[:, 0:width], in_=zs, func=AF.Silu), [z], [s_])
                    dma("sp", SG[r0:r0 + 128, meta:meta + width], s_.ap[:, 0:width], reads=[s_])
                elif kind in ("qb", "kb"):
                    h0, nh = meta
                    v3 = zs.rearrange("p (h d) -> p h d", h=nh)
                    rope_evac(v3, hd5.ap[:, h0:h0 + nh, :], nh, 128, 32, "bq" if kind == "qb" else "bk", t,
                              SB if kind == "qb" else 1.0, [z], [hd5])
                    if h0 + nh == 5:
                        fm_store(hd5, hd5.ap, 5, qbT if kind == "qb" else kbT, t)
                elif kind in ("vb", "vc"):
                    h0, nh = meta
                    v3 = zs.rearrange("p (h d) -> p h d", h=nh)
                    op("dve", lambda: V.tensor_copy(out=vx5.ap[:, h0:h0 + nh, 0:128], in_=v3), [z], [vx5])
                    if h0 + nh == 5:
                        dst = Vb if kind == "vb" else Vc
                        dma("sp", dst[r0:r0 + 128, :], vx5.ap.rearrange("p h d -> p (h d)"), reads=[vx5])
                elif kind == "qi":
                    v3 = zs.rearrange("p (h d) -> p h d", h=8)
                    rope_evac(v3, qis.ap, 8, 64, 16, "i", t, 1.0, [z], [qis])
                    fm_store(qis, qis.ap.rearrange("p (a b) d -> p a (b d)", b=2), 4, qiT, t)
                elif kind == "kiw":
                    rope_evac(z.ap[:, 0:64].rearrange("p (h d) -> p h d", h=1), kis.ap[:, 0:1, :], 1, 64, 16, "i", t,
                              1.0, [z], [kis])
                    op("act", lambda: A.copy(out=kis.ap[:, 1, :], in_=kis.ap[:, 0, :]), [kis], [kis])
                    fm_store(kis, kis.ap.rearrange("p (a b) d -> p a (b d)", b=2), 1, kiT, t)
                    op("act", lambda: A.activation(out=wi.ap[:, 0:8], in_=z.ap[:, 64:72], func=AF.Abs), [z], [wi])
                    op("dve", lambda: V.tensor_scalar(out=wi.ap[:, 8:16], in0=z.ap[:, 64:72], scalar1=0.0, scalar2=2.0,
                                                      op0=ALU.is_ge, op1=ALU.mult), [z], [wi])
                    op("dve", lambda: V.tensor_scalar(out=wi.ap[:, 8:16], in0=wi.ap[:, 8:16], scalar1=-1.0,
                                                      scalar2=None, op0=ALU.add), [wi], [wi])
                    dma("sp", WI[r0:r0 + 128, :], wi.ap, reads=[wi])
                elif kind in ("qc", "kc"):
                    h0, nh = meta
                    v3 = zs.rearrange("p (h d) -> p h d", h=nh)
                    d3 = hd5.ap.rearrange("p a (b d) -> p (a b) d", b=2)
                    rope_evac(v3, d3[:, h0:h0 + nh, :], nh, 64, 16, "cq" if kind == "qc" else "i", t,
                              SC if kind == "qc" else 1.0, [z], [hd5])
                    if h0 + nh == 10:
                        fm_store(hd5, hd5.ap, 5, qcT if kind == "qc" else kcT, t)
                else:
                    raise AssertionError(kind)

            NSEG = int(os.environ.get("MK_NSEG", len(SEGS)))
            EPI = os.environ.get("MK_EPI", "1") == "1"
            EPK = os.environ.get("MK_EPK")
            EPK = set(EPK.split(",")) if EPK else None
            if NSEG > 0:
                load_w(0)
            for si in range(NSEG):
                if si + 1 < NSEG:
                    load_w(si + 1)
                wb = wbuf[si % 2]
                c0 = CHUNKS[SEGS[si][0]][0]
                dma("sp", uTs[0].ap, uT_d[0], writes=[uTs[0]])
                for t in range(NT):
                    u = uTs[t % 3]
                    if t + 1 < NT:
                        dma("sp", uTs[(t + 1) % 3].ap, uT_d[t + 1], writes=[uTs[(t + 1) % 3]])
                    zl = []
                    for ci in SEGS[si]:
                        col0, width, kind, meta = CHUNKS[ci]
                        z = nxt("pz", pz)
                        for kc in range(16):
                            op("pe", lambda: PE.matmul(z.ap[:, 0:width], lhsT=u.ap[:, kc, :],
                                                       rhs=wb.ap[:, kc, col0 - c0:col0 - c0 + width],
                                                       start=(kc == 0), stop=(kc == 15)), [u, wb], [z])
                        zl.append((ci, z))
                    if CHUNKS[SEGS[si][0]][2] == "qi":
                        zl = zl[::-1]
                    for (ci, z) in zl:
                        if EPI and (EPK is None or CHUNKS[ci][2] in EPK):
                            epilogue(ci, z, t)
            sch.barrier()
            maybe_stop("A2")

        with ExitStack() as es:
            GQ = 4
            kiTs = sb(es, "kiTs", [128, S], BF16)
            dvl = sb(es, "dvl", [128, 4, 512], F32)
            dng = sb(es, "dng", [128, 4, 512], F32)
            pw2 = sb(es, "pw2", [128, NIT], F32)
            Sc = [sb(es, "Sc", [128, S], F32) for _ in range(GQ)]
            rl = [sb(es, "rl", [128, 512], F32) for _ in range(4)]
            qit = [sb(es, "qit", [128, 4, 128], BF16) for _ in range(GQ)]
            wit = [sb(es, "wit", [128, 16], F32) for _ in range(GQ)]
            sel = [sb(es, "sel", [128, S], BF16) for _ in range(2)]
            junkb = [sb(es, "junkb", [128, S], BF16) for _ in range(GQ)]
            nms = [sb(es, "nms", [128, 4 * NQB, 128], BF16) for _ in range(2)]
            bs = [sb(es, "bs", [128, 8], F32) for _ in range(GQ)]
            halves = [sb(es, "halves", [128, NIT], F32) for _ in range(GQ)]
            psI = [ps(es, "psI", [128, 512], F32) for _ in range(4)]
            pst = [ps(es, "pst", [128, 8, 128], BF16) for _ in range(2)]
            nI = [0, 0, 0]
            dma("sp", kiTs.ap, kiT[0], writes=[kiTs])
            dma("sp", dvl.ap, dvalid_d.rearrange("j p q -> p j q"), writes=[dvl])
            dma("sp", dng.ap, dneg_d.rearrange("j p q -> p j q"), writes=[dng])
            dma("sp", pw2.ap, pow2_d, writes=[pw2])
            for qb in range(NQB):
                nkb = qb + 1
                Lk = 512 * nkb
                tiles = list(range(GQ))
                for qs in tiles:
                    it = 4 * qb + qs
                    dma("sp", qit[qs].ap, qiT.rearrange("n p s -> p n s")[:, :, it * 128:(it + 1) * 128], writes=[qit[qs]])
                    dma("sp", wit[qs].ap, WI[it * 128:(it + 1) * 128, :], writes=[wit[qs]])
                for kb in range(nkb):
                    for h in range(8):
                        for qs in tiles:
                            q_, w_, sc = qit[qs], wit[qs], Sc[qs]
                            pI = psI[nI[0] % 4]
                            nI[0] += 1
                            r_ = rl[nI[1] % 4]
                            nI[1] += 1
                            lo_p = 64 * (h % 2)
                            op("pe", lambda: PE.matmul(pI.ap, lhsT=q_.ap[lo_p:lo_p + 64, h // 2, :],
                                                       rhs=kiTs.ap[lo_p:lo_p + 64, kb * 512:(kb + 1) * 512],
                                                       start=True, stop=True), [q_, kiTs], [pI])
                            op("act", lambda: A.activation(out=r_.ap, in_=pI.ap, func=AF.Relu, scale=w_.ap[:, h:h + 1]),
                               [pI, w_], [r_])
                            scs = sc.ap[:, kb * 512:(kb + 1) * 512]
                            if h == 0:
                                op("dve", lambda: V.tensor_scalar(out=scs, in0=r_.ap, scalar1=w_.ap[:, 8:9], scalar2=None,
                                                                  op0=ALU.mult), [r_, w_], [sc])
                            else:
                                op("dve", lambda: V.scalar_tensor_tensor(out=scs, in0=r_.ap, scalar=w_.ap[:, 8 + h:9 + h],
                                                                         in1=scs, op0=ALU.mult, op1=ALU.add),
                                   [r_, w_, sc], [sc])
                bis = [qs for qs in tiles if 4 * qb + qs >= 2]

                def each(fn, lst=tiles):
                    for qs in lst:
                        fn(qs)

                def dgv(qs):
                    return Sc[qs].ap[:, qb * 512:(qb + 1) * 512]

                each(lambda qs: op("dve", lambda: V.tensor_tensor(out=dgv(qs), in0=dgv(qs), in1=dvl.ap[:, qs, :],
                                                                   op=ALU.mult), [Sc[qs], dvl], [Sc[qs]]))
                each(lambda qs: op("dve", lambda: V.tensor_reduce(out=bs[qs].ap[:, 0:1], in_=Sc[qs].ap[:, 0:Lk], axis=AX.X,
                                                                   op=ALU.max), [Sc[qs]], [bs[qs]]), bis)
                each(lambda qs: op("dve", lambda: V.tensor_reduce(out=bs[qs].ap[:, 1:2], in_=Sc[qs].ap[:, 0:Lk], axis=AX.X,
                                                                   op=ALU.min), [Sc[qs]], [bs[qs]]), bis)
                each(lambda qs: op("dve", lambda: V.tensor_tensor(out=dgv(qs), in0=dgv(qs), in1=dng.ap[:, qs, :],
                                                                   op=ALU.add), [Sc[qs], dng], [Sc[qs]]))
                each(lambda qs: op("dve", lambda: V.tensor_tensor(out=bs[qs].ap[:, 2:3], in0=bs[qs].ap[:, 0:1],
                                                                   in1=bs[qs].ap[:, 1:2], op=ALU.subtract),
                                   [bs[qs]], [bs[qs]]), bis)
                each(lambda qs: op("dve", lambda: V.tensor_scalar(out=halves[qs].ap, in0=pw2.ap, scalar1=bs[qs].ap[:, 2:3],
                                                                   scalar2=None, op0=ALU.mult), [pw2, bs[qs]], [halves[qs]]),
                     bis)
                for k in range(NIT):
                    each(lambda qs: op("dve", lambda: V.tensor_tensor(out=bs[qs].ap[:, 3:4], in0=bs[qs].ap[:, 1:2],
                                                                       in1=halves[qs].ap[:, k:k + 1], op=ALU.add),
                                       [bs[qs], halves[qs]], [bs[qs]]), bis)
                    each(lambda qs: op("dve", lambda: V.tensor_scalar(out=junkb[qs].ap[:, 0:Lk], in0=Sc[qs].ap[:, 0:Lk],
                                                                       scalar1=bs[qs].ap[:, 3:4], scalar2=None,
                                                                       op0=ALU.is_ge, op1=ALU.add,
                                                                       accum_out=bs[qs].ap[:, 4:5]),
                                       [Sc[qs], bs[qs]], [junkb[qs], bs[qs]]), bis)
                    each(lambda qs: op("dve", lambda: V.tensor_scalar(out=bs[qs].ap[:, 5:6], in0=bs[qs].ap[:, 4:5],
                                                                       scalar1=255.5, scalar2=halves[qs].ap[:, k:k + 1],
                                                                       op0=ALU.is_ge, op1=ALU.mult),
                                       [bs[qs], halves[qs]], [bs[qs]]), bis)
                    each(lambda qs: op("dve", lambda: V.tensor_tensor(out=bs[qs].ap[:, 1:2], in0=bs[qs].ap[:, 1:2],
                                                                       in1=bs[qs].ap[:, 5:6], op=ALU.add),
                                       [bs[qs]], [bs[qs]]), bis)
                for qs in tiles:
                    if qs not in bis:
                        op("dve", lambda: V.memset(bs[qs].ap[:, 1:2], -1e29), [], [bs[qs]])
                for qs in tiles:
                    sl_ = sel[qs % 2]
                    op("dve", lambda: V.tensor_scalar(out=sl_.ap[:, 0:Lk], in0=Sc[qs].ap[:, 0:Lk], scalar1=bs[qs].ap[:, 1:2],
                                                      scalar2=None, op0=ALU.is_ge), [Sc[qs], bs[qs]], [sl_])
                    nm_ = nms[qs % 2]
                    for g in range(nkb):
                        pt = pst[nI[2] % 2]
                        nI[2] += 1
                        for k in range(4):
                            kt = 4 * g + k
                            op("pe", lambda: PE.transpose(out=pt.ap[:, k, :], in_=sl_.ap[:, kt * 128:(kt + 1) * 128],
                                                          identity=ident.ap), [sl_, ident], [pt])
                        op("act", lambda: A.activation(out=nm_.ap[:, 4 * g:4 * g + 4, :], in_=pt.ap[:, 0:4, :],
                                                       func=AF.Identity, bias=float(NEGM), scale=float(-NEGM)), [pt], [nm_])
                    dma("sp", NM[qb][:, 0:4 * nkb, qs * 128:(qs + 1) * 128], nm_.ap[:, 0:4 * nkb, :], reads=[nm_])
            sch.barrier()
            maybe_stop("B1")

        def attention(mixer):
            with ExitStack() as es:
                if mixer == "a":
                    H, width, col0 = 6, 768, 0
                    Vd, kTd, nkt_extra = Va, kaTn, True
                elif mixer == "b":
                    H, width, col0 = 5, 640, 768
                    Vd, kTd, nkt_extra = Vb, kbT, False
                else:
                    H, width, col0 = 5, 640, 1408
                    Vd, kTd, nkt_extra = Vc, kcT, False
                kTs = sb(es, "kTs", [128, H, S], BF16)
                Vs = sb(es, "Vs", [128, NT, H * VW], BF16)
                for h in range(H):
                    dma("sp", kTs.ap[:, h, :], kTd[h], writes=[kTs])
                for t0 in range(0, NT, 8):
                    t1 = min(NT, t0 + 8)
                    dma("sp", Vs.ap[:, t0:t1, :], Vd[t0 * 128:t1 * 128, :].rearrange("(t p) c -> p t c", p=128),
                        writes=[Vs])
                if mixer == "a":
                    krs = sb(es, "krs", [128, S], BF16)
                    dma("sp", krs.ap, kropeT[0], writes=[krs])
                    qrb = [sb(es, "qrb", [128, 3, 512], BF16) for _ in range(2)]
                if mixer == "b":
                    nmT = [sb(es, "nmT", [128, 4 * NQB, 512], BF16) for _ in range(1)]
                if mixer == "c":
                    subg = sb(es, "subg", [128, 128], F32)
                    lamt = sb(es, "lamt", [128, 4, 64], F32)
                    lams = sb(es, "lams", [128, 8], F32)
                    dma("sp", subg.ap, bcast_rows(subln_g_d[L:L + 1, :], 128), writes=[subg])
                    for i, n in enumerate(("lam_q1", "lam_k1", "lam_q2", "lam_k2")):
                        dma("sp", lamt.ap[:, i, :], bcast_rows(lam_d[n][L:L + 1, :], 64), writes=[lamt])
                    op("dve", lambda: V.tensor_scalar(out=subg.ap, in0=subg.ap, scalar1=float(1.0 - lam_init),
                                                      scalar2=None, op0=ALU.mult), [subg], [subg])
                    for j in range(2):
                        op("dve", lambda: V.tensor_tensor(out=lamt.ap[:, 2 * j, :], in0=lamt.ap[:, 2 * j, :],
                                                          in1=lamt.ap[:, 2 * j + 1, :], op=ALU.mult), [lamt], [lamt])
                        op("dve", lambda: V.reduce_sum(out=lams.ap[:, j:j + 1], in_=lamt.ap[:, 2 * j, :], axis=AX.X),
                           [lamt], [lams])
                    op("act", lambda: A.activation(out=lams.ap[:, 2:4], in_=lams.ap[:, 0:2], func=AF.Exp), [lams], [lams])
                    op("dve", lambda: V.tensor_tensor(out=lams.ap[:, 4:5], in0=lams.ap[:, 3:4], in1=lams.ap[:, 2:3],
                                                      op=ALU.subtract), [lams], [lams])
                    op("dve", lambda: V.tensor_scalar(out=lams.ap[:, 5:6], in0=lams.ap[:, 4:5], scalar1=float(-lam_init),
                                                      scalar2=None, op0=ALU.add), [lams], [lams])
                    t1s = [sb(es, "t1s", [128, 4, 128], F32) for _ in range(1)]
                    osb = [sb(es, "osb", [128, 128], F32) for _ in range(4)]
                    junkc = [sb(es, "junkc", [128, 128], F32) for _ in range(4)]
                qnb = [sb(es, "qnb", [128, H, 512], BF16) for _ in range(2)]
                sgb = [sb(es, "sgb", [128, 4, width], F32) for _ in range(2)]
                mxo = [sb(es, "mxo", [128, 4, width], BF16) for _ in range(2)]
                PTs = [sb(es, "PT", [128, 512], BF16) for _ in range(4)]
                rd = [sb(es, "rd", [128, 4], F32) for _ in range(4)]
                pS = [ps(es, "pS", [128, 512], F32) for _ in range(3)]
                pO = [[ps(es, "pO", [128, 512], F32) for _ in range(2)] for _ in range(2)]
                n = {"S": 0, "P": 0, "O": 0, "rd": 0, "osb": 0}

                def units_of(h):
                    if mixer == "c":
                        return [(h, 1), (h, 2)]
                    return [(h, 0)]

                qsrc = {"a": qaTn, "b": qbT, "c": qcT}[mixer]

                def load_blk(b_):
                    sl_ = slice(b_ * 512, (b_ + 1) * 512)
                    dma("sp", qnb[b_ % 2].ap, qsrc.rearrange("n p s -> p n s")[:, :, sl_], writes=[qnb[b_ % 2]])
                    if mixer == "a":
                        dma("sp", qrb[b_ % 2].ap, qaTr.rearrange("n p s -> p n s")[:, :, sl_], writes=[qrb[b_ % 2]])
                    dma("sp", sgb[b_ % 2].ap, SG[sl_, col0:col0 + width].rearrange("(a p) c -> p a c", p=128),
                        writes=[sgb[b_ % 2]])

                load_blk(0)
                for qb in range(NQB):
                    qq = qnb[qb % 2]
                    qsl = slice(qb * 512, (qb + 1) * 512)
                    if mixer == "a":
                        qr_ = qrb[qb % 2]
                    if mixer == "b":
                        nm_ = nmT[0]
                        dma("sp", nm_.ap[:, 0:4 * (qb + 1), :], NM[qb][:, 0:4 * (qb + 1), :], writes=[nm_])
                    sg_ = sgb[qb % 2]
                    mo = mxo[qb % 2]
                    if qb + 1 < NQB:
                        load_blk(qb + 1)
                    nkt = 4 * (qb + 1)
                    for h in range(H):
                        for (hh, u) in units_of(h):
                            po = pO[n["O"] % 2]
                            n["O"] += 1
                            for kt in range(nkt):
                                j = kt - 4 * qb
                                c_lo = 128 * j if j > 0 else 0
                                cs_ = slice(c_lo, 512)
                                ksl = slice(kt * 128, (kt + 1) * 128)
                                s_ = pS[n["S"] % 3]
                                n["S"] += 1
                                need_mask = (mixer == "b") or (j >= 0)
                                if mixer == "a":
                                    lo_p = 64 * (h % 2)
                                    op("pe", lambda: PE.matmul(s_.ap[:, cs_], lhsT=kTs.ap[:, h, ksl], rhs=qq.ap[:, h, cs_],
                                                               start=True, stop=False), [kTs, qq], [s_])
                                    op("pe", lambda: PE.matmul(s_.ap[:, cs_], lhsT=krs.ap[lo_p:lo_p + 64, ksl],
                                                               rhs=qr_.ap[lo_p:lo_p + 64, h // 2, cs_],
                                                               start=False, stop=not need_mask), [krs, qr_], [s_])
                                elif mixer == "b":
                                    op("pe", lambda: PE.matmul(s_.ap[:, cs_], lhsT=kTs.ap[:, h, ksl], rhs=qq.ap[:, h, cs_],
                                                               start=True, stop=False), [kTs, qq], [s_])
                                else:
                                    lo_p = 64 * (u - 1)
                                    op("pe", lambda: PE.matmul(s_.ap[:, cs_], lhsT=kTs.ap[lo_p:lo_p + 64, h, ksl],
                                                               rhs=qq.ap[lo_p:lo_p + 64, h, cs_],
                                                               start=True, stop=not need_mask), [kTs, qq], [s_])
                                if need_mask:
                                    if mixer == "b":
                                        op("pe", lambda: PE.matmul(s_.ap[:, cs_], lhsT=ident.ap, rhs=nm_.ap[:, kt, cs_],
                                                                   start=False, stop=True), [ident, nm_], [s_])
                                    else:
                                        op("pe", lambda: PE.matmul(s_.ap[:, cs_], lhsT=ident.ap, rhs=dmask.ap[:, j, cs_],
                                                                   start=False, stop=True), [ident, dmask], [s_])
                                P_ = PTs[n["P"] % 4]
                                n["P"] += 1
                                op("act", lambda: A.activation(out=P_.ap[:, cs_], in_=s_.ap[:, cs_], func=AF.Exp),
                                   [s_], [P_])
                                for qs in range(max(j, 0), 4):
                                    bank = po[qs // 2]
                                    oc = (qs % 2) * 256
                                    first = (kt == 0 and qs % 2 == 0)
                                    op("pe", lambda: PE.matmul(bank.ap[:, oc:oc + 129], lhsT=P_.ap[:, qs * 128:(qs + 1) * 128],
                                                               rhs=Vs.ap[:, kt, h * VW:h * VW + 129],
                                                               start=first, stop=(kt == 4 * qb + qs),
                                                               skip_group_check=True), [P_, Vs], [bank])
                            QS = range(4)
                            bk = [po[qs // 2] for qs in QS]
                            ocs = [(qs % 2) * 256 for qs in QS]
                            rr = [rd[qs] for qs in QS]
                            dsts = [mo.ap[:, qs, h * 128:(h + 1) * 128] for qs in QS]
                            gsls = [sg_.ap[:, qs, h * 128:(h + 1) * 128] for qs in QS]
                            for qs in QS:
                                op("dve", lambda: V.reciprocal(out=rr[qs].ap[:, 0:1], in_=bk[qs].ap[:, ocs[qs] + 128:ocs[qs] + 129]),
                                   [bk[qs]], [rr[qs]])
                            if mixer != "c":
                                for qs in QS:
                                    op("dve", lambda: V.scalar_tensor_tensor(out=dsts[qs], in0=bk[qs].ap[:, ocs[qs]:ocs[qs] + 128],
                                                                             scalar=rr[qs].ap[:, 0:1], in1=gsls[qs],
                                                                             op0=ALU.mult, op1=ALU.mult),
                                       [bk[qs], rr[qs], sg_], [mo])
                            elif u == 1:
                                t1_ = t1s[0]
                                for qs in QS:
                                    op("act", lambda: A.mul(out=t1_.ap[:, qs, :], in_=bk[qs].ap[:, ocs[qs]:ocs[qs] + 128],
                                                            mul=rr[qs].ap[:, 0:1]), [bk[qs], rr[qs]], [t1_])
                            else:
                                t1_ = t1s[0]
                                oo = [osb[qs] for qs in QS]
                                for qs in QS:
                                    op("dve", lambda: V.tensor_tensor(out=rr[qs].ap[:, 1:2], in0=rr[qs].ap[:, 0:1],
                                                                      in1=lams.ap[:, 5:6], op=ALU.mult), [rr[qs], lams], [rr[qs]])
                                for qs in QS:
                                    op("dve", lambda: V.scalar_tensor_tensor(out=oo[qs].ap, in0=bk[qs].ap[:, ocs[qs]:ocs[qs] + 128],
                                                                             scalar=rr[qs].ap[:, 1:2], in1=t1_.ap[:, qs, :],
                                                                             op0=ALU.mult, op1=ALU.add),
                                       [bk[qs], rr[qs], t1_], [oo[qs]])
                                for qs in QS:
                                    op("act", lambda: A.activation(out=junkc[qs].ap, in_=oo[qs].ap, func=AF.Square,
                                                                   accum_out=rr[qs].ap[:, 2:3]), [oo[qs]], [junkc[qs], rr[qs]])
                                for qs in QS:
                                    op("act", lambda: A.activation(out=rr[qs].ap[:, 3:4], in_=rr[qs].ap[:, 2:3], func=AF.Sqrt,
                                                                   bias=EPS, scale=1.0 / 128.0), [rr[qs]], [rr[qs]])
                                for qs in QS:
                                    op("dve", lambda: V.reciprocal(out=rr[qs].ap[:, 2:3], in_=rr[qs].ap[:, 3:4]), [rr[qs]], [rr[qs]])
                                for qs in QS:
                                    op("dve", lambda: V.scalar_tensor_tensor(out=oo[qs].ap, in0=oo[qs].ap, scalar=rr[qs].ap[:, 2:3],
                                                                             in1=subg.ap, op0=ALU.mult, op1=ALU.mult),
                                       [oo[qs], rr[qs], subg], [oo[qs]])
                                for qs in QS:
                                    op("dve", lambda: V.tensor_tensor(out=dsts[qs], in0=oo[qs].ap, in1=gsls[qs], op=ALU.mult),
                                       [oo[qs], sg_], [mo])
                    dma("sp", MX[qsl, col0:col0 + width].rearrange("(a p) c -> p a c", p=128), mo.ap, reads=[mo])
                sch.barrier()
                maybe_stop("B" + mixer)

        attention("a")
        attention("b")
        attention("c")

        with ExitStack() as es:
            wo = sb(es, "wo", [128, 16, D], BF16)
            for c in range(4):
                for kc0 in range(0, 16, 8):
                    dma("pool", wo.ap[:, kc0:kc0 + 8, c * 512:(c + 1) * 512],
                        w_o_d[L].rearrange("(c p) n -> p c n", p=128)[:, kc0:kc0 + 8, c * 512:(c + 1) * 512], writes=[wo])
            mxt = [sb(es, "mxt", [128, D], BF16) for _ in range(2)]
            mT = [sb(es, "mT", [128, 16, 128], BF16) for _ in range(2)]
            hin = [sb(es, "hin", [128, D], F32) for _ in range(2)]
            h1 = [sb(es, "h1", [128, D], F32) for _ in range(2)]
            h1b = [sb(es, "h1b", [128, D], BF16) for _ in range(2)]
            h1T = [sb(es, "h1T", [128, 16, 128], BF16) for _ in range(2)]
            pT = [ps(es, "pT", [128, 8, 128], BF16) for _ in range(4)]
            pz = [ps(es, "pz", [128, 512], F32) for _ in range(4)]
            npT = 0

            def c1_load(t):
                dma("sp", mxt[t % 2].ap, MX[t * 128:(t + 1) * 128, :], writes=[mxt[t % 2]])
                dma("sp", hin[t % 2].ap, h_src[t * 128:(t + 1) * 128, :], writes=[hin[t % 2]])

            c1_load(0)
            for t in range(NT):
                i = t % 2
                rs = slice(t * 128, (t + 1) * 128)
                if t + 1 < NT:
                    c1_load(t + 1)
                for g in range(2):
                    pt = pT[npT % 4]
                    npT += 1
                    for c in range(8):
                        cc = 8 * g + c
                        op("pe", lambda: PE.transpose(out=pt.ap[:, c, :], in_=mxt[i].ap[:, cc * 128:(cc + 1) * 128],
                                                      identity=ident.ap), [mxt[i], ident], [pt])
                    op("act", lambda: A.copy(out=mT[i].ap[:, 8 * g:8 * g + 8, :], in_=pt.ap), [pt], [mT[i]])
                for c in range(4):
                    z = pz[c]
                    for kc in range(16):
                        op("pe", lambda: PE.matmul(z.ap, lhsT=mT[i].ap[:, kc, :], rhs=wo.ap[:, kc, c * 512:(c + 1) * 512],
                                                   start=(kc == 0), stop=(kc == 15)), [mT[i], wo], [z])
                    op("dve", lambda: V.tensor_tensor(out=h1[i].ap[:, c * 512:(c + 1) * 512], in0=z.ap,
                                                      in1=hin[i].ap[:, c * 512:(c + 1) * 512], op=ALU.add),
                       [z, hin[i]], [h1[i]])
                dma("sp", hbuf[rs, :], h1[i].ap, reads=[h1[i]])
                op("dve", lambda: V.tensor_copy(out=h1b[i].ap, in_=h1[i].ap), [h1[i]], [h1b[i]])
                for g in range(2):
                    pt = pT[npT % 4]
                    npT += 1
                    for c in range(8):
                        cc = 8 * g + c
                        op("pe", lambda: PE.transpose(out=pt.ap[:, c, :], in_=h1b[i].ap[:, cc * 128:(cc + 1) * 128],
                                                      identity=ident.ap), [h1b[i], ident], [pt])
                    op("act", lambda: A.copy(out=h1T[i].ap[:, 8 * g:8 * g + 8, :], in_=pt.ap), [pt], [h1T[i]])
                dma("sp", uT_d[t], h1T[i].ap, reads=[h1T[i]])
            sch.barrier()
            maybe_stop("C1")

        with ExitStack() as es:
            wpg = sb(es, "wpg", [128, 16, D], BF16)
            wple = sb(es, "wple", [128, 2, D], BF16)
            for c in range(4):
                for kc0 in range(0, 16, 8):
                    dma("pool", wpg.ap[:, kc0:kc0 + 8, c * 512:(c + 1) * 512],
                        w_pg_d[L].rearrange("(c p) n -> p c n", p=128)[:, kc0:kc0 + 8, c * 512:(c + 1) * 512], writes=[wpg])
            dma("pool", wple.ap, w_ple_d[L].rearrange("(c p) n -> p c n", p=128), writes=[wple])
            if last:
                fgb = sb(es, "fgb", [128, D], F32)
                dma("sp", fgb.ap, bcast_rows(final_g_d[0:1, :], D), writes=[fgb])
                junk = sb(es, "junk", [128, D], BF16)
                st = [sb(es, "st", [128, 2], F32) for _ in range(2)]
            h1 = [sb(es, "h1", [128, D], F32) for _ in range(2)]
            h1T = [sb(es, "h1T", [128, 16, 128], BF16) for _ in range(2)]
            pt_ = [sb(es, "pt_", [128, PLE], F32) for _ in range(2)]
            pb = [sb(es, "pb", [128, PLE], BF16) for _ in range(2)]
            pTs = [sb(es, "pTs", [128, 2, 128], BF16) for _ in range(2)]
            sg = [sb(es, "sg", [128, 512], F32) for _ in range(2)]
            tm = [sb(es, "tm", [128, 512], F32) for _ in range(2)]
            h2 = [sb(es, "h2", [128, D], F32) for _ in range(2)]
            pz = [ps(es, "pz", [128, 512], F32) for _ in range(6)]
            pT = [ps(es, "pT", [128, 8, 128], BF16) for _ in range(2)]
            nz = 0
            ns = 0
            def c2_load(t):
                dma("sp", h1[t % 2].ap, hbuf[t * 128:(t + 1) * 128, :], writes=[h1[t % 2]])
                dma("sp", h1T[t % 2].ap, uT_d[t], writes=[h1T[t % 2]])
                dma("sp", pt_[t % 2].ap, p_d[L][t * 128:(t + 1) * 128, :], writes=[pt_[t % 2]])

            c2_load(0)
            for t in range(NT):
                i = t % 2
                rs = slice(t * 128, (t + 1) * 128)
                if t + 1 < NT:
                    c2_load(t + 1)
                op("dve", lambda: V.tensor_copy(out=pb[i].ap, in_=pt_[i].ap), [pt_[i]], [pb[i]])
                for c in range(2):
                    op("pe", lambda: PE.transpose(out=pT[i].ap[:, c, :], in_=pb[i].ap[:, c * 128:(c + 1) * 128],
                                                  identity=ident.ap), [pb[i], ident], [pT[i]])
                op("act", lambda: A.copy(out=pTs[i].ap, in_=pT[i].ap[:, 0:2, :]), [pT[i]], [pTs[i]])
                for c in range(4):
                    cs_ = slice(c * 512, (c + 1) * 512)
                    zg = pz[nz % 6]
                    nz += 1
                    for kc in range(16):
                        op("pe", lambda: PE.matmul(zg.ap, lhsT=h1T[i].ap[:, kc, :], rhs=wpg.ap[:, kc, cs_],
                                                   start=(kc == 0), stop=(kc == 15)), [h1T[i], wpg], [zg])
                    zp = pz[nz % 6]
                    nz += 1
                    for kc in range(2):
                        op("pe", lambda: PE.matmul(zp.ap, lhsT=pTs[i].ap[:, kc, :], rhs=wple.ap[:, kc, cs_],
                                                   start=(kc == 0), stop=(kc == 1)), [pTs[i], wple], [zp])
                    s_ = sg[ns % 2]
                    t_ = tm[ns % 2]
                    ns += 1
                    op("act", lambda: A.activation(out=s_.ap, in_=zg.ap, func=AF.Sigmoid), [zg], [s_])
                    op("dve", lambda: V.tensor_tensor(out=t_.ap, in0=zp.ap, in1=s_.ap, op=ALU.mult), [zp, s_], [t_])
                    op("dve", lambda: V.tensor_tensor(out=h2[i].ap[:, cs_], in0=t_.ap, in1=h1[i].ap[:, cs_], op=ALU.add),
                       [t_, h1[i]], [h2[i]])
                if not last:
                    dma("sp", hbuf[rs, :], h2[i].ap, reads=[h2[i]])
                else:
                    rms_rstd(es, h2[i].ap, D, junk, st[i], [h2[i]])
                    op("dve", lambda: V.scalar_tensor_tensor(out=h1[i].ap, in0=h2[i].ap, scalar=st[i].ap[:, 0:1],
                                                             in1=fgb.ap, op0=ALU.mult, op1=ALU.mult),
                       [h2[i], st[i], fgb], [h1[i]])
                    dma("sp", y_d[rs, :], h1[i].ap, reads=[h1[i]])
            sch.barrier()
            maybe_stop("C2")

    except _StopBuild:
        return nc, sch
    top.close()
    return nc, sch


_CACHE = {}


def kernel(x, p, positions, w_in, w_uq, w_ukv, w_o, norm_g, q_norm_g, kv_norm_g,
           lam_q1, lam_k1, lam_q2, lam_k2, subln_g, w_ple, w_pg, final_g):
    x = np.asarray(x)
    B, S, _ = x.shape
    DEPTH = int(np.asarray(w_in).shape[0])
    key = (S, DEPTH)
    if key not in _CACHE:
        _CACHE[key] = build_program(S, DEPTH)[0]
    nc = _CACHE[key]
    consts = host_consts()
    f32 = lambda a: np.ascontiguousarray(np.asarray(a), dtype=np.float32)
    shared = {
        "w_in": f32(w_in), "w_uq": f32(w_uq), "w_ukv": f32(w_ukv), "w_o": f32(w_o),
        "norm_g": f32(norm_g), "q_norm_g": f32(q_norm_g), "kv_norm_g": f32(kv_norm_g),
        "lam_q1": f32(lam_q1), "lam_k1": f32(lam_k1), "lam_q2": f32(lam_q2), "lam_k2": f32(lam_k2),
        "subln_g": f32(subln_g), "w_ple": f32(w_ple), "w_pg": f32(w_pg),
        "final_g": f32(final_g).reshape(1, D),
    }
    shared.update(consts)
    p = np.asarray(p)
    positions = np.asarray(positions)
    in_maps = []
    for c in range(8):
        b = c % B
        m = dict(shared)
        m["x"] = f32(x[b])
        m["p"] = f32(p[:, b])
        m["positions"] = np.ascontiguousarray(positions[b].astype(np.int32).reshape(S // 128, 128).T)
        in_maps.append(m)
    res = run_bass_kernel_spmd(nc, in_maps, core_ids=list(range(8)))
    out = np.stack([np.asarray(res.results[b]["y"], dtype=np.float32) for b in range(B)], axis=0)
    return out
```
